# Optimizing a Trainium2 kernel written in Bass

```python
import math
import jax, jax.numpy as jnp
from jax import lax
import numpy as np

D_MODEL = 1024
BATCH = 32
SEQ = 2048
DEPTH = 1

RET_HEADS = 4
RET_QK_DIM = 128
RET_V_DIM = 256
RET_CHUNK = 128
RET_QK_W = RET_HEADS * RET_QK_DIM
RET_V_W = RET_HEADS * RET_V_DIM
NA_HEADS = 8
NA_HEAD_DIM = 64
NA_W = NA_HEADS * NA_HEAD_DIM
GRID_W = 64
NA_WIN_ROWS = 8
NA_WIN_COLS = 16
NA_Q_BLOCK_COLS = 16
NA_K_BLOCK_COLS = 32
NA_REL_ROWS = 2 * NA_WIN_ROWS - 1
NA_REL_COLS = 2 * NA_WIN_COLS - 1
D_FF = 2816
RMS_EPS = 1e-6
ROPE_BASE = 10000.0
NEG_INF = -1e30
MIX_SPLITS = (RET_QK_W, RET_QK_W, RET_V_W, RET_V_W, NA_W, NA_W, NA_W, D_MODEL, D_MODEL)
MIX_IN_W = sum(MIX_SPLITS)

kernel_name = "hybrid_retention_natten_macaron_block"


def _rms_norm(x, gain):
    xf = x.astype(jnp.float32)
    y = xf * lax.rsqrt(jnp.mean(xf * xf, axis=-1, keepdims=True) + RMS_EPS)
    return (y * gain.astype(jnp.float32)).astype(x.dtype)


def _swiglu(x, w_in, w_out):
    g, u = jnp.split(x @ w_in, 2, axis=-1)
    return (jax.nn.silu(g) * u) @ w_out


def _rotary(x, pos):
    half = x.shape[-1] // 2
    inv = 1.0 / (ROPE_BASE ** jnp.linspace(0.0, 1.0, half, dtype=jnp.float32))
    ang = pos[:, None] * inv[None, :]
    cos, sin = jnp.cos(ang), jnp.sin(ang)
    x1, x2 = x[..., :half], x[..., half:]
    return jnp.concatenate([x1 * cos - x2 * sin, x1 * sin + x2 * cos], axis=-1)


def _retention_one_dir(q, k, v, log_gamma, strict):
    B, H, S, DK = q.shape
    DV = v.shape[-1]
    C = RET_CHUNK
    N = S // C
    qc = q.reshape(B, H, N, C, DK)
    kc = k.reshape(B, H, N, C, DK)
    vc = v.reshape(B, H, N, C, DV)
    idx = jnp.arange(C, dtype=jnp.float32)
    diff = idx[:, None] - idx[None, :]
    lower = (diff > 0) if strict else (diff >= 0)
    intra = jnp.where(lower[None], jnp.exp(log_gamma[:, None, None] * jnp.where(lower, diff, 0.0)[None]), 0.0)
    scores = jnp.einsum('bhnqd,bhnkd->bhnqk', qc, kc) * intra[None, :, None]
    inner = jnp.einsum('bhnqk,bhnkv->bhnqv', scores, vc)
    k_dec = jnp.exp(log_gamma[:, None] * (C - 1 - idx)[None, :])
    upd = jnp.einsum('bhnkd,bhnkv->nbhdv', kc * k_dec[None, :, None, :, None], vc)
    chunk_dec = jnp.exp(log_gamma * C)[None, :, None, None]

    def step(state, u):
        return chunk_dec * state + u, state

    _, prev = lax.scan(step, jnp.zeros((B, H, DK, DV), jnp.float32), upd)
    q_dec = jnp.exp(log_gamma[:, None] * (idx + 1.0)[None, :])
    cross = jnp.einsum('bhnqd,nbhdv->bhnqv', qc * q_dec[None, :, None, :, None], prev)
    return (inner + cross).reshape(B, H, S, DV)


def _bidirectional_retention(q, k, v, decay_fwd_logit, decay_bwd_logit):
    lg_f = jax.nn.log_sigmoid(decay_fwd_logit.astype(jnp.float32))
    lg_b = jax.nn.log_sigmoid(decay_bwd_logit.astype(jnp.float32))
    y_f = _retention_one_dir(q, k, v, lg_f, strict=False)
    flip = lambda a: jnp.flip(a, axis=2)
    y_b = flip(_retention_one_dir(flip(q), flip(k), flip(v), lg_b, strict=True))
    return y_f + y_b


def _neighborhood_attention(q, k, v, rel_bias):
    B, S, NH, DH = q.shape
    rows = S // GRID_W
    kr = min(NA_WIN_ROWS, rows)
    ncb = GRID_W // NA_Q_BLOCK_COLS
    qg = (q * (DH ** -0.5)).reshape(B, rows, GRID_W, NH, DH)
    kg = k.reshape(B, rows, GRID_W, NH, DH)
    vg = v.reshape(B, rows, GRID_W, NH, DH)
    c0 = jnp.arange(ncb, dtype=jnp.int32) * NA_Q_BLOCK_COLS
    q_cols = c0[:, None] + jnp.arange(NA_Q_BLOCK_COLS, dtype=jnp.int32)[None, :]
    key_cols = (jnp.clip(c0 - NA_WIN_COLS // 2, 0, GRID_W - NA_K_BLOCK_COLS)[:, None]
                + jnp.arange(NA_K_BLOCK_COLS, dtype=jnp.int32)[None, :])
    win_start = jnp.clip(q_cols - NA_WIN_COLS // 2, 0, GRID_W - NA_WIN_COLS)
    kcb = key_cols[:, None, :]
    col_mask = (kcb >= win_start[:, :, None]) & (kcb < win_start[:, :, None] + NA_WIN_COLS)
    col_idx = jnp.clip(kcb - q_cols[:, :, None] + NA_WIN_COLS - 1, 0, NA_REL_COLS - 1)

    def row_block(r):
        rs = jnp.clip(r - kr // 2, 0, rows - kr)
        q_r = lax.dynamic_index_in_dim(qg, r, axis=1, keepdims=False).reshape(B, ncb, NA_Q_BLOCK_COLS, NH, DH)
        k_blk = lax.dynamic_slice_in_dim(kg, rs, kr, axis=1)[:, :, key_cols]
        v_blk = lax.dynamic_slice_in_dim(vg, rs, kr, axis=1)[:, :, key_cols]
        s = jnp.einsum('bnqhd,brnkhd->bhnqrk', q_r, k_blk).astype(jnp.float32)
        row_idx = rs + jnp.arange(kr, dtype=jnp.int32) - r + NA_WIN_ROWS - 1
        bias = rel_bias[:, row_idx[None, None, :, None], col_idx[:, :, None, :]].astype(jnp.float32)
        s = jnp.where(col_mask[:, :, None, :], s + bias, NEG_INF)
        p = jax.nn.softmax(s.reshape(B, NH, ncb, NA_Q_BLOCK_COLS, kr * NA_K_BLOCK_COLS), axis=-1)
        p = p.reshape(s.shape).astype(v.dtype)
        o = jnp.einsum('bhnqrk,brnkhd->bnqhd', p, v_blk)
        return o.reshape(B, GRID_W, NH, DH)

    out = lax.map(row_block, jnp.arange(rows, dtype=jnp.int32))
    return out.transpose(1, 0, 2, 3, 4).reshape(B, S, NH * DH)


def _token_mixing(u, w_in, decay_fwd, decay_bwd, rel_bias, w_ret_out, w_na_out, w_out, pos):
    B, S, _ = u.shape
    points = [sum(MIX_SPLITS[:i + 1]) for i in range(len(MIX_SPLITS) - 1)]
    rq, rk, rv, rg, nq, nk, nv, g_ret, g_na = jnp.split(u @ w_in, points, axis=-1)
    heads = lambda a, d: a.reshape(B, S, RET_HEADS, d).transpose(0, 2, 1, 3).astype(jnp.float32)
    q = _rotary(heads(rq, RET_QK_DIM), pos)
    k = _rotary(heads(rk, RET_QK_DIM), pos) * (RET_QK_DIM ** -0.5)
    y = _bidirectional_retention(q, k, heads(rv, RET_V_DIM), decay_fwd, decay_bwd)
    y = y * lax.rsqrt(jnp.mean(y * y, axis=-1, keepdims=True) + RMS_EPS)
    y = y.transpose(0, 2, 1, 3).reshape(B, S, RET_V_W).astype(u.dtype)
    y_ret = (jax.nn.silu(rg) * y) @ w_ret_out
    na_heads = lambda a: a.reshape(B, S, NA_HEADS, NA_HEAD_DIM)
    y_na = _neighborhood_attention(na_heads(nq), na_heads(nk), na_heads(nv), rel_bias) @ w_na_out
    merged = jax.nn.sigmoid(g_ret) * y_ret + jax.nn.sigmoid(g_na) * y_na
    return merged @ w_out


def setup_inputs(seed: int = 0) -> dict:
    key = jax.random.key(seed)
    ks = jax.random.split(key, 20)
    f32 = jnp.float32

    def w(k, shape, fan_in):
        return jax.random.normal(k, (DEPTH,) + shape, f32) * (fan_in ** -0.5)

    def gain(k):
        return 1.0 + 0.05 * jax.random.normal(k, (DEPTH, D_MODEL), f32)

    gamma0 = 1.0 - 2.0 ** (-5.0 - np.arange(RET_HEADS, dtype=np.float32))
    logit0 = jnp.asarray(np.log(gamma0 / (1.0 - gamma0)), f32)
    return {
        "x": jax.random.normal(ks[0], (BATCH, SEQ, D_MODEL), f32),
        "ffn1_pre_norm": gain(ks[1]),
        "ffn1_w_in": w(ks[2], (D_MODEL, 2 * D_FF), D_MODEL),
        "ffn1_w_out": w(ks[3], (D_FF, D_MODEL), D_FF),
        "ffn1_post_norm": gain(ks[4]),
        "mix_pre_norm": gain(ks[5]),
        "w_mix_in": w(ks[6], (D_MODEL, MIX_IN_W), D_MODEL),
        "ret_decay_fwd": logit0[None] + 0.1 * jax.random.normal(ks[7], (DEPTH, RET_HEADS), f32),
        "ret_decay_bwd": logit0[None] + 0.1 * jax.random.normal(ks[8], (DEPTH, RET_HEADS), f32),
        "na_rel_bias": 0.02 * jax.random.normal(ks[9], (DEPTH, NA_HEADS, NA_REL_ROWS, NA_REL_COLS), f32),
        "w_ret_out": w(ks[10], (RET_V_W, D_MODEL), RET_V_W),
        "w_na_out": w(ks[11], (NA_W, D_MODEL), NA_W),
        "w_mix_out": w(ks[12], (D_MODEL, D_MODEL), D_MODEL),
        "mix_post_norm": gain(ks[13]),
        "ffn2_pre_norm": gain(ks[14]),
        "ffn2_w_in": w(ks[15], (D_MODEL, 2 * D_FF), D_MODEL),
        "ffn2_w_out": w(ks[16], (D_FF, D_MODEL), D_FF),
        "ffn2_post_norm": gain(ks[17]),
    }


def reference(x, ffn1_pre_norm, ffn1_w_in, ffn1_w_out, ffn1_post_norm, mix_pre_norm, w_mix_in,
              ret_decay_fwd, ret_decay_bwd, na_rel_bias, w_ret_out, w_na_out, w_mix_out, mix_post_norm,
              ffn2_pre_norm, ffn2_w_in, ffn2_w_out, ffn2_post_norm):
    S = x.shape[1]
    pos = jnp.arange(S, dtype=jnp.float32)
    for l in range(DEPTH):
        h = _swiglu(_rms_norm(x, ffn1_pre_norm[l]), ffn1_w_in[l], ffn1_w_out[l])
        x = x + 0.5 * _rms_norm(h, ffn1_post_norm[l])
        m = _token_mixing(_rms_norm(x, mix_pre_norm[l]), w_mix_in[l], ret_decay_fwd[l], ret_decay_bwd[l],
                          na_rel_bias[l], w_ret_out[l], w_na_out[l], w_mix_out[l], pos)
        x = x + _rms_norm(m, mix_post_norm[l])
        h = _swiglu(_rms_norm(x, ffn2_pre_norm[l]), ffn2_w_in[l], ffn2_w_out[l])
        x = x + 0.5 * _rms_norm(h, ffn2_post_norm[l])
    return x
```

```python
import math
import numpy as np
import concourse.bass as bass
import concourse.mybir as mybir
from concourse.bass_utils import run_bass_kernel_spmd

F32 = mybir.dt.float32
BF16 = mybir.dt.bfloat16
AF = mybir.ActivationFunctionType
ALU = mybir.AluOpType

D = 1024
S = 2048
NT = S // 128
DFF = 2816
NJ = DFF // 128
EPS = 1e-6
MIXW = 6656
N_CORES = 8
SEQ_PER_CORE = 4

PE, ACT, DVE, POOL, SP = "pe", "act", "dve", "pool", "sp"
ENGS = (PE, ACT, DVE, POOL, SP)


class Res:
    __slots__ = ("name", "w", "r")

    def __init__(self, name):
        self.name = name
        self.w = None
        self.r = {}


class Prog:
    def __init__(self, nc, n_dma_sems=32):
        self.nc = nc
        self.eng = {PE: nc.tensor, ACT: nc.scalar, DVE: nc.vector, POOL: nc.gpsimd, SP: nc.sync}
        self.streams = {e: [] for e in ENGS}
        self.cnt = {e: 0 for e in ENGS}
        self.waited = {e: {} for e in ENGS}
        self.sems = {}
        self.n_dma_sems = n_dma_sems
        self.dma_tot = [0] * n_dma_sems
        self.dma_rr = 0
        self.dma_rr_q = {}
        self.n_ops = 0

    def alloc_sems(self, stack):
        for e in (PE, ACT, DVE, POOL):
            self.sems[e] = stack.enter_context(self.nc.semaphore("s_" + e))
        for i in range(self.n_dma_sems):
            self.sems[("d", i)] = stack.enter_context(self.nc.semaphore("s_d%d" % i))

    def _need(self, e, key, val, waits):
        if key == e and e == PE:
            return
        if self.waited[e].get(key, 0) >= val:
            return
        self.waited[e][key] = val
        waits.append((key, val))

    def _deps(self, e, reads, writes):
        waits = []
        for r in reads:
            if r.w is not None:
                self._need(e, r.w[0], r.w[1], waits)
        for w in writes:
            if w.w is not None:
                self._need(e, w.w[0], w.w[1], waits)
            for k, v in w.r.items():
                self._need(e, k, v, waits)
        return waits

    def _mark(self, key, val, reads, writes):
        for r in reads:
            if r.r.get(key, 0) < val:
                r.r[key] = val
        for w in writes:
            w.w = (key, val)
            w.r = {}

    def op(self, e, fn, reads=(), writes=(), inc=True):
        waits = self._deps(e, reads, writes)
        val = self.cnt[e] + 1
        if inc:
            self.cnt[e] = val
        self.streams[e].append((waits, fn, (e, 1) if inc else None))
        self._mark(e, val, reads, writes)
        self.n_ops += 1

    def dma(self, q, out_ap, in_ap, reads=(), writes=()):
        if q == SP:
            lo, hi = 0, self.n_dma_sems - 8
        else:
            lo, hi = self.n_dma_sems - 8, self.n_dma_sems
        rr = self.dma_rr_q.get(q, lo)
        i = rr
        self.dma_rr_q[q] = lo + (rr + 1 - lo) % (hi - lo)
        key = ("d", i)
        waits = self._deps(q, reads, writes)
        if self.dma_tot[i] > 0:
            self._need(q, key, self.dma_tot[i], waits)
        self.dma_tot[i] += 16
        val = self.dma_tot[i]
        self.streams[q].append((waits, lambda eng: eng.dma_start(out=out_ap, in_=in_ap), (key, 16)))
        self._mark(key, val, reads, writes)
        self.n_ops += 1

    def barrier(self):
        for e in ENGS:
            waits = []
            for k in (PE, ACT, DVE, POOL):
                if self.cnt[k] > 0:
                    self._need(e, k, self.cnt[k], waits)
            for i in range(self.n_dma_sems):
                if self.dma_tot[i] > 0:
                    self._need(e, ("d", i), self.dma_tot[i], waits)
            if waits:
                self.streams[e].append((waits, None, None))

    def check_deadlock(self):
        val = {}
        pos = {e: 0 for e in ENGS}
        progress = True
        while progress:
            progress = False
            for e in ENGS:
                st = self.streams[e]
                while pos[e] < len(st):
                    waits, fn, inc = st[pos[e]]
                    if any(val.get(k, 0) < v for k, v in waits):
                        break
                    if inc is not None:
                        val[inc[0]] = val.get(inc[0], 0) + inc[1]
                    pos[e] += 1
                    progress = True
        stuck = {e: (pos[e], len(self.streams[e]), [(k, v, val.get(k, 0)) for k, v in self.streams[e][pos[e]][0] if val.get(k, 0) < v])
                 for e in ENGS if pos[e] < len(self.streams[e])}
        if stuck:
            raise RuntimeError("deadlock in emitted program: %r" % (stuck,))

    def emit(self, block):
        prog = self
        self.check_deadlock()

        def run(e):
            def body(eng):
                for waits, fn, inc in prog.streams[e]:
                    for key, val in waits:
                        eng.wait_ge(prog.sems[key], val)
                    if fn is not None:
                        ins = fn(eng)
                        if inc is not None:
                            ins.then_inc(prog.sems[inc[0]], inc[1])
            return body

        block.tensor(run(PE))
        block.scalar(run(ACT))
        block.vector(run(DVE))
        block.gpsimd(run(POOL))
        block.sync(run(SP))


WEIGHTS = {
    "ffn1_w_in": (D, 2 * DFF), "ffn1_w_out": (DFF, D),
    "w_mix_in": (D, MIXW), "w_ret_out": (D, D), "w_na_out": (512, D), "w_mix_out": (D, D),
    "ffn2_w_in": (D, 2 * DFF), "ffn2_w_out": (DFF, D),
}
GAINS = ["ffn1_pre_norm", "ffn1_post_norm", "mix_pre_norm", "mix_post_norm", "ffn2_pre_norm", "ffn2_post_norm"]


_UNAME = [0]


def uname(base):
    _UNAME[0] += 1
    return "%s_%d" % (base, _UNAME[0])


def bcast_rows(ap_1xn, n):
    return bass.AP(ap_1xn.tensor, ap_1xn.offset, [[0, 128], [1, n]])


def build(n_seq=SEQ_PER_CORE, stop_after="C", t_ffn=1024):
    from contextlib import ExitStack
    nc = bass.Bass("TRN2", target_bir_lowering=False)
    ntok = n_seq * S
    x_in = nc.dram_tensor("x", [ntok, D], F32, kind="ExternalInput").ap()
    y_out = nc.dram_tensor("y", [ntok, D], F32, kind="ExternalOutput").ap()
    w_in = {k: nc.dram_tensor(k, list(v), F32, kind="ExternalInput").ap() for k, v in WEIGHTS.items()}
    g_in = {k: nc.dram_tensor(k, [1, D], F32, kind="ExternalInput").ap() for k in GAINS}
    ident_in = nc.dram_tensor("ident", [128, 128], F32, kind="ExternalInput").ap()
    cst_in = nc.dram_tensor("cst", [128, NCST], F32, kind="ExternalInput").ap()
    rope_in = nc.dram_tensor("rope", [128, 2 * NT * 64], F32, kind="ExternalInput").ap()
    nab_in = nc.dram_tensor("nab", [NVAR * 128, 1024], F32, kind="ExternalInput").ap()
    dec_in = nc.dram_tensor("dec", [1, 8], F32, kind="ExternalInput").ap()
    sb_d = nc.dram_tensor("sb_scr", [NT * 128, 1024], BF16, kind="Internal").ap()
    nab_bf = nc.dram_tensor("nab_bf", [128, NVAR, 1024], BF16, kind="Internal").ap()
    w_bf = {k: nc.dram_tensor(k + "_bf", [128, v[0] // 128, v[1]], BF16, kind="Internal").ap()
            for k, v in WEIGHTS.items()}
    x1_d = nc.dram_tensor("x1_scr", [ntok, D], F32, kind="Internal").ap()
    x2_d = nc.dram_tensor("x2_scr", [ntok, D], F32, kind="Internal").ap()

    P = Prog(nc)
    with ExitStack() as top:
        P.alloc_sems(top)
        r_wbf = {k: Res("wbf_" + k) for k in WEIGHTS}
        r_nabbf = Res("nabbf")

        def cast_jobs(name, src3, dst3, r_dst, nkc, N, nk_step):
            jobs = []
            for k0 in range(0, nkc, nk_step):
                k1 = min(nkc, k0 + nk_step)
                for c0 in range(0, N, 2048):
                    c1 = min(N, c0 + 2048)
                    jobs.append((dst3[:, k0:k1, c0:c1], src3[:, k0:k1, c0:c1], r_dst))
            return jobs

        def wsrc(name):
            return w_in[name].rearrange("(kc p) n -> p kc n", p=128)

        front = cast_jobs("ffn1_w_in", wsrc("ffn1_w_in"), w_bf["ffn1_w_in"], r_wbf["ffn1_w_in"], 8, 2 * DFF, 1)
        front += cast_jobs("ffn1_w_out", wsrc("ffn1_w_out"), w_bf["ffn1_w_out"], r_wbf["ffn1_w_out"], NJ, D, 2)
        bg_jobs = cast_jobs("w_mix_in", wsrc("w_mix_in"), w_bf["w_mix_in"], r_wbf["w_mix_in"], 8, MIXW, 1)
        bg_jobs += cast_jobs("w_ret_out", wsrc("w_ret_out"), w_bf["w_ret_out"], r_wbf["w_ret_out"], 8, D, 2)
        bg_jobs += cast_jobs("w_na_out", wsrc("w_na_out"), w_bf["w_na_out"], r_wbf["w_na_out"], 4, D, 2)
        bg_jobs += cast_jobs("w_mix_out", wsrc("w_mix_out"), w_bf["w_mix_out"], r_wbf["w_mix_out"], 8, D, 2)
        bg_jobs += cast_jobs("nab", nab_in.rearrange("(v p) n -> p v n", p=128), nab_bf, r_nabbf, NVAR, 1024, 2)
        bg_jobs += cast_jobs("ffn2_w_in", wsrc("ffn2_w_in"), w_bf["ffn2_w_in"], r_wbf["ffn2_w_in"], 8, 2 * DFF, 1)
        bg_jobs += cast_jobs("ffn2_w_out", wsrc("ffn2_w_out"), w_bf["ffn2_w_out"], r_wbf["ffn2_w_out"], NJ, D, 2)
        for dst, src, r_dst in front:
            P.dma(POOL, dst, src, writes=[r_dst])

        MT = top.enter_context(nc.sbuf_tensor("g_MT", [128, 512], F32))
        DFt = top.enter_context(nc.sbuf_tensor("g_DFt", [128, 512], F32))
        DBt = top.enter_context(nc.sbuf_tensor("g_DBt", [128, 512], F32))
        dtok = top.enter_context(nc.sbuf_tensor("g_dtok", [128, 8], F32))
        gC = top.enter_context(nc.sbuf_tensor("g_gC", [128, 8], F32))
        tabs = (MT, DFt, DBt, dtok, gC)
        with ExitStack() as st:
            cst = st.enter_context(nc.sbuf_tensor("g_cst", [128, NCST], F32))
            lgt = st.enter_context(nc.sbuf_tensor("g_lgt", [128, 8], F32))
            tA = st.enter_context(nc.sbuf_tensor("g_tA", [128, 512], F32))
            tB = st.enter_context(nc.sbuf_tensor("g_tB", [128, 512], F32))
            r_c = Res("gconst")
            P.dma(SP, cst[:], cst_in[:, :], writes=[r_c])
            P.dma(SP, lgt[:], bcast_rows(dec_in, 8), writes=[r_c])
            P.op(ACT, lambda e: e.activation(out=lgt[:], in_=lgt[:], func=AF.Exp, scale=-1.0), reads=[r_c], writes=[r_c])
            P.op(ACT, lambda e: e.activation(out=lgt[:], in_=lgt[:], func=AF.Ln, bias=1.0), reads=[r_c], writes=[r_c])
            P.op(DVE, lambda e: e.tensor_scalar(out=lgt[:], in0=lgt[:], scalar1=-1.0, scalar2=None, op0=ALU.mult), reads=[r_c], writes=[r_c])
            for h in range(4):
                hs = slice(h * 128, (h + 1) * 128)
                P.op(ACT, lambda e, h=h: e.activation(out=tA[:, 0:128], in_=cst[:, 128:256], func=AF.Exp, scale=lgt[:, h:h + 1]), reads=[r_c], writes=[r_c])
                P.op(DVE, lambda e, hs=hs: e.tensor_tensor(out=MT[:, hs], in0=tA[:, 0:128], in1=cst[:, 384:512], op=ALU.mult), reads=[r_c], writes=[r_c])
                P.op(ACT, lambda e, h=h: e.activation(out=tB[:, 0:128], in_=cst[:, 256:384], func=AF.Exp, scale=lgt[:, 4 + h:5 + h]), reads=[r_c], writes=[r_c])
                P.op(DVE, lambda e: e.tensor_tensor(out=tB[:, 0:128], in0=tB[:, 0:128], in1=cst[:, 512:640], op=ALU.mult), reads=[r_c], writes=[r_c])
                P.op(DVE, lambda e, hs=hs: e.tensor_tensor(out=MT[:, hs], in0=MT[:, hs], in1=tB[:, 0:128], op=ALU.add), reads=[r_c], writes=[r_c])
                P.op(ACT, lambda e, h=h, hs=hs: e.activation(out=DFt[:, hs], in_=cst[:, 640:768], func=AF.Exp, scale=lgt[:, h:h + 1]), reads=[r_c], writes=[r_c])
                P.op(ACT, lambda e, h=h, hs=hs: e.activation(out=DBt[:, hs], in_=cst[:, 768:896], func=AF.Exp, scale=lgt[:, 4 + h:5 + h]), reads=[r_c], writes=[r_c])
            P.op(DVE, lambda e: e.tensor_scalar(out=DFt[:], in0=DFt[:], scalar1=RET_SCALE, scalar2=None, op0=ALU.mult), reads=[r_c], writes=[r_c])
            P.op(DVE, lambda e: e.tensor_scalar(out=DBt[:], in0=DBt[:], scalar1=RET_SCALE, scalar2=None, op0=ALU.mult), reads=[r_c], writes=[r_c])
            P.op(ACT, lambda e: e.activation(out=dtok[:, 0:4], in_=lgt[:, 0:4], func=AF.Exp, scale=cst[:, 896:897]), reads=[r_c], writes=[r_c])
            P.op(ACT, lambda e: e.activation(out=dtok[:, 4:8], in_=lgt[:, 4:8], func=AF.Exp, scale=cst[:, 897:898]), reads=[r_c], writes=[r_c])
            P.op(ACT, lambda e: e.activation(out=gC[:], in_=lgt[:], func=AF.Exp, scale=128.0), reads=[r_c], writes=[r_c])
            P.barrier()

        ffn_phase(nc, P, x_in, x1_d if stop_after != "A" else y_out,
                  w_bf["ffn1_w_in"], w_bf["ffn1_w_out"], r_wbf["ffn1_w_in"], r_wbf["ffn1_w_out"],
                  g_in["ffn1_pre_norm"], g_in["ffn1_post_norm"], ident_in, t_ffn, "f1", ntok, bg_jobs)
        for dst, src, r_dst in bg_jobs:
            P.dma(POOL, dst, src, writes=[r_dst])
        P.barrier()
        if stop_after != "A":
            for sq in range(n_seq):
                tok0 = sq * S
                mix_phase(nc, P, x1_d[tok0:tok0 + S, :], x2_d[tok0:tok0 + S, :] if stop_after != "B" else y_out[tok0:tok0 + S, :],
                          w_bf, r_wbf, g_in["mix_pre_norm"], g_in["mix_post_norm"], cst_in, rope_in, (nab_bf, r_nabbf), tabs, sb_d, "mxs%d" % sq)
                P.barrier()
            if stop_after != "B":
                ffn_phase(nc, P, x2_d, y_out,
                          w_bf["ffn2_w_in"], w_bf["ffn2_w_out"], r_wbf["ffn2_w_in"], r_wbf["ffn2_w_out"],
                          g_in["ffn2_pre_norm"], g_in["ffn2_post_norm"], ident_in, t_ffn, "f2", ntok)
                P.barrier()

        P.barrier()
        with nc.Block() as block:
            P.emit(block)
    return nc


def norm_rstd(P, pool_ssq, pool_rstd, r_ssq, r_rstd, neg_half, r_consts, n, inv_n):
    P.op(POOL, lambda e: e.tensor_scalar(out=pool_rstd, in0=pool_ssq, scalar1=inv_n, scalar2=EPS, op0=ALU.mult, op1=ALU.add),
         reads=[r_ssq], writes=[r_rstd])
    P.op(POOL, lambda e: e.tensor_tensor(out=pool_rstd, in0=pool_rstd, in1=neg_half[:, 0:n], op=ALU.pow),
         reads=[r_rstd, r_consts], writes=[r_rstd])


def ffn_phase(nc, P, x_src, x_dst, win_bf, wout_bf, r_win, r_wout, g_pre, g_post, ident_in, T, tag, ntok=S, bg_jobs=None):
    from contextlib import ExitStack
    NG = ntok // T
    TT = T // 128
    TB = T // 512
    NWB = 3
    with ExitStack() as st:
        sb = lambda name, shape, dt: st.enter_context(nc.sbuf_tensor(uname(tag + name), shape, dt))
        ps = lambda name, shape, dt: st.enter_context(nc.psum_tensor(uname(tag + name), shape, dt))
        NXN, NXR = 3, 2
        xn = [sb("xn%d" % i, [128, D], F32) for i in range(NXN)]
        r_xn = [Res("xn") for _ in range(NXN)]
        xr = [sb("xr%d" % i, [128, D], F32) for i in range(NXR)]
        r_xr = [Res("xr") for _ in range(NXR)]
        uT = [sb("uT%d" % i, [128, 8, T], BF16) for i in range(2)]
        r_uT = [[Res("uT") for _ in range(TB)] for _ in range(2)]
        hT = sb("hT", [128, NJ, T], BF16)
        r_hT = Res("hT")
        wout = sb("wout", [128, NJ, D], BF16)
        r_wo = Res("wout")
        wg = [sb("wg%d" % i, [128, 8, 256], BF16) for i in range(NWB)]
        wu = [sb("wu%d" % i, [128, 8, 256], BF16) for i in range(NWB)]
        r_w = [Res("w") for _ in range(NWB)]
        gpre = sb("gpre", [128, D], F32)
        gpost = sb("gpost", [128, D], F32)
        ident_f = sb("identf", [128, 128], F32)
        ident = sb("ident", [128, 128], BF16)
        neg_half = sb("nh", [128, 8], F32)
        r_c = Res("consts")
        ub = [sb("ub%d" % i, [128, D], BF16) for i in range(2)]
        r_ub = [Res("ub") for _ in range(2)]
        junk = sb("junk", [128, D], BF16)
        r_junk = Res("junk")
        ssq = [sb("ssq%d" % i, [128, 1], F32) for i in range(4)]
        rstd = [sb("rstd%d" % i, [128, 1], F32) for i in range(4)]
        r_ssq = [Res("ssq") for _ in range(4)]
        r_rstd = [Res("rstd") for _ in range(4)]
        sg = [sb("sg%d" % i, [128, 512], F32) for i in range(2)]
        r_sg = [Res("sg") for _ in range(2)]
        tmp = [sb("tmp%d" % i, [128, D], F32) for i in range(2)]
        r_tmp = [Res("tmp") for _ in range(2)]
        p2 = [ps("p2_%d" % i, [128, 512], F32) for i in range(3)]
        r_p2 = [Res("p2") for _ in range(3)]
        p3 = [ps("p3_%d" % i, [128, D], F32) for i in range(2)]
        r_p3 = [Res("p3") for _ in range(2)]
        pT = ps("pT", [128, D], BF16)
        r_pT = Res("pT")

        P.dma(SP, gpre[:], bcast_rows(g_pre, D), writes=[r_c])
        P.dma(SP, gpost[:], bcast_rows(g_post, D), writes=[r_c])
        P.dma(SP, ident_f[:], ident_in[:, :], writes=[r_c])
        P.op(DVE, lambda e: e.tensor_copy(out=ident[:], in_=ident_f[:]), reads=[r_c], writes=[r_c])
        P.op(DVE, lambda e: e.tensor_scalar(out=gpost[:], in0=gpost[:], scalar1=0.5, scalar2=None, op0=ALU.mult), reads=[r_c], writes=[r_c])
        P.op(DVE, lambda e: e.memset(neg_half[:], -0.5), writes=[r_c])

        cnt = {"n": 0, "p2": 0, "p3": 0, "w": 0, "sg": 0, "tmp": 0, "xn": 0, "xr": 0}

        nrec = {}

        def stage_x(g, t):
            k = cnt["n"] % 4
            kb = cnt["n"] % 2
            cnt["n"] += 1
            ix = cnt["xn"] % NXN
            cnt["xn"] += 1
            nrec[(g, t)] = kb
            r0 = g * T + t * 128
            P.dma(SP, xn[ix][:], x_src[r0:r0 + 128, :], writes=[r_xn[ix]])
            xin = xn[ix][:]
            P.op(ACT, lambda e, xin=xin, k=k: e.activation(out=junk[:], in_=xin, func=AF.Square, accum_out=ssq[k][:]),
                 reads=[r_xn[ix]], writes=[r_junk, r_ssq[k]])
            norm_rstd(P, ssq[k][:], rstd[k][:], r_ssq[k], r_rstd[k], neg_half, r_c, 1, 1.0 / D)
            P.op(DVE, lambda e, xin=xin, k=k, kb=kb: e.scalar_tensor_tensor(out=ub[kb][:], in0=xin, scalar=rstd[k][:], in1=gpre[:], op0=ALU.mult, op1=ALU.mult),
                 reads=[r_xn[ix], r_rstd[k], r_c], writes=[r_ub[kb]])

        def stage_y(g, t):
            b = g % 2
            kb = nrec[(g, t)]
            for kc in range(8):
                P.op(PE, lambda e, kb=kb, kc=kc: e.transpose(out=pT[:, kc * 128:(kc + 1) * 128], in_=ub[kb][:, kc * 128:(kc + 1) * 128], identity=ident[:]),
                     reads=[r_ub[kb], r_c], writes=[r_pT], inc=(kc == 7))
            P.op(ACT, lambda e, b=b, t=t: e.activation(out=uT[b][:, :, t * 128:(t + 1) * 128], in_=pT[:].rearrange("p (k c) -> p k c", k=8), func=AF.Copy),
                 reads=[r_pT], writes=[r_uT[b][t // 4]])

        def norm_group(g):
            stage_x(g, 0)
            for t in range(TT):
                if t + 1 < TT:
                    stage_x(g, t + 1)
                stage_y(g, t)

        def load_w(jb):
            s = cnt["w"] % NWB
            cnt["w"] += 1
            c0 = jb * 256
            P.dma(SP, wg[s][:], win_bf[:, :, c0:c0 + 256], reads=[r_win], writes=[r_w[s]])
            P.dma(SP, wu[s][:], win_bf[:, :, DFF + c0:DFF + c0 + 256], reads=[r_win], writes=[r_w[s]])
            return s

        NJB = NJ // 2
        norm_group(0)
        slots = [load_w(0), load_w(1)]
        for g in range(NG):
            b = g % 2
            for jb in range(NJB):
                s = slots.pop(0)
                nxt = jb + 2
                if nxt < NJB:
                    slots.append(load_w(nxt))
                elif g + 1 < NG:
                    slots.append(load_w(nxt - NJB))
                if jb == 0:
                    for j0 in range(0, NJ, 6):
                        j1 = min(NJ, j0 + 6)
                        P.dma(SP, wout[:, j0:j1, :], wout_bf[:, j0:j1, :], reads=[r_wout], writes=[r_wo])
                for jj in range(2):
                    j = jb * 2 + jj
                    for tb in range(TB):
                        ig = cnt["p2"] % 3
                        iu = (cnt["p2"] + 1) % 3
                        cnt["p2"] += 2
                        for kc in range(8):
                            P.op(PE, lambda e, ig=ig, s=s, jj=jj, kc=kc, b=b, tb=tb: e.matmul(
                                p2[ig][:], lhsT=wg[s][:, kc, jj * 128:(jj + 1) * 128], rhs=uT[b][:, kc, tb * 512:(tb + 1) * 512],
                                start=(kc == 0), stop=(kc == 7)),
                                reads=[r_w[s], r_uT[b][tb]], writes=[r_p2[ig]], inc=(kc == 7))
                        for kc in range(8):
                            P.op(PE, lambda e, iu=iu, s=s, jj=jj, kc=kc, b=b, tb=tb: e.matmul(
                                p2[iu][:], lhsT=wu[s][:, kc, jj * 128:(jj + 1) * 128], rhs=uT[b][:, kc, tb * 512:(tb + 1) * 512],
                                start=(kc == 0), stop=(kc == 7)),
                                reads=[r_w[s], r_uT[b][tb]], writes=[r_p2[iu]], inc=(kc == 7))
                        k = cnt["sg"] % 2
                        cnt["sg"] += 1
                        P.op(ACT, lambda e, k=k, ig=ig: e.activation(out=sg[k][:], in_=p2[ig][:], func=AF.Silu),
                             reads=[r_p2[ig]], writes=[r_sg[k]])
                        P.op(DVE, lambda e, k=k, iu=iu, j=j, tb=tb: e.tensor_tensor(out=hT[:, j, tb * 512:(tb + 1) * 512], in0=p2[iu][:], in1=sg[k][:], op=ALU.mult),
                             reads=[r_p2[iu], r_sg[k]], writes=[r_hT])
                if bg_jobs:
                    for _ in range(min(2, len(bg_jobs))):
                        dst_, src_, r_dst_ = bg_jobs.pop(0)
                        P.dma(POOL, dst_, src_, writes=[r_dst_])
                if g + 1 < NG:
                    if 1 <= jb <= TT:
                        stage_x(g + 1, jb - 1)
                    if 2 <= jb <= TT + 1:
                        stage_y(g + 1, jb - 2)
            for t in range(TT):
                ip = cnt["p3"] % 2
                cnt["p3"] += 1
                ir = cnt["xr"] % NXR
                cnt["xr"] += 1
                r0 = g * T + t * 128
                if g == 0 and t == 0:
                    P.dma(SP, xr[ir][:], x_src[r0:r0 + 128, :], writes=[r_xr[ir]])
                for half in range(2):
                    for j in range(NJ):
                        P.op(PE, lambda e, ip=ip, half=half, j=j, t=t: e.matmul(
                            p3[ip][:, half * 512:(half + 1) * 512], lhsT=hT[:, j, t * 128:(t + 1) * 128], rhs=wout[:, j, half * 512:(half + 1) * 512],
                            start=(j == 0), stop=(j == NJ - 1)),
                            reads=[r_hT, r_wo], writes=[r_p3[ip]], inc=(half == 1 and j == NJ - 1))
                nt_ = g * TT + t + 1
                if nt_ < NG * TT:
                    irn = cnt["xr"] % NXR
                    P.dma(SP, xr[irn][:], x_src[nt_ * 128:(nt_ + 1) * 128, :], writes=[r_xr[irn]])
                k = cnt["n"] % 4
                cnt["n"] += 1
                kt = cnt["tmp"] % 2
                cnt["tmp"] += 1
                P.op(ACT, lambda e, ip=ip, k=k: e.activation(out=junk[:], in_=p3[ip][:], func=AF.Square, accum_out=ssq[k][:]),
                     reads=[r_p3[ip]], writes=[r_junk, r_ssq[k]])
                norm_rstd(P, ssq[k][:], rstd[k][:], r_ssq[k], r_rstd[k], neg_half, r_c, 1, 1.0 / D)
                P.op(DVE, lambda e, ip=ip, k=k, kt=kt: e.scalar_tensor_tensor(out=tmp[kt][:], in0=p3[ip][:], scalar=rstd[k][:], in1=gpost[:], op0=ALU.mult, op1=ALU.mult),
                     reads=[r_p3[ip], r_rstd[k], r_c], writes=[r_tmp[kt]])
                P.op(POOL, lambda e, ir=ir, kt=kt: e.tensor_tensor(out=xr[ir][:], in0=xr[ir][:], in1=tmp[kt][:], op=ALU.add),
                     reads=[r_tmp[kt], r_xr[ir]], writes=[r_xr[ir]])
                P.dma(SP, x_dst[r0:r0 + 128, :], xr[ir][:], reads=[r_xr[ir]])


def common_inputs(inputs):
    common = {k: np.ascontiguousarray(inputs[k][0], dtype=np.float32) for k in WEIGHTS}
    for k in GAINS:
        common[k] = np.ascontiguousarray(inputs[k], dtype=np.float32).reshape(1, D)
    common["ident"] = np.eye(128, dtype=np.float32)
    common["cst"] = make_cst()
    common["rope"] = make_rope()
    common["nab"] = make_nab(np.asarray(inputs["na_rel_bias"])[0])
    common["dec"] = np.concatenate([np.asarray(inputs["ret_decay_fwd"], np.float32).reshape(4),
                                    np.asarray(inputs["ret_decay_bwd"], np.float32).reshape(4)]).reshape(1, 8)
    return common


def kernel(**inputs):
    x = np.ascontiguousarray(inputs["x"], dtype=np.float32)
    B = x.shape[0]
    assert B == N_CORES * SEQ_PER_CORE
    nc = build()
    common = common_inputs(inputs)
    in_maps = []
    for c in range(N_CORES):
        m = dict(common)
        m["x"] = x[c * SEQ_PER_CORE:(c + 1) * SEQ_PER_CORE].reshape(SEQ_PER_CORE * S, D)
        in_maps.append(m)
    res = run_bass_kernel_spmd(nc, in_maps, core_ids=list(range(N_CORES)))
    out = np.stack([r["y"].reshape(SEQ_PER_CORE, S, D) for r in res.results], axis=0)
    return out.reshape(B, S, D).astype(np.float32)


RET_SCALE = 128.0 ** -0.5
NCST = 898


def make_cst():
    c = np.zeros((128, NCST), np.float32)
    c[:, 0:128] = np.eye(128, dtype=np.float32)
    k = np.arange(128)[:, None]
    q = np.arange(128)[None, :]
    c[:, 128:256] = np.maximum(q - k, 0)
    c[:, 256:384] = np.maximum(k - q, 0)
    c[:, 384:512] = np.where(q >= k, RET_SCALE, 0.0)
    c[:, 512:640] = np.where(k > q, RET_SCALE, 0.0)
    c[:, 640:768] = np.broadcast_to(q + 1, (128, 128))
    c[:, 768:896] = np.broadcast_to(128 - q, (128, 128))
    c[:, 896] = 127 - np.arange(128)
    c[:, 897] = np.arange(128)
    return c


def make_rope():
    inv = (np.float32(1.0) / (np.float32(10000.0) ** np.linspace(0.0, 1.0, 64, dtype=np.float32))).astype(np.float32)
    pos = np.arange(S, dtype=np.float32)
    ang = (pos[:, None] * inv[None, :]).astype(np.float32)
    cos = np.cos(ang).astype(np.float32).reshape(NT, 128, 64).transpose(1, 0, 2)
    sin = np.sin(ang).astype(np.float32).reshape(NT, 128, 64).transpose(1, 0, 2)
    return np.ascontiguousarray(np.concatenate([cos.reshape(128, NT * 64), sin.reshape(128, NT * 64)], axis=1))


def na_offsets(a):
    rows = []
    for qr in (0, 1):
        rq = 2 * a + qr
        rs = min(max(rq - 4, 0), 24)
        rows += [rs, rs + 7]
    return list(range(min(rows) // 2 - a, max(rows) // 2 - a + 1))


def _na_variants():
    var_of, idxs, seen = {}, [], {}
    ck = np.arange(64)[:, None]
    cq = np.arange(64)[None, :]
    ws = np.clip(cq - 8, 0, 48)
    cvalid = (ck >= ws) & (ck < ws + 16)
    ci = np.clip(ck - cq + 15, 0, 30)
    for a in range(16):
        for o in na_offsets(a):
            idx = np.full((128, 128), -1, np.int64)
            for kr in (0, 1):
                for qr in (0, 1):
                    rk = 2 * (a + o) + kr
                    rq = 2 * a + qr
                    rs = min(max(rq - 4, 0), 24)
                    if not (rs <= rk < rs + 8):
                        continue
                    ri = rk - rq + 7
                    idx[kr * 64:(kr + 1) * 64, qr * 64:(qr + 1) * 64] = np.where(cvalid, ri * 31 + ci, -1)
            key = idx.tobytes()
            if key not in seen:
                seen[key] = len(idxs)
                idxs.append(idx)
            var_of[(a, o)] = seen[key]
    return var_of, idxs


NA_VAR_OF, NA_IDX = _na_variants()
NVAR = len(NA_IDX)
NA_NEG = -30000.0


def make_nab(rel_bias):
    rb = np.asarray(rel_bias, np.float32).reshape(8, 15 * 31)
    out = np.empty((NVAR, 128, 8, 128), np.float32)
    for v, idx in enumerate(NA_IDX):
        safe = np.where(idx >= 0, idx, 0)
        for h in range(8):
            out[v, :, h, :] = np.where(idx >= 0, rb[h][safe], np.float32(NA_NEG))
    return np.ascontiguousarray(out.reshape(NVAR * 128, 1024))


def bc_ap(base, dims):
    return bass.AP(base.tensor, base.offset, [list(base.ap[0])] + [list(d) for d in dims])


C_RQ, C_RK, C_RV, C_RG, C_NQ, C_NK, C_NV, C_GR, C_GN = 0, 512, 1024, 2048, 3072, 3584, 4096, 4608, 5632


_DBG = {'n': 0, 'max': 99}


def _dbg_run():
    _DBG['n'] += 1
    return _DBG['n'] <= _DBG['max']


def mix_phase(nc, P, x_src, x_dst, w_bf, r_wbf, g_pre, g_post, cst_in, rope_in, nab_in, tabs, sb_d, tag):
    MT, DFt, DBt, dtok, gC = tabs
    from contextlib import ExitStack
    wmi, r_wmi = w_bf["w_mix_in"], r_wbf["w_mix_in"]
    with ExitStack() as M:
        sbM = lambda name, shape, dt: M.enter_context(nc.sbuf_tensor(uname(tag + name), shape, dt))
        uT = sbM("uT", [128, 8, S], BF16)
        r_uT = Res("uT")
        VA = sbM("VA", [128, NT, 1024], BF16)
        r_VA = [Res("VA") for _ in range(NT)]
        onaT = sbM("onaT", [128, 4, S], BF16)
        r_onaT = Res("onaT")
        cst = sbM("cst", [128, NCST], F32)
        ident = sbM("ident", [128, 128], BF16)
        neg_half = sbM("nh", [128, 8], F32)
        r_c = Res("c")
        P.dma(SP, cst[:], cst_in[:, :], writes=[r_c])
        P.op(DVE, lambda e: e.tensor_copy(out=ident[:], in_=cst[:, 0:128]), reads=[r_c], writes=[r_c])
        P.op(DVE, lambda e: e.memset(neg_half[:], -0.5), writes=[r_c])

        def _subphase():
            with ExitStack() as st:
                sb = lambda name, shape, dt: st.enter_context(nc.sbuf_tensor(uname(tag + name), shape, dt))
                ps = lambda name, shape, dt: st.enter_context(nc.psum_tensor(uname(tag + name), shape, dt))
                gpre = sb("gpre", [128, D], F32)
                xn = [sb("xn%d" % i, [128, D], F32) for i in range(6)]
                r_xn = [Res("xn") for _ in range(6)]
                ub = [sb("ub%d" % i, [128, D], BF16) for i in range(2)]
                r_ub = [Res("ub") for _ in range(2)]
                junk = sb("junk", [128, D], BF16)
                r_junk = Res("junk")
                ssq = [sb("ssq%d" % i, [128, 1], F32) for i in range(4)]
                rstd = [sb("rstd%d" % i, [128, 1], F32) for i in range(4)]
                r_ssq = [Res("ssq") for _ in range(4)]
                r_rstd = [Res("rstd") for _ in range(4)]
                pT = [ps("pT%d" % i, [128, D], BF16) for i in range(2)]
                r_pT = [Res("pT") for _ in range(2)]
                P.dma(SP, gpre[:], bcast_rows(g_pre, D), writes=[r_c])
                def m1_x(t):
                    ix, k, kb = t % 6, t % 4, t % 2
                    P.op(ACT, lambda e, ix=ix, k=k: e.activation(out=junk[:], in_=xn[ix][:], func=AF.Square, accum_out=ssq[k][:]),
                         reads=[r_xn[ix]], writes=[r_junk, r_ssq[k]])
                    norm_rstd(P, ssq[k][:], rstd[k][:], r_ssq[k], r_rstd[k], neg_half, r_c, 1, 1.0 / D)
                    P.op(DVE, lambda e, ix=ix, k=k, kb=kb: e.scalar_tensor_tensor(out=ub[kb][:], in0=xn[ix][:], scalar=rstd[k][:], in1=gpre[:], op0=ALU.mult, op1=ALU.mult),
                         reads=[r_xn[ix], r_rstd[k], r_c], writes=[r_ub[kb]])

                def m1_y(t):
                    kb = t % 2
                    for kc in range(8):
                        P.op(PE, lambda e, kb=kb, kc=kc: e.transpose(out=pT[kb][:, kc * 128:(kc + 1) * 128], in_=ub[kb][:, kc * 128:(kc + 1) * 128], identity=ident[:]),
                             reads=[r_ub[kb], r_c], writes=[r_pT[kb]], inc=(kc == 7))
                    P.op(ACT, lambda e, t=t, kb=kb: e.activation(out=uT[:, :, t * 128:(t + 1) * 128], in_=pT[kb][:].rearrange("p (k c) -> p k c", k=8), func=AF.Copy),
                         reads=[r_pT[kb]], writes=[r_uT])

                for t in range(min(6, NT)):
                    P.dma(SP, xn[t % 6][:], x_src[t * 128:(t + 1) * 128, :], writes=[r_xn[t % 6]])
                m1_x(0)
                for t in range(NT):
                    if t + 1 < NT:
                        m1_x(t + 1)
                    m1_y(t)
                    if t + 6 < NT:
                        P.dma(SP, xn[t % 6][:], x_src[(t + 6) * 128:(t + 7) * 128, :], writes=[r_xn[t % 6]])
                P.barrier()

        if _dbg_run():
            _subphase()

        def _subphase():
            with ExitStack() as st:
                sb = lambda name, shape, dt: st.enter_context(nc.sbuf_tensor(uname(tag + name), shape, dt))
                ps = lambda name, shape, dt: st.enter_context(nc.psum_tensor(uname(tag + name), shape, dt))
                Ktm = sb("Ktm", [128, NT, 512], BF16)
                r_K = [Res("K") for _ in range(NT)]
                wk = sb("wk", [128, 8, 512], BF16)
                wv = sb("wv", [128, 8, 1024], BF16)
                r_wk, r_wv = Res("wk"), Res("wv")
                rope = sb("rope", [128, 2 * NT * 64], F32)
                Sf32 = sb("Sf32", [128, 1024], F32)
                Sb32 = sb("Sb32", [128, 1024], F32)
                Sfb = sb("Sfb", [128, 1024], BF16)
                Sbb = [sb("Sbb%d" % i, [128, 1024], BF16) for i in range(2)]
                r_Sf32, r_Sb32, r_Sfb = Res("Sf32"), Res("Sb32"), Res("Sfb")
                r_Sbb = [Res("Sbb") for _ in range(2)]
                r_sbd = [Res("sbd") for _ in range(NT)]
                tA = sb("tA", [128, 512], F32)
                tB = sb("tB", [128, 512], F32)
                r_tA, r_tB = Res("tA"), Res("tB")
                qrot = sb("qrot", [128, 512], BF16)
                r_qrot = Res("qrot")
                kdec = sb("kdec", [128, 512], BF16)
                r_kdec = Res("kdec")
                qT = sb("qT", [128, 512], BF16)
                qfT = sb("qfT", [128, 512], BF16)
                qbT = sb("qbT", [128, 512], BF16)
                kT = sb("kT", [128, 512], BF16)
                r_qT, r_qfT, r_qbT, r_kT = Res("qT"), Res("qfT"), Res("qbT"), Res("kT")
                Sm = sb("Sm", [128, 512], BF16)
                r_Sm = Res("Sm")
                srg = sb("srg", [128, 1024], F32)
                r_srg = Res("srg")
                Abuf = sb("A", [128, 1024], BF16)
                r_A = Res("A")
                junk = sb("junk", [128, 256], BF16)
                r_junk = Res("junk")
                ssq4 = sb("ssq4", [128, 4], F32)
                rstd4 = sb("rstd4", [128, 4], F32)
                r_ssq4, r_rstd4 = Res("ssq4"), Res("rstd4")
                b0 = ps("b0", [128, 512], F32)
                pR = ps("pR", [128, 1024], F32)
                pTr = ps("pTr", [128, 1024], BF16)
                pY = ps("pY", [128, 1024], F32)
                pU = ps("pU", [128, 1024], F32)
                r_b0, r_pR, r_pTr, r_pY, r_pU = Res("b0"), Res("pR"), Res("pTr"), Res("pY"), Res("pU")

                P.dma(SP, rope[:], rope_in[:, :], writes=[r_c])

                if _DBG.get('rstop') == 'setup':
                    P.barrier()
                    return

                def rotary(dst, n, r_dst):
                    cb = rope[:, n * 64:(n + 1) * 64]
                    sn = rope[:, NT * 64 + n * 64:NT * 64 + (n + 1) * 64]
                    cosb = bc_ap(cb, [[0, 4], [0, 2], [1, 64]])
                    sinb = bc_ap(sn, [[0, 4], [1, 64]])
                    v4 = b0[:].rearrange("p (h t d) -> p h t d", h=4, t=2)
                    a4 = tA[:].rearrange("p (h t d) -> p h t d", h=4, t=2)
                    b4 = tB[:].rearrange("p (h t d) -> p h t d", h=4, t=2)
                    d4 = dst.rearrange("p (h t d) -> p h t d", h=4, t=2)
                    P.op(DVE, lambda e: e.tensor_tensor(out=a4, in0=v4, in1=cosb, op=ALU.mult), reads=[r_b0, r_c], writes=[r_tA])
                    P.op(DVE, lambda e: e.tensor_tensor(out=b4[:, :, 0, :], in0=v4[:, :, 1, :], in1=sinb, op=ALU.mult), reads=[r_b0, r_c], writes=[r_tB])
                    P.op(DVE, lambda e: e.tensor_tensor(out=b4[:, :, 1, :], in0=v4[:, :, 0, :], in1=sinb, op=ALU.mult), reads=[r_b0, r_c], writes=[r_tB])
                    P.op(POOL, lambda e: e.tensor_tensor(out=d4[:, :, 0, :], in0=a4[:, :, 0, :], in1=b4[:, :, 0, :], op=ALU.subtract), reads=[r_tA, r_tB], writes=[r_dst])
                    P.op(POOL, lambda e: e.tensor_tensor(out=d4[:, :, 1, :], in0=a4[:, :, 1, :], in1=b4[:, :, 1, :], op=ALU.add), reads=[r_tA, r_tB], writes=[r_dst])

                def proj_tok(dst_ps, r_dst, wbuf, r_w, n, ncols):
                    for half in range(ncols // 512):
                        for kc in range(8):
                            P.op(PE, lambda e, half=half, kc=kc: e.matmul(dst_ps[:, half * 512:(half + 1) * 512], lhsT=uT[:, kc, n * 128:(n + 1) * 128],
                                                                          rhs=wbuf[:, kc, half * 512:(half + 1) * 512], start=(kc == 0), stop=(kc == 7)),
                                 reads=[r_uT, r_w], writes=[r_dst], inc=(kc == 7 and half == ncols // 512 - 1))

                def state_update(S32, r_S32, goff, kdec_col0):
                    pass

                P.dma(SP, wk[:], wmi[:, :, C_RK:C_RK + 512], reads=[r_wmi], writes=[r_wk])
                P.dma(SP, wv[:], wmi[:, :, C_RV:C_RV + 1024], reads=[r_wmi], writes=[r_wv])
                P.op(DVE, lambda e: e.memset(Sb32[:], 0.0), writes=[r_Sb32])
                for n in range(NT - 1, -1, -1):
                    proj_tok(b0, r_b0, wk, r_wk, n, 512)
                    rotary(Ktm[:, n, :], n, r_K[n])
                    proj_tok(pR, r_pR, wv, r_wv, n, 1024)
                    P.op(ACT, lambda e, n=n: e.activation(out=VA[:, n, :], in_=pR[:], func=AF.Copy), reads=[r_pR], writes=[r_VA[n]])
                    i = n % 2
                    P.op(ACT, lambda e, i=i: e.activation(out=Sbb[i][:], in_=Sb32[:], func=AF.Copy), reads=[r_Sb32], writes=[r_Sbb[i]])
                    P.dma(SP, sb_d[n * 128:(n + 1) * 128, :], Sbb[i][:], reads=[r_Sbb[i]], writes=[r_sbd[n]])
                    if n > 0:
                        P.op(DVE, lambda e, n=n: e.tensor_tensor(out=kdec[:].rearrange("p (h d) -> p h d", h=4), in0=Ktm[:, n, :].rearrange("p (h d) -> p h d", h=4),
                                                                 in1=dtok[:, 4:8].to_broadcast([128, 4, 128]), op=ALU.mult),
                             reads=[r_K[n], r_c], writes=[r_kdec])
                        for h in range(4):
                            P.op(PE, lambda e, h=h, n=n: e.matmul(pU[:, h * 256:(h + 1) * 256], lhsT=kdec[:, h * 128:(h + 1) * 128], rhs=VA[:, n, h * 256:(h + 1) * 256], start=True, stop=True),
                                 reads=[r_kdec, r_VA[n]], writes=[r_pU], inc=(h == 3))
                        for h in range(4):
                            P.op(DVE, lambda e, h=h: e.scalar_tensor_tensor(out=Sb32[:, h * 256:(h + 1) * 256], in0=Sb32[:, h * 256:(h + 1) * 256], scalar=gC[:, 4 + h:5 + h],
                                                                             in1=pU[:, h * 256:(h + 1) * 256], op0=ALU.mult, op1=ALU.add),
                                 reads=[r_pU, r_Sb32, r_c], writes=[r_Sb32])

                if _DBG.get('rstop') == 'pass1':
                    P.barrier()
                    return
                P.dma(SP, wk[:], wmi[:, :, C_RQ:C_RQ + 512], reads=[r_wmi], writes=[r_wk])
                P.dma(SP, wv[:], wmi[:, :, C_RG:C_RG + 1024], reads=[r_wmi], writes=[r_wv])
                P.op(DVE, lambda e: e.memset(Sf32[:], 0.0), writes=[r_Sf32])
                P.op(DVE, lambda e: e.memset(Sfb[:], 0.0), writes=[r_Sfb])
                qrot2 = [qrot, sb("qrot1", [128, 512], BF16)]
                r_qrot2 = [r_qrot, Res("qrot1")]
                srg2 = [srg, sb("srg1", [128, 1024], F32)]
                r_srg2 = [r_srg, Res("srg1")]
                kdec2 = [kdec, sb("kdec1", [128, 512], BF16)]
                r_kdec2 = [r_kdec, Res("kdec1")]
                pU0, pU1 = pU[:, 0:512], pU[:, 512:1024]
                r_pU0, r_pU1 = Res("pU0"), Res("pU1")

                def stage_a(n):
                    i = n % 2
                    P.dma(SP, Sbb[i][:], sb_d[n * 128:(n + 1) * 128, :], reads=[r_sbd[n]], writes=[r_Sbb[i]])
                    proj_tok(b0, r_b0, wk, r_wk, n, 512)
                    rotary(qrot2[i][:], n, r_qrot2[i])
                    proj_tok(pR, r_pR, wv, r_wv, n, 1024)
                    P.op(ACT, lambda e, i=i: e.activation(out=srg2[i][:], in_=pR[:], func=AF.Silu), reads=[r_pR], writes=[r_srg2[i]])
                    P.op(DVE, lambda e, n=n, i=i: e.tensor_tensor(out=kdec2[i][:].rearrange("p (h d) -> p h d", h=4), in0=Ktm[:, n, :].rearrange("p (h d) -> p h d", h=4),
                                                                 in1=dtok[:, 0:4].to_broadcast([128, 4, 128]), op=ALU.mult),
                         reads=[r_K[n], r_c], writes=[r_kdec2[i]])

                def stage_t(n):
                    i = n % 2
                    for h in range(4):
                        P.op(PE, lambda e, h=h, i=i: e.transpose(out=pTr[:, h * 128:(h + 1) * 128], in_=qrot2[i][:, h * 128:(h + 1) * 128], identity=ident[:]),
                             reads=[r_qrot2[i], r_c], writes=[r_pTr], inc=False)
                    for h in range(4):
                        P.op(PE, lambda e, h=h, n=n: e.transpose(out=pTr[:, 512 + h * 128:512 + (h + 1) * 128], in_=Ktm[:, n, h * 128:(h + 1) * 128], identity=ident[:]),
                             reads=[r_K[n], r_c], writes=[r_pTr], inc=(h == 3))
                    P.op(ACT, lambda e: e.activation(out=qT[:], in_=pTr[:, 0:512], func=AF.Copy), reads=[r_pTr], writes=[r_qT])
                    P.op(ACT, lambda e: e.activation(out=kT[:], in_=pTr[:, 512:1024], func=AF.Copy), reads=[r_pTr], writes=[r_kT])
                    P.op(DVE, lambda e: e.tensor_tensor(out=qfT[:], in0=qT[:], in1=DFt[:], op=ALU.mult), reads=[r_qT, r_c], writes=[r_qfT])
                    P.op(DVE, lambda e: e.tensor_tensor(out=qbT[:], in0=qT[:], in1=DBt[:], op=ALU.mult), reads=[r_qT, r_c], writes=[r_qbT])

                def stage_s(n):
                    for h in range(4):
                        P.op(PE, lambda e, h=h: e.matmul(pU1[:, h * 128:(h + 1) * 128], lhsT=kT[:, h * 128:(h + 1) * 128], rhs=qT[:, h * 128:(h + 1) * 128], start=True, stop=True),
                             reads=[r_kT, r_qT], writes=[r_pU1], inc=(h == 3))
                    P.op(DVE, lambda e: e.tensor_tensor(out=Sm[:], in0=pU1, in1=MT[:], op=ALU.mult), reads=[r_pU1, r_c], writes=[r_Sm])

                def stage_u(n, half):
                    i = n % 2
                    for hh in range(2):
                        h = half * 2 + hh
                        P.op(PE, lambda e, h=h, hh=hh, n=n, i=i: e.matmul(pU0[:, hh * 256:(hh + 1) * 256], lhsT=kdec2[i][:, h * 128:(h + 1) * 128], rhs=VA[:, n, h * 256:(h + 1) * 256], start=True, stop=True),
                             reads=[r_kdec2[i], r_VA[n]], writes=[r_pU0], inc=(hh == 1))
                    for hh in range(2):
                        h = half * 2 + hh
                        P.op(DVE, lambda e, h=h, hh=hh: e.scalar_tensor_tensor(out=Sf32[:, h * 256:(h + 1) * 256], in0=Sf32[:, h * 256:(h + 1) * 256], scalar=gC[:, h:h + 1],
                                                                              in1=pU0[:, hh * 256:(hh + 1) * 256], op0=ALU.mult, op1=ALU.add),
                             reads=[r_pU0, r_Sf32, r_c], writes=[r_Sf32])

                def stage_y(n):
                    i = n % 2
                    for h in range(4):
                        vs = slice(h * 256, (h + 1) * 256)
                        hs = slice(h * 128, (h + 1) * 128)
                        P.op(PE, lambda e, vs=vs, hs=hs, n=n: e.matmul(pY[:, vs], lhsT=Sm[:, hs], rhs=VA[:, n, vs], start=True, stop=False),
                             reads=[r_Sm, r_VA[n]], writes=[r_pY], inc=False)
                        P.op(PE, lambda e, vs=vs, hs=hs: e.matmul(pY[:, vs], lhsT=qfT[:, hs], rhs=Sfb[:, vs], start=False, stop=False),
                             reads=[r_qfT, r_Sfb], writes=[r_pY], inc=False)
                        P.op(PE, lambda e, vs=vs, hs=hs, i=i: e.matmul(pY[:, vs], lhsT=qbT[:, hs], rhs=Sbb[i][:, vs], start=False, stop=True),
                             reads=[r_qbT, r_Sbb[i]], writes=[r_pY], inc=(h == 3))

                def stage_c(n):
                    i = n % 2
                    for h in range(4):
                        P.op(ACT, lambda e, h=h: e.activation(out=junk[:], in_=pY[:, h * 256:(h + 1) * 256], func=AF.Square, accum_out=ssq4[:, h:h + 1]),
                             reads=[r_pY], writes=[r_junk, r_ssq4])
                    norm_rstd(P, ssq4[:], rstd4[:], r_ssq4, r_rstd4, neg_half, r_c, 4, 1.0 / 256)
                    for h in range(4):
                        P.op(DVE, lambda e, h=h, i=i: e.scalar_tensor_tensor(out=Abuf[:, h * 256:(h + 1) * 256], in0=pY[:, h * 256:(h + 1) * 256], scalar=rstd4[:, h:h + 1],
                                                                            in1=srg2[i][:, h * 256:(h + 1) * 256], op0=ALU.mult, op1=ALU.mult),
                             reads=[r_pY, r_rstd4, r_srg2[i]], writes=[r_A])
                    P.op(ACT, lambda e: e.activation(out=Sfb[:], in_=Sf32[:], func=AF.Copy), reads=[r_Sf32], writes=[r_Sfb])

                def stage_at(n):
                    for kc in range(8):
                        P.op(PE, lambda e, kc=kc: e.transpose(out=pTr[:, kc * 128:(kc + 1) * 128], in_=Abuf[:, kc * 128:(kc + 1) * 128], identity=ident[:]),
                             reads=[r_A, r_c], writes=[r_pTr], inc=(kc == 7))
                    P.op(ACT, lambda e, n=n: e.activation(out=VA[:, n, :], in_=pTr[:], func=AF.Copy), reads=[r_pTr], writes=[r_VA[n]])

                stage_a(0)
                for n in range(NT):
                    stage_t(n)
                    if n + 1 < NT:
                        stage_a(n + 1)
                    if n >= 1:
                        stage_at(n - 1)
                    stage_s(n)
                    stage_u(n, 0)
                    stage_y(n)
                    stage_u(n, 1)
                    stage_c(n)
                stage_at(NT - 1)
                P.barrier()

        if _dbg_run():
            _subphase()

        def _subphase():
            with ExitStack() as st:
                sb = lambda name, shape, dt: st.enter_context(nc.sbuf_tensor(uname(tag + name), shape, dt))
                ps = lambda name, shape, dt: st.enter_context(nc.psum_tensor(uname(tag + name), shape, dt))
                nqT = sb("nqT", [128, 4, S], BF16)
                nkT = sb("nkT", [128, 4, S], BF16)
                nv = sb("nv", [128, NT, 8, 65], BF16)
                r_nq, r_nk, r_nv = Res("nq"), Res("nk"), Res("nv")
                Bt = sb("Bt", [128, NVAR, 1024], BF16)
                r_Bt = Res("Bt")
                wa = [sb("wa%d" % i, [128, 8, 512], BF16) for i in range(3)]
                r_wa = [Res("wa") for _ in range(3)]
                PT = [sb("PT%d" % i, [128, 1024], BF16) for i in range(2)]
                r_PT = [Res("PT") for _ in range(2)]
                ona = sb("ona", [128, 512], BF16)
                r_ona = Res("ona")
                rden = sb("rden", [128, 8], F32)
                r_rden = Res("rden")
                pS = [ps("pS%d" % i, [128, 1024], F32) for i in range(2)]
                r_pS = [Res("pS") for _ in range(2)]
                pO = ps("pO", [128, 2, 512], F32)
                r_pO = Res("pO")
                pTn = ps("pTn", [128, 1024], BF16)
                r_pTn = Res("pTn")

                for j, c0 in enumerate((C_NQ, C_NK, C_NV)):
                    P.dma(SP, wa[j][:], wmi[:, :, c0:c0 + 512], reads=[r_wmi], writes=[r_wa[j]])
                nabbf_ap, r_nabbf = nab_in
                P.dma(SP, Bt[:, 0:5, :], nabbf_ap[:, 0:5, :], reads=[r_nabbf], writes=[r_Bt])
                P.dma(SP, Bt[:, 5:NVAR, :], nabbf_ap[:, 5:NVAR, :], reads=[r_nabbf], writes=[r_Bt])
                P.op(DVE, lambda e: e.memset(nv[:], 1.0), writes=[r_nv])
                cnt = 0
                for j, (dst, r_dst, scale) in enumerate(((nqT, r_nq, 0.125), (nkT, r_nk, 1.0))):
                    for c in range(4):
                        for tb in range(4):
                            i = cnt % 2
                            cnt += 1
                            for kc in range(8):
                                P.op(PE, lambda e, i=i, j=j, c=c, kc=kc, tb=tb: e.matmul(pS[i][:, 0:512], lhsT=wa[j][:, kc, c * 128:(c + 1) * 128], rhs=uT[:, kc, tb * 512:(tb + 1) * 512],
                                                                                          start=(kc == 0), stop=(kc == 7)),
                                     reads=[r_wa[j], r_uT], writes=[r_pS[i]], inc=(kc == 7))
                            P.op(ACT, lambda e, i=i, dst=dst, c=c, tb=tb, scale=scale: e.activation(out=dst[:, c, tb * 512:(tb + 1) * 512], in_=pS[i][:, 0:512], func=AF.Copy, scale=scale),
                                 reads=[r_pS[i]], writes=[r_dst])
                for t in range(NT):
                    i = cnt % 2
                    cnt += 1
                    for kc in range(8):
                        P.op(PE, lambda e, i=i, kc=kc, t=t: e.matmul(pS[i][:, 0:512], lhsT=uT[:, kc, t * 128:(t + 1) * 128], rhs=wa[2][:, kc, :], start=(kc == 0), stop=(kc == 7)),
                             reads=[r_wa[2], r_uT], writes=[r_pS[i]], inc=(kc == 7))
                    P.op(ACT, lambda e, i=i, t=t: e.activation(out=nv[:, t, :, 0:64], in_=pS[i][:, 0:512].rearrange("p (h d) -> p h d", h=8), func=AF.Copy),
                         reads=[r_pS[i]], writes=[r_nv])
                nqM = [[sb("nqM%d_%d" % (s_, i), [128, 4, 128], BF16) for i in range(2)] for s_ in range(2)]
                r_nqM = [[Res("nqM") for _ in range(2)] for _ in range(2)]
                ona2 = [ona, sb("ona1", [128, 512], BF16)]
                r_ona2 = [r_ona, Res("ona1")]
                for s_ in range(2):
                    for i in range(2):
                        P.op(DVE, lambda e, s_=s_, i=i: e.memset(nqM[s_][i][:], 0.0), writes=[r_nqM[s_][i]])
                units = [(a, oi, o, len(na_offsets(a))) for a in range(NT) for oi, o in enumerate(na_offsets(a))]

                def na_scores(u):
                    a, oi, o, _ = units[u]
                    sa = a % 2
                    if oi == 0:
                        P.op(ACT, lambda e, a=a, sa=sa: e.activation(out=nqM[sa][0][0:64, :, :], in_=nqT[0:64, :, a * 128:(a + 1) * 128], func=AF.Copy),
                             reads=[r_nq], writes=[r_nqM[sa][0]])
                        P.op(ACT, lambda e, a=a, sa=sa: e.activation(out=nqM[sa][1][64:128, :, :], in_=nqT[64:128, :, a * 128:(a + 1) * 128], func=AF.Copy),
                             reads=[r_nq], writes=[r_nqM[sa][1]])
                    kt = a + o
                    var = NA_VAR_OF[(a, o)]
                    i = u % 2
                    for bank in range(2):
                        P.op(PE, lambda e, i=i, bank=bank, var=var: e.matmul(pS[i][:, bank * 512:(bank + 1) * 512], lhsT=ident[:], rhs=Bt[:, var, bank * 512:(bank + 1) * 512], start=True, stop=False),
                             reads=[r_Bt, r_c], writes=[r_pS[i]], inc=False)
                        for hh in range(4):
                            h = bank * 4 + hh
                            c = h // 2
                            P.op(PE, lambda e, i=i, h=h, c=c, kt=kt, sa=sa, hh=hh: e.matmul(
                                pS[i][:, h * 128:(h + 1) * 128], lhsT=nkT[:, c, kt * 128:(kt + 1) * 128], rhs=nqM[sa][h % 2][:, c, :],
                                start=False, stop=(hh == 3)),
                                reads=[r_nk, r_nqM[sa][h % 2]], writes=[r_pS[i]], inc=(bank == 1 and hh == 3))

                def na_pv(u):
                    a, oi, o, no = units[u]
                    kt = a + o
                    i = u % 2
                    P.op(ACT, lambda e, i=i: e.activation(out=PT[i][:], in_=pS[i][:], func=AF.Exp), reads=[r_pS[i]], writes=[r_PT[i]])
                    for h in range(8):
                        c0 = (h % 4) * 65
                        P.op(PE, lambda e, i=i, h=h, c0=c0, kt=kt, oi=oi, last=(oi == no - 1): e.matmul(
                            pO[:, h // 4, c0:c0 + 65], lhsT=PT[i][:, h * 128:(h + 1) * 128], rhs=nv[:, kt, h, :],
                            start=(oi == 0 and h % 4 == 0), stop=last),
                            reads=[r_PT[i], r_nv], writes=[r_pO], inc=(h == 7))

                def na_fin_dve(a):
                    sa = a % 2
                    po4 = pO[:, :, 0:260].rearrange("p b (h e) -> p b h e", e=65)
                    P.op(DVE, lambda e, po4=po4: e.reciprocal(out=rden[:].rearrange("p (b h e) -> p b h e", b=2, e=1), in_=po4[:, :, :, 64:65]),
                         reads=[r_pO], writes=[r_rden])
                    P.op(DVE, lambda e, po4=po4, sa=sa: e.tensor_tensor(out=ona2[sa][:].rearrange("p (b h d) -> p b h d", b=2, d=64), in0=po4[:, :, :, 0:64],
                                                                       in1=bc_ap(rden[:], [[4, 2], [1, 4], [0, 64]]), op=ALU.mult),
                         reads=[r_pO, r_rden], writes=[r_ona2[sa]])

                def na_fin_pe(a):
                    sa = a % 2
                    for c in range(4):
                        P.op(PE, lambda e, c=c, sa=sa: e.transpose(out=pTn[:, c * 128:(c + 1) * 128], in_=ona2[sa][:, c * 128:(c + 1) * 128], identity=ident[:]),
                             reads=[r_ona2[sa], r_c], writes=[r_pTn], inc=(c == 3))
                    P.op(ACT, lambda e, a=a: e.activation(out=onaT[:, :, a * 128:(a + 1) * 128], in_=pTn[:, 0:512].rearrange("p (k c) -> p k c", k=4), func=AF.Copy),
                         reads=[r_pTn], writes=[r_onaT])

                na_scores(0)
                pend = None
                for u in range(len(units)):
                    if u + 1 < len(units):
                        na_scores(u + 1)
                    na_pv(u)
                    if pend is not None:
                        na_fin_pe(pend)
                        pend = None
                    a, oi, o, no = units[u]
                    if oi == no - 1:
                        na_fin_dve(a)
                        pend = a
                na_fin_pe(pend)
                P.barrier()

        if _dbg_run():
            _subphase()

        def _subphase():
            with ExitStack() as st:
                sb = lambda name, shape, dt: st.enter_context(nc.sbuf_tensor(uname(tag + name), shape, dt))
                ps = lambda name, shape, dt: st.enter_context(nc.psum_tensor(uname(tag + name), shape, dt))
                wro = sb("wro", [128, 8, D], BF16)
                wno = sb("wno", [128, 4, D], BF16)
                wgr = sb("wgr", [128, 8, D], BF16)
                wgn = sb("wgn", [128, 8, D], BF16)
                wmo = sb("wmo", [128, 8, D], BF16)
                r_w = Res("w")
                gpost = sb("gpost", [128, D], F32)
                mT = sb("mT", [128, 8, 512], BF16)
                r_mT = Res("mT")
                t1 = [sb("t1_%d" % i, [128, 512], F32) for i in range(2)]
                t2 = [sb("t2_%d" % i, [128, 512], F32) for i in range(2)]
                m1 = [sb("m1_0", [128, 512], F32)] * 2
                m2 = [sb("m2_0", [128, 512], F32)] * 2
                r_t1 = [Res("t1") for _ in range(2)]
                r_t2 = [Res("t2") for _ in range(2)]
                r_m1 = [Res("m1")] * 2
                r_m2 = [Res("m2")] * 2
                tmp = [sb("tmp%d" % i, [128, D], F32) for i in range(2)]
                r_tmp = [Res("tmp") for _ in range(2)]
                xr = [sb("xr%d" % i, [128, D], F32) for i in range(2)]
                r_xr = [Res("xr") for _ in range(2)]
                junk = sb("junk", [128, D], BF16)
                r_junk = Res("junk")
                ssq = [sb("ssq%d" % i, [128, 1], F32) for i in range(2)]
                rstd = [sb("rstd%d" % i, [128, 1], F32) for i in range(2)]
                r_ssq = [Res("ssq") for _ in range(2)]
                r_rstd = [Res("rstd") for _ in range(2)]
                pA = [ps("pA%d" % i, [128, 512], F32) for i in range(4)]
                r_pA = [Res("pA") for _ in range(4)]
                pM = [ps("pM%d" % i, [128, D], F32) for i in range(2)]
                r_pM = [Res("pM") for _ in range(2)]

                P.dma(SP, gpost[:], bcast_rows(g_post, D), writes=[r_c])
                P.op(DVE, lambda e: e.tensor_scalar(out=gpost[:], in0=gpost[:], scalar1=0.5, scalar2=None, op0=ALU.mult), reads=[r_c], writes=[r_c])
                r_wro = [Res("wro") for _ in range(8)]
                r_wgr = [Res("wgr") for _ in range(8)]
                r_wno = [Res("wno") for _ in range(8)]
                r_wgn = [Res("wgn") for _ in range(8)]
                r_wmo = [Res("wmo") for _ in range(2)]
                for c in range(8):
                    cs = slice(c * 128, (c + 1) * 128)
                    P.dma(SP, wro[:, :, cs], w_bf["w_ret_out"][:, :, cs], reads=[r_wbf["w_ret_out"]], writes=[r_wro[c]])
                    P.dma(SP, wgr[:, :, cs], wmi[:, :, C_GR + c * 128:C_GR + (c + 1) * 128], reads=[r_wmi], writes=[r_wgr[c]])
                    P.dma(SP, wno[:, :, cs], w_bf["w_na_out"][:, :, cs], reads=[r_wbf["w_na_out"]], writes=[r_wno[c]])
                    P.dma(SP, wgn[:, :, cs], wmi[:, :, C_GN + c * 128:C_GN + (c + 1) * 128], reads=[r_wmi], writes=[r_wgn[c]])
                for half in range(2):
                    hs_ = slice(half * 512, (half + 1) * 512)
                    P.dma(SP, wmo[:, :, hs_], w_bf["w_mix_out"][:, :, hs_], reads=[r_wbf["w_mix_out"]], writes=[r_wmo[half]])
                k2 = 0
                for tb in range(4):
                    tsl = slice(tb * 512, (tb + 1) * 512)
                    for c in range(8):
                        cs = slice(c * 128, (c + 1) * 128)
                        b = k2 % 2
                        k2 += 1
                        for kc in range(8):
                            P.op(PE, lambda e, kc=kc, cs=cs, tsl=tsl: e.matmul(pA[1][:], lhsT=wgr[:, kc, cs], rhs=uT[:, kc, tsl], start=(kc == 0), stop=(kc == 7)),
                                 reads=[r_wgr[c], r_uT], writes=[r_pA[1]], inc=(kc == 7))
                        for kc in range(8):
                            P.op(PE, lambda e, kc=kc, cs=cs, tsl=tsl: e.matmul(pA[3][:], lhsT=wgn[:, kc, cs], rhs=uT[:, kc, tsl], start=(kc == 0), stop=(kc == 7)),
                                 reads=[r_wgn[c], r_uT], writes=[r_pA[3]], inc=(kc == 7))
                        for kc in range(8):
                            P.op(PE, lambda e, kc=kc, cs=cs, tb=tb: e.matmul(pA[0][:], lhsT=wro[:, kc, cs], rhs=VA[:, tb * 4:(tb + 1) * 4, kc * 128:(kc + 1) * 128],
                                                                            start=(kc == 0), stop=(kc == 7)),
                                 reads=[r_wro[c]] + r_VA[tb * 4:(tb + 1) * 4], writes=[r_pA[0]], inc=(kc == 7))
                        for kc in range(4):
                            P.op(PE, lambda e, kc=kc, cs=cs, tsl=tsl: e.matmul(pA[2][:], lhsT=wno[:, kc, cs], rhs=onaT[:, kc, tsl], start=(kc == 0), stop=(kc == 3)),
                                 reads=[r_wno[c], r_onaT], writes=[r_pA[2]], inc=(kc == 3))
                        P.op(ACT, lambda e, b=b: e.activation(out=t1[b][:], in_=pA[1][:], func=AF.Tanh, scale=0.5), reads=[r_pA[1]], writes=[r_t1[b]])
                        P.op(ACT, lambda e, b=b: e.activation(out=t2[b][:], in_=pA[3][:], func=AF.Tanh, scale=0.5), reads=[r_pA[3]], writes=[r_t2[b]])
                        P.op(DVE, lambda e, b=b: e.scalar_tensor_tensor(out=m1[b][:], in0=t1[b][:], scalar=1.0, in1=pA[0][:], op0=ALU.add, op1=ALU.mult),
                             reads=[r_t1[b], r_pA[0]], writes=[r_m1[b]])
                        P.op(DVE, lambda e, b=b: e.scalar_tensor_tensor(out=m2[b][:], in0=t2[b][:], scalar=1.0, in1=pA[2][:], op0=ALU.add, op1=ALU.mult),
                             reads=[r_t2[b], r_pA[2]], writes=[r_m2[b]])
                        P.op(POOL, lambda e, b=b, c=c: e.tensor_tensor(out=mT[:, c, :], in0=m1[b][:], in1=m2[b][:], op=ALU.add),
                             reads=[r_m1[b], r_m2[b]], writes=[r_mT])
                    for tt in range(4):
                        t = tb * 4 + tt
                        ip = t % 2
                        if t == 0:
                            P.dma(SP, xr[ip][:], x_src[t * 128:(t + 1) * 128, :], writes=[r_xr[ip]])
                        for half in range(2):
                            for kc in range(8):
                                P.op(PE, lambda e, ip=ip, half=half, kc=kc, tt=tt: e.matmul(pM[ip][:, half * 512:(half + 1) * 512], lhsT=mT[:, kc, tt * 128:(tt + 1) * 128],
                                                                                            rhs=wmo[:, kc, half * 512:(half + 1) * 512], start=(kc == 0), stop=(kc == 7)),
                                     reads=[r_mT, r_wmo[half]], writes=[r_pM[ip]], inc=(half == 1 and kc == 7))
                        if t + 1 < NT:
                            P.dma(SP, xr[1 - ip][:], x_src[(t + 1) * 128:(t + 2) * 128, :], writes=[r_xr[1 - ip]])
                        P.op(ACT, lambda e, ip=ip: e.activation(out=junk[:], in_=pM[ip][:], func=AF.Square, accum_out=ssq[ip][:]),
                             reads=[r_pM[ip]], writes=[r_junk, r_ssq[ip]])
                        norm_rstd(P, ssq[ip][:], rstd[ip][:], r_ssq[ip], r_rstd[ip], neg_half, r_c, 1, 0.25 / D)
                        P.op(DVE, lambda e, ip=ip: e.scalar_tensor_tensor(out=tmp[ip][:], in0=pM[ip][:], scalar=rstd[ip][:], in1=gpost[:], op0=ALU.mult, op1=ALU.mult),
                             reads=[r_pM[ip], r_rstd[ip], r_c], writes=[r_tmp[ip]])
                        P.op(POOL, lambda e, ip=ip: e.tensor_tensor(out=xr[ip][:], in0=xr[ip][:], in1=tmp[ip][:], op=ALU.add),
                             reads=[r_tmp[ip], r_xr[ip]], writes=[r_xr[ip]])
                        P.dma(SP, x_dst[t * 128:(t + 1) * 128, :], xr[ip][:], reads=[r_xr[ip]])
                P.barrier()

        if _dbg_run():
            _subphase()
```

```python
import math
import numpy as np
import concourse.bass as bass
import concourse.mybir as mybir
from concourse.bass_utils import run_bass_kernel_spmd

F32 = mybir.dt.float32
BF16 = mybir.dt.bfloat16
AF = mybir.ActivationFunctionType
ALU = mybir.AluOpType

D = 1024
S = 2048
NT = S // 128
DFF = 2816
NJ = DFF // 128
EPS = 1e-6
MIXW = 6656
N_CORES = 8
SEQ_PER_CORE = 4

PE, ACT, DVE, POOL, SP = "pe", "act", "dve", "pool", "sp"
ENGS = (PE, ACT, DVE, POOL, SP)


class Res:
    __slots__ = ("name", "w", "r")

    def __init__(self, name):
        self.name = name
        self.w = None
        self.r = {}


class Prog:
    def __init__(self, nc, n_dma_sems=48):
        self.nc = nc
        self.eng = {PE: nc.tensor, ACT: nc.scalar, DVE: nc.vector, POOL: nc.gpsimd, SP: nc.sync}
        self.streams = {e: [] for e in ENGS}
        self.cnt = {e: 0 for e in ENGS}
        self.waited = {e: {} for e in ENGS}
        self.sems = {}
        self.n_dma_sems = n_dma_sems
        self.dma_tot = [0] * n_dma_sems
        self.dma_rr = 0
        self.dma_rr_q = {}
        self.n_ops = 0

    def alloc_sems(self, stack):
        for e in (PE, ACT, DVE, POOL):
            self.sems[e] = stack.enter_context(self.nc.semaphore("s_" + e))
        for i in range(self.n_dma_sems):
            self.sems[("d", i)] = stack.enter_context(self.nc.semaphore("s_d%d" % i))

    def _need(self, e, key, val, waits):
        if key == e and e == PE:
            return
        if self.waited[e].get(key, 0) >= val:
            return
        self.waited[e][key] = val
        waits.append((key, val))

    def _deps(self, e, reads, writes):
        waits = []
        for r in reads:
            if r.w is not None:
                self._need(e, r.w[0], r.w[1], waits)
        for w in writes:
            if w.w is not None:
                self._need(e, w.w[0], w.w[1], waits)
            for k, v in w.r.items():
                self._need(e, k, v, waits)
        return waits

    def _mark(self, key, val, reads, writes):
        for r in reads:
            if r.r.get(key, 0) < val:
                r.r[key] = val
        for w in writes:
            w.w = (key, val)
            w.r = {}

    def op(self, e, fn, reads=(), writes=(), inc=True):
        waits = self._deps(e, reads, writes)
        val = self.cnt[e] + 1
        if inc:
            self.cnt[e] = val
        self.streams[e].append((waits, fn, (e, 1) if inc else None))
        self._mark(e, val, reads, writes)
        self.n_ops += 1

    def dma(self, q, out_ap, in_ap, reads=(), writes=()):
        if q == SP:
            lo, hi = 0, self.n_dma_sems - 24
        else:
            lo, hi = self.n_dma_sems - 24, self.n_dma_sems
        rr = self.dma_rr_q.get(q, lo)
        i = rr
        self.dma_rr_q[q] = lo + (rr + 1 - lo) % (hi - lo)
        key = ("d", i)
        waits = self._deps(q, reads, writes)
        if self.dma_tot[i] > 0:
            self._need(q, key, self.dma_tot[i], waits)
        self.dma_tot[i] += 16
        val = self.dma_tot[i]
        self.streams[q].append((waits, lambda eng: eng.dma_start(out=out_ap, in_=in_ap), (key, 16)))
        self._mark(key, val, reads, writes)
        self.n_ops += 1

    def barrier(self):
        for e in ENGS:
            waits = []
            for k in (PE, ACT, DVE, POOL):
                if self.cnt[k] > 0:
                    self._need(e, k, self.cnt[k], waits)
            for i in range(self.n_dma_sems):
                if self.dma_tot[i] > 0:
                    self._need(e, ("d", i), self.dma_tot[i], waits)
            if waits:
                self.streams[e].append((waits, None, None))

    def check_deadlock(self):
        val = {}
        pos = {e: 0 for e in ENGS}
        progress = True
        while progress:
            progress = False
            for e in ENGS:
                st = self.streams[e]
                while pos[e] < len(st):
                    waits, fn, inc = st[pos[e]]
                    if any(val.get(k, 0) < v for k, v in waits):
                        break
                    if inc is not None:
                        val[inc[0]] = val.get(inc[0], 0) + inc[1]
                    pos[e] += 1
                    progress = True
        stuck = {e: (pos[e], len(self.streams[e]), [(k, v, val.get(k, 0)) for k, v in self.streams[e][pos[e]][0] if val.get(k, 0) < v])
                 for e in ENGS if pos[e] < len(self.streams[e])}
        if stuck:
            raise RuntimeError("deadlock in emitted program: %r" % (stuck,))

    def emit(self, block):
        prog = self
        self.check_deadlock()

        def run(e):
            def body(eng):
                for waits, fn, inc in prog.streams[e]:
                    for key, val in waits:
                        eng.wait_ge(prog.sems[key], val)
                    if fn is not None:
                        ins = fn(eng)
                        if inc is not None:
                            ins.then_inc(prog.sems[inc[0]], inc[1])
            return body

        block.tensor(run(PE))
        block.scalar(run(ACT))
        block.vector(run(DVE))
        block.gpsimd(run(POOL))
        block.sync(run(SP))


WEIGHTS = {
    "ffn1_w_in": (D, 2 * DFF), "ffn1_w_out": (DFF, D),
    "w_mix_in": (D, MIXW), "w_ret_out": (D, D), "w_na_out": (512, D), "w_mix_out": (D, D),
    "ffn2_w_in": (D, 2 * DFF), "ffn2_w_out": (DFF, D),
}
GAINS = ["ffn1_pre_norm", "ffn1_post_norm", "mix_pre_norm", "mix_post_norm", "ffn2_pre_norm", "ffn2_post_norm"]


_UNAME = [0]


def uname(base):
    _UNAME[0] += 1
    return "%s_%d" % (base, _UNAME[0])


def bcast_rows(ap_1xn, n):
    return bass.AP(ap_1xn.tensor, ap_1xn.offset, [[0, 128], [1, n]])


def build(n_seq=SEQ_PER_CORE, stop_after="C", t_ffn=1024):
    from contextlib import ExitStack
    nc = bass.Bass("TRN2", target_bir_lowering=False)
    ntok = n_seq * S
    x_in = nc.dram_tensor("x", [ntok, D], F32, kind="ExternalInput").ap()
    y_out = nc.dram_tensor("y", [ntok, D], F32, kind="ExternalOutput").ap()
    w_in = {k: nc.dram_tensor(k, list(v), F32, kind="ExternalInput").ap() for k, v in WEIGHTS.items()}
    g_in = {k: nc.dram_tensor(k, [1, D], F32, kind="ExternalInput").ap() for k in GAINS}
    ident_in = nc.dram_tensor("ident", [128, 128], F32, kind="ExternalInput").ap()
    cst_in = nc.dram_tensor("cst", [128, NCST], F32, kind="ExternalInput").ap()
    rope_in = nc.dram_tensor("rope", [128, 2 * NT * 64], F32, kind="ExternalInput").ap()
    nab_in = nc.dram_tensor("nab", [NVAR * 128, 1024], F32, kind="ExternalInput").ap()
    dec_in = nc.dram_tensor("dec", [1, 8], F32, kind="ExternalInput").ap()
    sb_d = nc.dram_tensor("sb_scr", [NT * 128, 1024], BF16, kind="Internal").ap()
    nab_bf = nc.dram_tensor("nab_bf", [128, NVAR, 1024], BF16, kind="Internal").ap()
    w_bf = {k: nc.dram_tensor(k + "_bf", [128, v[0] // 128, v[1]], BF16, kind="Internal").ap()
            for k, v in WEIGHTS.items()}
    x1_d = nc.dram_tensor("x1_scr", [ntok, D], F32, kind="Internal").ap()
    x2_d = nc.dram_tensor("x2_scr", [ntok, D], F32, kind="Internal").ap()

    P = Prog(nc)
    with ExitStack() as top:
        P.alloc_sems(top)
        r_wbf = {k: Res("wbf_" + k) for k in WEIGHTS}
        r_nabbf = Res("nabbf")

        def cast_jobs(name, src3, dst3, r_dst, nkc, N, nk_step):
            jobs = []
            for k0 in range(0, nkc, nk_step):
                k1 = min(nkc, k0 + nk_step)
                for c0 in range(0, N, 2048):
                    c1 = min(N, c0 + 2048)
                    jobs.append((dst3[:, k0:k1, c0:c1], src3[:, k0:k1, c0:c1], r_dst))
            return jobs

        def wsrc(name):
            return w_in[name].rearrange("(kc p) n -> p kc n", p=128)

        front = cast_jobs("ffn1_w_in", wsrc("ffn1_w_in"), w_bf["ffn1_w_in"], r_wbf["ffn1_w_in"], 8, 2 * DFF, 1)
        front += cast_jobs("ffn1_w_out", wsrc("ffn1_w_out"), w_bf["ffn1_w_out"], r_wbf["ffn1_w_out"], NJ, D, 2)
        bg_jobs = cast_jobs("w_mix_in", wsrc("w_mix_in"), w_bf["w_mix_in"], r_wbf["w_mix_in"], 8, MIXW, 1)
        bg_jobs += cast_jobs("w_ret_out", wsrc("w_ret_out"), w_bf["w_ret_out"], r_wbf["w_ret_out"], 8, D, 2)
        bg_jobs += cast_jobs("w_na_out", wsrc("w_na_out"), w_bf["w_na_out"], r_wbf["w_na_out"], 4, D, 2)
        bg_jobs += cast_jobs("w_mix_out", wsrc("w_mix_out"), w_bf["w_mix_out"], r_wbf["w_mix_out"], 8, D, 2)
        bg_jobs += cast_jobs("nab", nab_in.rearrange("(v p) n -> p v n", p=128), nab_bf, r_nabbf, NVAR, 1024, 2)
        bg_jobs += cast_jobs("ffn2_w_in", wsrc("ffn2_w_in"), w_bf["ffn2_w_in"], r_wbf["ffn2_w_in"], 8, 2 * DFF, 1)
        bg_jobs += cast_jobs("ffn2_w_out", wsrc("ffn2_w_out"), w_bf["ffn2_w_out"], r_wbf["ffn2_w_out"], NJ, D, 2)
        for dst, src, r_dst in front:
            P.dma(POOL, dst, src, writes=[r_dst])

        MT = top.enter_context(nc.sbuf_tensor("g_MT", [128, 512], F32))
        DFt = top.enter_context(nc.sbuf_tensor("g_DFt", [128, 512], F32))
        DBt = top.enter_context(nc.sbuf_tensor("g_DBt", [128, 512], F32))
        dtok = top.enter_context(nc.sbuf_tensor("g_dtok", [128, 8], F32))
        gC = top.enter_context(nc.sbuf_tensor("g_gC", [128, 8], F32))
        tabs = (MT, DFt, DBt, dtok, gC)
        with ExitStack() as st:
            cst = st.enter_context(nc.sbuf_tensor("g_cst", [128, NCST], F32))
            lgt = st.enter_context(nc.sbuf_tensor("g_lgt", [128, 8], F32))
            tA = st.enter_context(nc.sbuf_tensor("g_tA", [128, 512], F32))
            tB = st.enter_context(nc.sbuf_tensor("g_tB", [128, 512], F32))
            r_c = Res("gconst")
            P.dma(SP, cst[:], cst_in[:, :], writes=[r_c])
            P.dma(SP, lgt[:], bcast_rows(dec_in, 8), writes=[r_c])
            P.op(ACT, lambda e: e.activation(out=lgt[:], in_=lgt[:], func=AF.Exp, scale=-1.0), reads=[r_c], writes=[r_c])
            P.op(ACT, lambda e: e.activation(out=lgt[:], in_=lgt[:], func=AF.Ln, bias=1.0), reads=[r_c], writes=[r_c])
            P.op(DVE, lambda e: e.tensor_scalar(out=lgt[:], in0=lgt[:], scalar1=-1.0, scalar2=None, op0=ALU.mult), reads=[r_c], writes=[r_c])
            for h in range(4):
                hs = slice(h * 128, (h + 1) * 128)
                P.op(ACT, lambda e, h=h: e.activation(out=tA[:, 0:128], in_=cst[:, 128:256], func=AF.Exp, scale=lgt[:, h:h + 1]), reads=[r_c], writes=[r_c])
                P.op(DVE, lambda e, hs=hs: e.tensor_tensor(out=MT[:, hs], in0=tA[:, 0:128], in1=cst[:, 384:512], op=ALU.mult), reads=[r_c], writes=[r_c])
                P.op(ACT, lambda e, h=h: e.activation(out=tB[:, 0:128], in_=cst[:, 256:384], func=AF.Exp, scale=lgt[:, 4 + h:5 + h]), reads=[r_c], writes=[r_c])
                P.op(DVE, lambda e: e.tensor_tensor(out=tB[:, 0:128], in0=tB[:, 0:128], in1=cst[:, 512:640], op=ALU.mult), reads=[r_c], writes=[r_c])
                P.op(DVE, lambda e, hs=hs: e.tensor_tensor(out=MT[:, hs], in0=MT[:, hs], in1=tB[:, 0:128], op=ALU.add), reads=[r_c], writes=[r_c])
                P.op(ACT, lambda e, h=h, hs=hs: e.activation(out=DFt[:, hs], in_=cst[:, 640:768], func=AF.Exp, scale=lgt[:, h:h + 1]), reads=[r_c], writes=[r_c])
                P.op(ACT, lambda e, h=h, hs=hs: e.activation(out=DBt[:, hs], in_=cst[:, 768:896], func=AF.Exp, scale=lgt[:, 4 + h:5 + h]), reads=[r_c], writes=[r_c])
            P.op(DVE, lambda e: e.tensor_scalar(out=DFt[:], in0=DFt[:], scalar1=RET_SCALE, scalar2=None, op0=ALU.mult), reads=[r_c], writes=[r_c])
            P.op(DVE, lambda e: e.tensor_scalar(out=DBt[:], in0=DBt[:], scalar1=RET_SCALE, scalar2=None, op0=ALU.mult), reads=[r_c], writes=[r_c])
            P.op(ACT, lambda e: e.activation(out=dtok[:, 0:4], in_=lgt[:, 0:4], func=AF.Exp, scale=cst[:, 896:897]), reads=[r_c], writes=[r_c])
            P.op(ACT, lambda e: e.activation(out=dtok[:, 4:8], in_=lgt[:, 4:8], func=AF.Exp, scale=cst[:, 897:898]), reads=[r_c], writes=[r_c])
            P.op(ACT, lambda e: e.activation(out=gC[:], in_=lgt[:], func=AF.Exp, scale=128.0), reads=[r_c], writes=[r_c])
            P.barrier()

        ffn_phase(nc, P, x_in, x1_d if stop_after != "A" else y_out,
                  w_bf["ffn1_w_in"], w_bf["ffn1_w_out"], r_wbf["ffn1_w_in"], r_wbf["ffn1_w_out"],
                  g_in["ffn1_pre_norm"], g_in["ffn1_post_norm"], ident_in, t_ffn, "f1", ntok, bg_jobs)
        for dst, src, r_dst in bg_jobs:
            P.dma(POOL, dst, src, writes=[r_dst])
        P.barrier()
        if stop_after != "A":
            for sq in range(n_seq):
                tok0 = sq * S
                mix_phase(nc, P, x1_d[tok0:tok0 + S, :], x2_d[tok0:tok0 + S, :] if stop_after != "B" else y_out[tok0:tok0 + S, :],
                          w_bf, r_wbf, g_in["mix_pre_norm"], g_in["mix_post_norm"], cst_in, rope_in, (nab_bf, r_nabbf), tabs, sb_d, "mxs%d" % sq)
                P.barrier()
            if stop_after != "B":
                ffn_phase(nc, P, x2_d, y_out,
                          w_bf["ffn2_w_in"], w_bf["ffn2_w_out"], r_wbf["ffn2_w_in"], r_wbf["ffn2_w_out"],
                          g_in["ffn2_pre_norm"], g_in["ffn2_post_norm"], ident_in, t_ffn, "f2", ntok)
                P.barrier()

        P.barrier()
        with nc.Block() as block:
            P.emit(block)
    return nc


def norm_rstd(P, pool_ssq, pool_rstd, r_ssq, r_rstd, neg_half, r_consts, n, inv_n):
    P.op(POOL, lambda e: e.tensor_scalar(out=pool_rstd, in0=pool_ssq, scalar1=inv_n, scalar2=EPS, op0=ALU.mult, op1=ALU.add),
         reads=[r_ssq], writes=[r_rstd])
    P.op(POOL, lambda e: e.tensor_tensor(out=pool_rstd, in0=pool_rstd, in1=neg_half[:, 0:n], op=ALU.pow),
         reads=[r_rstd, r_consts], writes=[r_rstd])


def ffn_phase(nc, P, x_src, x_dst, win_bf, wout_bf, r_win, r_wout, g_pre, g_post, ident_in, T, tag, ntok=S, bg_jobs=None):
    from contextlib import ExitStack
    NG = ntok // T
    TT = T // 128
    TB = T // 512
    NWB = 3
    with ExitStack() as st:
        sb = lambda name, shape, dt: st.enter_context(nc.sbuf_tensor(uname(tag + name), shape, dt))
        ps = lambda name, shape, dt: st.enter_context(nc.psum_tensor(uname(tag + name), shape, dt))
        NXN, NXR = 3, 2
        xn = [sb("xn%d" % i, [128, D], F32) for i in range(NXN)]
        r_xn = [Res("xn") for _ in range(NXN)]
        xr = [sb("xr%d" % i, [128, D], F32) for i in range(NXR)]
        r_xr = [Res("xr") for _ in range(NXR)]
        uT = [sb("uT%d" % i, [128, 8, T], BF16) for i in range(2)]
        r_uT = [[Res("uT") for _ in range(TB)] for _ in range(2)]
        hT = sb("hT", [128, NJ, T], BF16)
        r_hT = Res("hT")
        wout = sb("wout", [128, NJ, D], BF16)
        r_wo = Res("wout")
        wg = [sb("wg%d" % i, [128, 8, 256], BF16) for i in range(NWB)]
        wu = [sb("wu%d" % i, [128, 8, 256], BF16) for i in range(NWB)]
        r_w = [Res("w") for _ in range(NWB)]
        gpre = sb("gpre", [128, D], F32)
        gpost = sb("gpost", [128, D], F32)
        ident_f = sb("identf", [128, 128], F32)
        ident = sb("ident", [128, 128], BF16)
        neg_half = sb("nh", [128, 8], F32)
        r_c = Res("consts")
        ub = [sb("ub%d" % i, [128, D], BF16) for i in range(2)]
        r_ub = [Res("ub") for _ in range(2)]
        junk = sb("junk", [128, D], BF16)
        r_junk = Res("junk")
        ssq = [sb("ssq%d" % i, [128, 1], F32) for i in range(4)]
        rstd = [sb("rstd%d" % i, [128, 1], F32) for i in range(4)]
        r_ssq = [Res("ssq") for _ in range(4)]
        r_rstd = [Res("rstd") for _ in range(4)]
        sg = [sb("sg%d" % i, [128, 512], F32) for i in range(2)]
        r_sg = [Res("sg") for _ in range(2)]
        tmp = [sb("tmp%d" % i, [128, D], F32) for i in range(2)]
        r_tmp = [Res("tmp") for _ in range(2)]
        p2 = [ps("p2_%d" % i, [128, 512], F32) for i in range(3)]
        r_p2 = [Res("p2") for _ in range(3)]
        p3 = [ps("p3_%d" % i, [128, D], F32) for i in range(2)]
        r_p3 = [Res("p3") for _ in range(2)]
        pT = ps("pT", [128, D], BF16)
        r_pT = Res("pT")

        P.dma(SP, gpre[:], bcast_rows(g_pre, D), writes=[r_c])
        P.dma(SP, gpost[:], bcast_rows(g_post, D), writes=[r_c])
        P.dma(SP, ident_f[:], ident_in[:, :], writes=[r_c])
        P.op(DVE, lambda e: e.tensor_copy(out=ident[:], in_=ident_f[:]), reads=[r_c], writes=[r_c])
        P.op(DVE, lambda e: e.tensor_scalar(out=gpost[:], in0=gpost[:], scalar1=0.5, scalar2=None, op0=ALU.mult), reads=[r_c], writes=[r_c])
        P.op(DVE, lambda e: e.memset(neg_half[:], -0.5), writes=[r_c])

        cnt = {"n": 0, "p2": 0, "p3": 0, "w": 0, "sg": 0, "tmp": 0, "xn": 0, "xr": 0}

        nrec = {}

        def stage_x(g, t):
            k = cnt["n"] % 4
            kb = cnt["n"] % 2
            cnt["n"] += 1
            ix = cnt["xn"] % NXN
            cnt["xn"] += 1
            nrec[(g, t)] = kb
            r0 = g * T + t * 128
            P.dma(SP, xn[ix][:], x_src[r0:r0 + 128, :], writes=[r_xn[ix]])
            xin = xn[ix][:]
            P.op(ACT, lambda e, xin=xin, k=k: e.activation(out=junk[:], in_=xin, func=AF.Square, accum_out=ssq[k][:]),
                 reads=[r_xn[ix]], writes=[r_junk, r_ssq[k]])
            norm_rstd(P, ssq[k][:], rstd[k][:], r_ssq[k], r_rstd[k], neg_half, r_c, 1, 1.0 / D)
            P.op(DVE, lambda e, xin=xin, k=k, kb=kb: e.scalar_tensor_tensor(out=ub[kb][:], in0=xin, scalar=rstd[k][:], in1=gpre[:], op0=ALU.mult, op1=ALU.mult),
                 reads=[r_xn[ix], r_rstd[k], r_c], writes=[r_ub[kb]])

        def stage_y(g, t):
            b = g % 2
            kb = nrec[(g, t)]
            for kc in range(8):
                P.op(PE, lambda e, kb=kb, kc=kc: e.transpose(out=pT[:, kc * 128:(kc + 1) * 128], in_=ub[kb][:, kc * 128:(kc + 1) * 128], identity=ident[:]),
                     reads=[r_ub[kb], r_c], writes=[r_pT], inc=(kc == 7))
            P.op(ACT, lambda e, b=b, t=t: e.activation(out=uT[b][:, :, t * 128:(t + 1) * 128], in_=pT[:].rearrange("p (k c) -> p k c", k=8), func=AF.Copy),
                 reads=[r_pT], writes=[r_uT[b][t // 4]])

        def norm_group(g):
            stage_x(g, 0)
            for t in range(TT):
                if t + 1 < TT:
                    stage_x(g, t + 1)
                stage_y(g, t)

        def load_w(jb):
            s = cnt["w"] % NWB
            cnt["w"] += 1
            c0 = jb * 256
            P.dma(SP, wg[s][:], win_bf[:, :, c0:c0 + 256], reads=[r_win], writes=[r_w[s]])
            P.dma(SP, wu[s][:], win_bf[:, :, DFF + c0:DFF + c0 + 256], reads=[r_win], writes=[r_w[s]])
            return s

        NJB = NJ // 2
        norm_group(0)
        slots = [load_w(0), load_w(1)]
        for g in range(NG):
            b = g % 2
            for jb in range(NJB):
                s = slots.pop(0)
                nxt = jb + 2
                if nxt < NJB:
                    slots.append(load_w(nxt))
                elif g + 1 < NG:
                    slots.append(load_w(nxt - NJB))
                if jb == 0:
                    for j0 in range(0, NJ, 6):
                        j1 = min(NJ, j0 + 6)
                        P.dma(SP, wout[:, j0:j1, :], wout_bf[:, j0:j1, :], reads=[r_wout], writes=[r_wo])
                for jj in range(2):
                    j = jb * 2 + jj
                    for tb in range(TB):
                        ig = cnt["p2"] % 3
                        iu = (cnt["p2"] + 1) % 3
                        cnt["p2"] += 2
                        for kc in range(8):
                            P.op(PE, lambda e, ig=ig, s=s, jj=jj, kc=kc, b=b, tb=tb: e.matmul(
                                p2[ig][:], lhsT=wg[s][:, kc, jj * 128:(jj + 1) * 128], rhs=uT[b][:, kc, tb * 512:(tb + 1) * 512],
                                start=(kc == 0), stop=(kc == 7)),
                                reads=[r_w[s], r_uT[b][tb]], writes=[r_p2[ig]], inc=(kc == 7))
                        for kc in range(8):
                            P.op(PE, lambda e, iu=iu, s=s, jj=jj, kc=kc, b=b, tb=tb: e.matmul(
                                p2[iu][:], lhsT=wu[s][:, kc, jj * 128:(jj + 1) * 128], rhs=uT[b][:, kc, tb * 512:(tb + 1) * 512],
                                start=(kc == 0), stop=(kc == 7)),
                                reads=[r_w[s], r_uT[b][tb]], writes=[r_p2[iu]], inc=(kc == 7))
                        k = cnt["sg"] % 2
                        cnt["sg"] += 1
                        P.op(ACT, lambda e, k=k, ig=ig: e.activation(out=sg[k][:], in_=p2[ig][:], func=AF.Silu),
                             reads=[r_p2[ig]], writes=[r_sg[k]])
                        P.op(DVE, lambda e, k=k, iu=iu, j=j, tb=tb: e.tensor_tensor(out=hT[:, j, tb * 512:(tb + 1) * 512], in0=p2[iu][:], in1=sg[k][:], op=ALU.mult),
                             reads=[r_p2[iu], r_sg[k]], writes=[r_hT])
                if bg_jobs:
                    for _ in range(min(1, len(bg_jobs))):
                        dst_, src_, r_dst_ = bg_jobs.pop(0)
                        P.dma(POOL, dst_, src_, writes=[r_dst_])
                if g + 1 < NG:
                    if 1 <= jb <= TT:
                        stage_x(g + 1, jb - 1)
                    if 2 <= jb <= TT + 1:
                        stage_y(g + 1, jb - 2)
            for t in range(TT):
                ip = cnt["p3"] % 2
                cnt["p3"] += 1
                ir = cnt["xr"] % NXR
                cnt["xr"] += 1
                r0 = g * T + t * 128
                if g == 0 and t == 0:
                    P.dma(SP, xr[ir][:], x_src[r0:r0 + 128, :], writes=[r_xr[ir]])
                for half in range(2):
                    for j in range(NJ):
                        P.op(PE, lambda e, ip=ip, half=half, j=j, t=t: e.matmul(
                            p3[ip][:, half * 512:(half + 1) * 512], lhsT=hT[:, j, t * 128:(t + 1) * 128], rhs=wout[:, j, half * 512:(half + 1) * 512],
                            start=(j == 0), stop=(j == NJ - 1)),
                            reads=[r_hT, r_wo], writes=[r_p3[ip]], inc=(half == 1 and j == NJ - 1))
                nt_ = g * TT + t + 1
                if nt_ < NG * TT:
                    irn = cnt["xr"] % NXR
                    P.dma(SP, xr[irn][:], x_src[nt_ * 128:(nt_ + 1) * 128, :], writes=[r_xr[irn]])
                k = cnt["n"] % 4
                cnt["n"] += 1
                kt = cnt["tmp"] % 2
                cnt["tmp"] += 1
                P.op(ACT, lambda e, ip=ip, k=k: e.activation(out=junk[:], in_=p3[ip][:], func=AF.Square, accum_out=ssq[k][:]),
                     reads=[r_p3[ip]], writes=[r_junk, r_ssq[k]])
                norm_rstd(P, ssq[k][:], rstd[k][:], r_ssq[k], r_rstd[k], neg_half, r_c, 1, 1.0 / D)
                P.op(DVE, lambda e, ip=ip, k=k, kt=kt: e.scalar_tensor_tensor(out=tmp[kt][:], in0=p3[ip][:], scalar=rstd[k][:], in1=gpost[:], op0=ALU.mult, op1=ALU.mult),
                     reads=[r_p3[ip], r_rstd[k], r_c], writes=[r_tmp[kt]])
                P.op(POOL, lambda e, ir=ir, kt=kt: e.tensor_tensor(out=xr[ir][:], in0=xr[ir][:], in1=tmp[kt][:], op=ALU.add),
                     reads=[r_tmp[kt], r_xr[ir]], writes=[r_xr[ir]])
                P.dma(SP, x_dst[r0:r0 + 128, :], xr[ir][:], reads=[r_xr[ir]])


def common_inputs(inputs):
    common = {k: np.ascontiguousarray(inputs[k][0], dtype=np.float32) for k in WEIGHTS}
    for k in GAINS:
        common[k] = np.ascontiguousarray(inputs[k], dtype=np.float32).reshape(1, D)
    common["ident"] = np.eye(128, dtype=np.float32)
    common["cst"] = make_cst()
    common["rope"] = make_rope()
    common["nab"] = make_nab(np.asarray(inputs["na_rel_bias"])[0])
    common["dec"] = np.concatenate([np.asarray(inputs["ret_decay_fwd"], np.float32).reshape(4),
                                    np.asarray(inputs["ret_decay_bwd"], np.float32).reshape(4)]).reshape(1, 8)
    return common


def kernel(**inputs):
    x = np.ascontiguousarray(inputs["x"], dtype=np.float32)
    B = x.shape[0]
    assert B == N_CORES * SEQ_PER_CORE
    nc = build()
    common = common_inputs(inputs)
    in_maps = []
    for c in range(N_CORES):
        m = dict(common)
        m["x"] = x[c * SEQ_PER_CORE:(c + 1) * SEQ_PER_CORE].reshape(SEQ_PER_CORE * S, D)
        in_maps.append(m)
    res = run_bass_kernel_spmd(nc, in_maps, core_ids=list(range(N_CORES)))
    out = np.stack([r["y"].reshape(SEQ_PER_CORE, S, D) for r in res.results], axis=0)
    return out.reshape(B, S, D).astype(np.float32)


RET_SCALE = 128.0 ** -0.5
NCST = 898


def make_cst():
    c = np.zeros((128, NCST), np.float32)
    c[:, 0:128] = np.eye(128, dtype=np.float32)
    k = np.arange(128)[:, None]
    q = np.arange(128)[None, :]
    c[:, 128:256] = np.maximum(q - k, 0)
    c[:, 256:384] = np.maximum(k - q, 0)
    c[:, 384:512] = np.where(q >= k, RET_SCALE, 0.0)
    c[:, 512:640] = np.where(k > q, RET_SCALE, 0.0)
    c[:, 640:768] = np.broadcast_to(q + 1, (128, 128))
    c[:, 768:896] = np.broadcast_to(128 - q, (128, 128))
    c[:, 896] = 127 - np.arange(128)
    c[:, 897] = np.arange(128)
    return c


def make_rope():
    inv = (np.float32(1.0) / (np.float32(10000.0) ** np.linspace(0.0, 1.0, 64, dtype=np.float32))).astype(np.float32)
    pos = np.arange(S, dtype=np.float32)
    ang = (pos[:, None] * inv[None, :]).astype(np.float32)
    cos = np.cos(ang).astype(np.float32).reshape(NT, 128, 64).transpose(1, 0, 2)
    sin = np.sin(ang).astype(np.float32).reshape(NT, 128, 64).transpose(1, 0, 2)
    return np.ascontiguousarray(np.concatenate([cos.reshape(128, NT * 64), sin.reshape(128, NT * 64)], axis=1))


def na_offsets(a):
    rows = []
    for qr in (0, 1):
        rq = 2 * a + qr
        rs = min(max(rq - 4, 0), 24)
        rows += [rs, rs + 7]
    return list(range(min(rows) // 2 - a, max(rows) // 2 - a + 1))


def _na_variants():
    var_of, idxs, seen = {}, [], {}
    ck = np.arange(64)[:, None]
    cq = np.arange(64)[None, :]
    ws = np.clip(cq - 8, 0, 48)
    cvalid = (ck >= ws) & (ck < ws + 16)
    ci = np.clip(ck - cq + 15, 0, 30)
    for a in range(16):
        for o in na_offsets(a):
            idx = np.full((128, 128), -1, np.int64)
            for kr in (0, 1):
                for qr in (0, 1):
                    rk = 2 * (a + o) + kr
                    rq = 2 * a + qr
                    rs = min(max(rq - 4, 0), 24)
                    if not (rs <= rk < rs + 8):
                        continue
                    ri = rk - rq + 7
                    idx[kr * 64:(kr + 1) * 64, qr * 64:(qr + 1) * 64] = np.where(cvalid, ri * 31 + ci, -1)
            key = idx.tobytes()
            if key not in seen:
                seen[key] = len(idxs)
                idxs.append(idx)
            var_of[(a, o)] = seen[key]
    return var_of, idxs


NA_VAR_OF, NA_IDX = _na_variants()
NVAR = len(NA_IDX)
NA_NEG = -30000.0


def make_nab(rel_bias):
    rb = np.asarray(rel_bias, np.float32).reshape(8, 15 * 31)
    out = np.empty((NVAR, 128, 8, 128), np.float32)
    for v, idx in enumerate(NA_IDX):
        safe = np.where(idx >= 0, idx, 0)
        for h in range(8):
            out[v, :, h, :] = np.where(idx >= 0, rb[h][safe], np.float32(NA_NEG))
    return np.ascontiguousarray(out.reshape(NVAR * 128, 1024))


def bc_ap(base, dims):
    return bass.AP(base.tensor, base.offset, [list(base.ap[0])] + [list(d) for d in dims])


C_RQ, C_RK, C_RV, C_RG, C_NQ, C_NK, C_NV, C_GR, C_GN = 0, 512, 1024, 2048, 3072, 3584, 4096, 4608, 5632


_DBG = {'n': 0, 'max': 99}


def _dbg_run():
    _DBG['n'] += 1
    return _DBG['n'] <= _DBG['max']


def mix_phase(nc, P, x_src, x_dst, w_bf, r_wbf, g_pre, g_post, cst_in, rope_in, nab_in, tabs, sb_d, tag):
    MT, DFt, DBt, dtok, gC = tabs
    from contextlib import ExitStack
    wmi, r_wmi = w_bf["w_mix_in"], r_wbf["w_mix_in"]
    with ExitStack() as M:
        sbM = lambda name, shape, dt: M.enter_context(nc.sbuf_tensor(uname(tag + name), shape, dt))
        uT = sbM("uT", [128, 8, S], BF16)
        r_uT = Res("uT")
        VA = sbM("VA", [128, NT, 1024], BF16)
        r_VA = [Res("VA") for _ in range(NT)]
        onaT = sbM("onaT", [128, 4, S], BF16)
        r_onaT = Res("onaT")
        cst = sbM("cst", [128, NCST], F32)
        ident = sbM("ident", [128, 128], BF16)
        neg_half = sbM("nh", [128, 8], F32)
        r_c = Res("c")
        P.dma(SP, cst[:], cst_in[:, :], writes=[r_c])
        P.op(DVE, lambda e: e.tensor_copy(out=ident[:], in_=cst[:, 0:128]), reads=[r_c], writes=[r_c])
        P.op(DVE, lambda e: e.memset(neg_half[:], -0.5), writes=[r_c])

        def _subphase():
            with ExitStack() as st:
                sb = lambda name, shape, dt: st.enter_context(nc.sbuf_tensor(uname(tag + name), shape, dt))
                ps = lambda name, shape, dt: st.enter_context(nc.psum_tensor(uname(tag + name), shape, dt))
                gpre = sb("gpre", [128, D], F32)
                xn = [sb("xn%d" % i, [128, D], F32) for i in range(6)]
                r_xn = [Res("xn") for _ in range(6)]
                ub = [sb("ub%d" % i, [128, D], BF16) for i in range(2)]
                r_ub = [Res("ub") for _ in range(2)]
                junk = sb("junk", [128, D], BF16)
                r_junk = Res("junk")
                ssq = [sb("ssq%d" % i, [128, 1], F32) for i in range(4)]
                rstd = [sb("rstd%d" % i, [128, 1], F32) for i in range(4)]
                r_ssq = [Res("ssq") for _ in range(4)]
                r_rstd = [Res("rstd") for _ in range(4)]
                pT = [ps("pT%d" % i, [128, D], BF16) for i in range(2)]
                r_pT = [Res("pT") for _ in range(2)]
                P.dma(SP, gpre[:], bcast_rows(g_pre, D), writes=[r_c])
                def m1_x(t):
                    ix, k, kb = t % 6, t % 4, t % 2
                    P.op(ACT, lambda e, ix=ix, k=k: e.activation(out=junk[:], in_=xn[ix][:], func=AF.Square, accum_out=ssq[k][:]),
                         reads=[r_xn[ix]], writes=[r_junk, r_ssq[k]])
                    norm_rstd(P, ssq[k][:], rstd[k][:], r_ssq[k], r_rstd[k], neg_half, r_c, 1, 1.0 / D)
                    P.op(DVE, lambda e, ix=ix, k=k, kb=kb: e.scalar_tensor_tensor(out=ub[kb][:], in0=xn[ix][:], scalar=rstd[k][:], in1=gpre[:], op0=ALU.mult, op1=ALU.mult),
                         reads=[r_xn[ix], r_rstd[k], r_c], writes=[r_ub[kb]])

                def m1_y(t):
                    kb = t % 2
                    for kc in range(8):
                        P.op(PE, lambda e, kb=kb, kc=kc: e.transpose(out=pT[kb][:, kc * 128:(kc + 1) * 128], in_=ub[kb][:, kc * 128:(kc + 1) * 128], identity=ident[:]),
                             reads=[r_ub[kb], r_c], writes=[r_pT[kb]], inc=(kc == 7))
                    P.op(ACT, lambda e, t=t, kb=kb: e.activation(out=uT[:, :, t * 128:(t + 1) * 128], in_=pT[kb][:].rearrange("p (k c) -> p k c", k=8), func=AF.Copy),
                         reads=[r_pT[kb]], writes=[r_uT])

                for t in range(min(6, NT)):
                    P.dma(SP, xn[t % 6][:], x_src[t * 128:(t + 1) * 128, :], writes=[r_xn[t % 6]])
                m1_x(0)
                for t in range(NT):
                    if t + 1 < NT:
                        m1_x(t + 1)
                    m1_y(t)
                    if t + 6 < NT:
                        P.dma(SP, xn[t % 6][:], x_src[(t + 6) * 128:(t + 7) * 128, :], writes=[r_xn[t % 6]])
                P.barrier()

        if _dbg_run():
            _subphase()

        def _subphase():
            with ExitStack() as st:
                sb = lambda name, shape, dt: st.enter_context(nc.sbuf_tensor(uname(tag + name), shape, dt))
                ps = lambda name, shape, dt: st.enter_context(nc.psum_tensor(uname(tag + name), shape, dt))
                Ktm = sb("Ktm", [128, NT, 512], BF16)
                r_K = [Res("K") for _ in range(NT)]
                wk = sb("wk", [128, 8, 512], BF16)
                wv = sb("wv", [128, 8, 1024], BF16)
                r_wk, r_wv = Res("wk"), Res("wv")
                rope = sb("rope", [128, 2 * NT * 64], F32)
                Sf32 = sb("Sf32", [128, 1024], F32)
                Sb32 = sb("Sb32", [128, 1024], F32)
                Sfb = sb("Sfb", [128, 1024], BF16)
                Sbb = [sb("Sbb%d" % i, [128, 1024], BF16) for i in range(2)]
                r_Sf32, r_Sb32, r_Sfb = Res("Sf32"), Res("Sb32"), Res("Sfb")
                r_Sbb = [Res("Sbb") for _ in range(2)]
                r_sbd = [Res("sbd") for _ in range(NT)]
                tA = sb("tA", [128, 512], F32)
                tB = sb("tB", [128, 512], F32)
                r_tA, r_tB = Res("tA"), Res("tB")
                qrot = sb("qrot", [128, 512], BF16)
                r_qrot = Res("qrot")
                kdec = sb("kdec", [128, 512], BF16)
                r_kdec = Res("kdec")
                qT = sb("qT", [128, 512], BF16)
                qfT = sb("qfT", [128, 512], BF16)
                qbT = sb("qbT", [128, 512], BF16)
                kT = sb("kT", [128, 512], BF16)
                r_qT, r_qfT, r_qbT, r_kT = Res("qT"), Res("qfT"), Res("qbT"), Res("kT")
                Sm = sb("Sm", [128, 512], BF16)
                r_Sm = Res("Sm")
                srg = sb("srg", [128, 1024], F32)
                r_srg = Res("srg")
                Abuf = sb("A", [128, 1024], BF16)
                r_A = Res("A")
                junk = sb("junk", [128, 256], BF16)
                r_junk = Res("junk")
                ssq4 = sb("ssq4", [128, 4], F32)
                rstd4 = sb("rstd4", [128, 4], F32)
                r_ssq4, r_rstd4 = Res("ssq4"), Res("rstd4")
                b0 = ps("b0", [128, 512], F32)
                pR = ps("pR", [128, 1024], F32)
                pTr = ps("pTr", [128, 1024], BF16)
                pY = ps("pY", [128, 1024], F32)
                pU = ps("pU", [128, 1024], F32)
                r_b0, r_pR, r_pTr, r_pY, r_pU = Res("b0"), Res("pR"), Res("pTr"), Res("pY"), Res("pU")

                P.dma(SP, rope[:], rope_in[:, :], writes=[r_c])

                if _DBG.get('rstop') == 'setup':
                    P.barrier()
                    return

                def rotary(dst, n, r_dst):
                    cb = rope[:, n * 64:(n + 1) * 64]
                    sn = rope[:, NT * 64 + n * 64:NT * 64 + (n + 1) * 64]
                    cosb = bc_ap(cb, [[0, 4], [0, 2], [1, 64]])
                    sinb = bc_ap(sn, [[0, 4], [1, 64]])
                    v4 = b0[:].rearrange("p (h t d) -> p h t d", h=4, t=2)
                    a4 = tA[:].rearrange("p (h t d) -> p h t d", h=4, t=2)
                    b4 = tB[:].rearrange("p (h t d) -> p h t d", h=4, t=2)
                    d4 = dst.rearrange("p (h t d) -> p h t d", h=4, t=2)
                    P.op(DVE, lambda e: e.tensor_tensor(out=a4, in0=v4, in1=cosb, op=ALU.mult), reads=[r_b0, r_c], writes=[r_tA])
                    P.op(DVE, lambda e: e.tensor_tensor(out=b4[:, :, 0, :], in0=v4[:, :, 1, :], in1=sinb, op=ALU.mult), reads=[r_b0, r_c], writes=[r_tB])
                    P.op(DVE, lambda e: e.tensor_tensor(out=b4[:, :, 1, :], in0=v4[:, :, 0, :], in1=sinb, op=ALU.mult), reads=[r_b0, r_c], writes=[r_tB])
                    P.op(POOL, lambda e: e.tensor_tensor(out=d4[:, :, 0, :], in0=a4[:, :, 0, :], in1=b4[:, :, 0, :], op=ALU.subtract), reads=[r_tA, r_tB], writes=[r_dst])
                    P.op(POOL, lambda e: e.tensor_tensor(out=d4[:, :, 1, :], in0=a4[:, :, 1, :], in1=b4[:, :, 1, :], op=ALU.add), reads=[r_tA, r_tB], writes=[r_dst])

                def proj_tok(dst_ps, r_dst, wbuf, r_w, n, ncols):
                    for half in range(ncols // 512):
                        for kc in range(8):
                            P.op(PE, lambda e, half=half, kc=kc: e.matmul(dst_ps[:, half * 512:(half + 1) * 512], lhsT=uT[:, kc, n * 128:(n + 1) * 128],
                                                                          rhs=wbuf[:, kc, half * 512:(half + 1) * 512], start=(kc == 0), stop=(kc == 7)),
                                 reads=[r_uT, r_w], writes=[r_dst], inc=(kc == 7 and half == ncols // 512 - 1))

                def state_update(S32, r_S32, goff, kdec_col0):
                    pass

                P.dma(SP, wk[:], wmi[:, :, C_RK:C_RK + 512], reads=[r_wmi], writes=[r_wk])
                P.dma(SP, wv[:], wmi[:, :, C_RV:C_RV + 1024], reads=[r_wmi], writes=[r_wv])
                P.op(DVE, lambda e: e.memset(Sb32[:], 0.0), writes=[r_Sb32])
                for n in range(NT - 1, -1, -1):
                    proj_tok(b0, r_b0, wk, r_wk, n, 512)
                    rotary(Ktm[:, n, :], n, r_K[n])
                    proj_tok(pR, r_pR, wv, r_wv, n, 1024)
                    P.op(ACT, lambda e, n=n: e.activation(out=VA[:, n, :], in_=pR[:], func=AF.Copy), reads=[r_pR], writes=[r_VA[n]])
                    i = n % 2
                    P.op(ACT, lambda e, i=i: e.activation(out=Sbb[i][:], in_=Sb32[:], func=AF.Copy), reads=[r_Sb32], writes=[r_Sbb[i]])
                    P.dma(SP, sb_d[n * 128:(n + 1) * 128, :], Sbb[i][:], reads=[r_Sbb[i]], writes=[r_sbd[n]])
                    if n > 0:
                        P.op(DVE, lambda e, n=n: e.tensor_tensor(out=kdec[:].rearrange("p (h d) -> p h d", h=4), in0=Ktm[:, n, :].rearrange("p (h d) -> p h d", h=4),
                                                                 in1=dtok[:, 4:8].to_broadcast([128, 4, 128]), op=ALU.mult),
                             reads=[r_K[n], r_c], writes=[r_kdec])
                        for h in range(4):
                            P.op(PE, lambda e, h=h, n=n: e.matmul(pU[:, h * 256:(h + 1) * 256], lhsT=kdec[:, h * 128:(h + 1) * 128], rhs=VA[:, n, h * 256:(h + 1) * 256], start=True, stop=True),
                                 reads=[r_kdec, r_VA[n]], writes=[r_pU], inc=(h == 3))
                        for h in range(4):
                            P.op(DVE, lambda e, h=h: e.scalar_tensor_tensor(out=Sb32[:, h * 256:(h + 1) * 256], in0=Sb32[:, h * 256:(h + 1) * 256], scalar=gC[:, 4 + h:5 + h],
                                                                             in1=pU[:, h * 256:(h + 1) * 256], op0=ALU.mult, op1=ALU.add),
                                 reads=[r_pU, r_Sb32, r_c], writes=[r_Sb32])

                if _DBG.get('rstop') == 'pass1':
                    P.barrier()
                    return
                P.dma(SP, wk[:], wmi[:, :, C_RQ:C_RQ + 512], reads=[r_wmi], writes=[r_wk])
                P.dma(SP, wv[:], wmi[:, :, C_RG:C_RG + 1024], reads=[r_wmi], writes=[r_wv])
                P.op(DVE, lambda e: e.memset(Sf32[:], 0.0), writes=[r_Sf32])
                P.op(DVE, lambda e: e.memset(Sfb[:], 0.0), writes=[r_Sfb])
                qrot2 = [qrot, sb("qrot1", [128, 512], BF16)]
                r_qrot2 = [r_qrot, Res("qrot1")]
                srg2 = [srg, sb("srg1", [128, 1024], F32)]
                r_srg2 = [r_srg, Res("srg1")]
                kdec2 = [kdec, sb("kdec1", [128, 512], BF16)]
                r_kdec2 = [r_kdec, Res("kdec1")]
                pU0, pU1 = pU[:, 0:512], pU[:, 512:1024]
                r_pU0, r_pU1 = Res("pU0"), Res("pU1")

                def stage_a(n):
                    i = n % 2
                    P.dma(SP, Sbb[i][:], sb_d[n * 128:(n + 1) * 128, :], reads=[r_sbd[n]], writes=[r_Sbb[i]])
                    proj_tok(b0, r_b0, wk, r_wk, n, 512)
                    rotary(qrot2[i][:], n, r_qrot2[i])
                    proj_tok(pR, r_pR, wv, r_wv, n, 1024)
                    P.op(ACT, lambda e, i=i: e.activation(out=srg2[i][:], in_=pR[:], func=AF.Silu), reads=[r_pR], writes=[r_srg2[i]])
                    P.op(DVE, lambda e, n=n, i=i: e.tensor_tensor(out=kdec2[i][:].rearrange("p (h d) -> p h d", h=4), in0=Ktm[:, n, :].rearrange("p (h d) -> p h d", h=4),
                                                                 in1=dtok[:, 0:4].to_broadcast([128, 4, 128]), op=ALU.mult),
                         reads=[r_K[n], r_c], writes=[r_kdec2[i]])

                def stage_t(n):
                    i = n % 2
                    for h in range(4):
                        P.op(PE, lambda e, h=h, i=i: e.transpose(out=pTr[:, h * 128:(h + 1) * 128], in_=qrot2[i][:, h * 128:(h + 1) * 128], identity=ident[:]),
                             reads=[r_qrot2[i], r_c], writes=[r_pTr], inc=False)
                    for h in range(4):
                        P.op(PE, lambda e, h=h, n=n: e.transpose(out=pTr[:, 512 + h * 128:512 + (h + 1) * 128], in_=Ktm[:, n, h * 128:(h + 1) * 128], identity=ident[:]),
                             reads=[r_K[n], r_c], writes=[r_pTr], inc=(h == 3))
                    P.op(ACT, lambda e: e.activation(out=qT[:], in_=pTr[:, 0:512], func=AF.Copy), reads=[r_pTr], writes=[r_qT])
                    P.op(ACT, lambda e: e.activation(out=kT[:], in_=pTr[:, 512:1024], func=AF.Copy), reads=[r_pTr], writes=[r_kT])
                    P.op(DVE, lambda e: e.tensor_tensor(out=qfT[:], in0=qT[:], in1=DFt[:], op=ALU.mult), reads=[r_qT, r_c], writes=[r_qfT])
                    P.op(DVE, lambda e: e.tensor_tensor(out=qbT[:], in0=qT[:], in1=DBt[:], op=ALU.mult), reads=[r_qT, r_c], writes=[r_qbT])

                def stage_s(n):
                    for h in range(4):
                        P.op(PE, lambda e, h=h: e.matmul(pU1[:, h * 128:(h + 1) * 128], lhsT=kT[:, h * 128:(h + 1) * 128], rhs=qT[:, h * 128:(h + 1) * 128], start=True, stop=True),
                             reads=[r_kT, r_qT], writes=[r_pU1], inc=(h == 3))
                    P.op(DVE, lambda e: e.tensor_tensor(out=Sm[:], in0=pU1, in1=MT[:], op=ALU.mult), reads=[r_pU1, r_c], writes=[r_Sm])

                def stage_u(n, half):
                    i = n % 2
                    for hh in range(2):
                        h = half * 2 + hh
                        P.op(PE, lambda e, h=h, hh=hh, n=n, i=i: e.matmul(pU0[:, hh * 256:(hh + 1) * 256], lhsT=kdec2[i][:, h * 128:(h + 1) * 128], rhs=VA[:, n, h * 256:(h + 1) * 256], start=True, stop=True),
                             reads=[r_kdec2[i], r_VA[n]], writes=[r_pU0], inc=(hh == 1))
                    for hh in range(2):
                        h = half * 2 + hh
                        P.op(DVE, lambda e, h=h, hh=hh: e.scalar_tensor_tensor(out=Sf32[:, h * 256:(h + 1) * 256], in0=Sf32[:, h * 256:(h + 1) * 256], scalar=gC[:, h:h + 1],
                                                                              in1=pU0[:, hh * 256:(hh + 1) * 256], op0=ALU.mult, op1=ALU.add),
                             reads=[r_pU0, r_Sf32, r_c], writes=[r_Sf32])

                def stage_y(n):
                    i = n % 2
                    for h in range(4):
                        vs = slice(h * 256, (h + 1) * 256)
                        hs = slice(h * 128, (h + 1) * 128)
                        P.op(PE, lambda e, vs=vs, hs=hs, n=n: e.matmul(pY[:, vs], lhsT=Sm[:, hs], rhs=VA[:, n, vs], start=True, stop=False),
                             reads=[r_Sm, r_VA[n]], writes=[r_pY], inc=False)
                        P.op(PE, lambda e, vs=vs, hs=hs: e.matmul(pY[:, vs], lhsT=qfT[:, hs], rhs=Sfb[:, vs], start=False, stop=False),
                             reads=[r_qfT, r_Sfb], writes=[r_pY], inc=False)
                        P.op(PE, lambda e, vs=vs, hs=hs, i=i: e.matmul(pY[:, vs], lhsT=qbT[:, hs], rhs=Sbb[i][:, vs], start=False, stop=True),
                             reads=[r_qbT, r_Sbb[i]], writes=[r_pY], inc=(h == 3))

                def stage_c(n):
                    i = n % 2
                    for h in range(4):
                        P.op(ACT, lambda e, h=h: e.activation(out=junk[:], in_=pY[:, h * 256:(h + 1) * 256], func=AF.Square, accum_out=ssq4[:, h:h + 1]),
                             reads=[r_pY], writes=[r_junk, r_ssq4])
                    norm_rstd(P, ssq4[:], rstd4[:], r_ssq4, r_rstd4, neg_half, r_c, 4, 1.0 / 256)
                    for h in range(4):
                        P.op(DVE, lambda e, h=h, i=i: e.scalar_tensor_tensor(out=Abuf[:, h * 256:(h + 1) * 256], in0=pY[:, h * 256:(h + 1) * 256], scalar=rstd4[:, h:h + 1],
                                                                            in1=srg2[i][:, h * 256:(h + 1) * 256], op0=ALU.mult, op1=ALU.mult),
                             reads=[r_pY, r_rstd4, r_srg2[i]], writes=[r_A])
                    P.op(ACT, lambda e: e.activation(out=Sfb[:], in_=Sf32[:], func=AF.Copy), reads=[r_Sf32], writes=[r_Sfb])

                def stage_at(n):
                    for kc in range(8):
                        P.op(PE, lambda e, kc=kc: e.transpose(out=pTr[:, kc * 128:(kc + 1) * 128], in_=Abuf[:, kc * 128:(kc + 1) * 128], identity=ident[:]),
                             reads=[r_A, r_c], writes=[r_pTr], inc=(kc == 7))
                    P.op(ACT, lambda e, n=n: e.activation(out=VA[:, n, :], in_=pTr[:], func=AF.Copy), reads=[r_pTr], writes=[r_VA[n]])

                stage_a(0)
                for n in range(NT):
                    stage_t(n)
                    if n + 1 < NT:
                        stage_a(n + 1)
                    if n >= 1:
                        stage_at(n - 1)
                    stage_s(n)
                    stage_u(n, 0)
                    stage_y(n)
                    stage_u(n, 1)
                    stage_c(n)
                stage_at(NT - 1)
                P.barrier()

        if _dbg_run():
            _subphase()

        def _subphase():
            with ExitStack() as st:
                sb = lambda name, shape, dt: st.enter_context(nc.sbuf_tensor(uname(tag + name), shape, dt))
                ps = lambda name, shape, dt: st.enter_context(nc.psum_tensor(uname(tag + name), shape, dt))
                nqT = sb("nqT", [128, 4, S], BF16)
                nkT = sb("nkT", [128, 4, S], BF16)
                nv = sb("nv", [128, NT, 8, 65], BF16)
                r_nq, r_nk, r_nv = Res("nq"), Res("nk"), Res("nv")
                Bt = sb("Bt", [128, NVAR, 1024], BF16)
                r_Bt = Res("Bt")
                wa = [sb("wa%d" % i, [128, 8, 512], BF16) for i in range(3)]
                r_wa = [Res("wa") for _ in range(3)]
                PT = [sb("PT%d" % i, [128, 1024], BF16) for i in range(2)]
                r_PT = [Res("PT") for _ in range(2)]
                ona = sb("ona", [128, 512], BF16)
                r_ona = Res("ona")
                rden = sb("rden", [128, 8], F32)
                r_rden = Res("rden")
                pS = [ps("pS%d" % i, [128, 1024], F32) for i in range(2)]
                r_pS = [Res("pS") for _ in range(2)]
                pO = ps("pO", [128, 2, 512], F32)
                r_pO = Res("pO")
                pTn = ps("pTn", [128, 1024], BF16)
                r_pTn = Res("pTn")

                for j, c0 in enumerate((C_NQ, C_NK, C_NV)):
                    P.dma(SP, wa[j][:], wmi[:, :, c0:c0 + 512], reads=[r_wmi], writes=[r_wa[j]])
                nabbf_ap, r_nabbf = nab_in
                P.dma(SP, Bt[:, 0:5, :], nabbf_ap[:, 0:5, :], reads=[r_nabbf], writes=[r_Bt])
                P.dma(SP, Bt[:, 5:NVAR, :], nabbf_ap[:, 5:NVAR, :], reads=[r_nabbf], writes=[r_Bt])
                P.op(DVE, lambda e: e.memset(nv[:], 1.0), writes=[r_nv])
                cnt = 0
                for j, (dst, r_dst, scale) in enumerate(((nqT, r_nq, 0.125), (nkT, r_nk, 1.0))):
                    for c in range(4):
                        for tb in range(4):
                            i = cnt % 2
                            cnt += 1
                            for kc in range(8):
                                P.op(PE, lambda e, i=i, j=j, c=c, kc=kc, tb=tb: e.matmul(pS[i][:, 0:512], lhsT=wa[j][:, kc, c * 128:(c + 1) * 128], rhs=uT[:, kc, tb * 512:(tb + 1) * 512],
                                                                                          start=(kc == 0), stop=(kc == 7)),
                                     reads=[r_wa[j], r_uT], writes=[r_pS[i]], inc=(kc == 7))
                            P.op(ACT, lambda e, i=i, dst=dst, c=c, tb=tb, scale=scale: e.activation(out=dst[:, c, tb * 512:(tb + 1) * 512], in_=pS[i][:, 0:512], func=AF.Copy, scale=scale),
                                 reads=[r_pS[i]], writes=[r_dst])
                for t in range(NT):
                    i = cnt % 2
                    cnt += 1
                    for kc in range(8):
                        P.op(PE, lambda e, i=i, kc=kc, t=t: e.matmul(pS[i][:, 0:512], lhsT=uT[:, kc, t * 128:(t + 1) * 128], rhs=wa[2][:, kc, :], start=(kc == 0), stop=(kc == 7)),
                             reads=[r_wa[2], r_uT], writes=[r_pS[i]], inc=(kc == 7))
                    P.op(ACT, lambda e, i=i, t=t: e.activation(out=nv[:, t, :, 0:64], in_=pS[i][:, 0:512].rearrange("p (h d) -> p h d", h=8), func=AF.Copy),
                         reads=[r_pS[i]], writes=[r_nv])
                nqM = [[sb("nqM%d_%d" % (s_, i), [128, 4, 128], BF16) for i in range(2)] for s_ in range(2)]
                r_nqM = [[Res("nqM") for _ in range(2)] for _ in range(2)]
                ona2 = [ona, sb("ona1", [128, 512], BF16)]
                r_ona2 = [r_ona, Res("ona1")]
                for s_ in range(2):
                    for i in range(2):
                        P.op(DVE, lambda e, s_=s_, i=i: e.memset(nqM[s_][i][:], 0.0), writes=[r_nqM[s_][i]])
                units = [(a, oi, o, len(na_offsets(a))) for a in range(NT) for oi, o in enumerate(na_offsets(a))]

                def na_scores(u):
                    a, oi, o, _ = units[u]
                    sa = a % 2
                    if oi == 0:
                        P.op(ACT, lambda e, a=a, sa=sa: e.activation(out=nqM[sa][0][0:64, :, :], in_=nqT[0:64, :, a * 128:(a + 1) * 128], func=AF.Copy),
                             reads=[r_nq], writes=[r_nqM[sa][0]])
                        P.op(ACT, lambda e, a=a, sa=sa: e.activation(out=nqM[sa][1][64:128, :, :], in_=nqT[64:128, :, a * 128:(a + 1) * 128], func=AF.Copy),
                             reads=[r_nq], writes=[r_nqM[sa][1]])
                    kt = a + o
                    var = NA_VAR_OF[(a, o)]
                    i = u % 2
                    for bank in range(2):
                        P.op(PE, lambda e, i=i, bank=bank, var=var: e.matmul(pS[i][:, bank * 512:(bank + 1) * 512], lhsT=ident[:], rhs=Bt[:, var, bank * 512:(bank + 1) * 512], start=True, stop=False),
                             reads=[r_Bt, r_c], writes=[r_pS[i]], inc=False)
                        for hh in range(4):
                            h = bank * 4 + hh
                            c = h // 2
                            P.op(PE, lambda e, i=i, h=h, c=c, kt=kt, sa=sa, hh=hh: e.matmul(
                                pS[i][:, h * 128:(h + 1) * 128], lhsT=nkT[:, c, kt * 128:(kt + 1) * 128], rhs=nqM[sa][h % 2][:, c, :],
                                start=False, stop=(hh == 3)),
                                reads=[r_nk, r_nqM[sa][h % 2]], writes=[r_pS[i]], inc=(bank == 1 and hh == 3))

                def na_pv(u):
                    a, oi, o, no = units[u]
                    kt = a + o
                    i = u % 2
                    P.op(ACT, lambda e, i=i: e.activation(out=PT[i][:], in_=pS[i][:], func=AF.Exp), reads=[r_pS[i]], writes=[r_PT[i]])
                    for h in range(8):
                        c0 = (h % 4) * 65
                        P.op(PE, lambda e, i=i, h=h, c0=c0, kt=kt, oi=oi, last=(oi == no - 1): e.matmul(
                            pO[:, h // 4, c0:c0 + 65], lhsT=PT[i][:, h * 128:(h + 1) * 128], rhs=nv[:, kt, h, :],
                            start=(oi == 0 and h % 4 == 0), stop=last),
                            reads=[r_PT[i], r_nv], writes=[r_pO], inc=(h == 7))

                def na_fin_dve(a):
                    sa = a % 2
                    po4 = pO[:, :, 0:260].rearrange("p b (h e) -> p b h e", e=65)
                    P.op(DVE, lambda e, po4=po4: e.reciprocal(out=rden[:].rearrange("p (b h e) -> p b h e", b=2, e=1), in_=po4[:, :, :, 64:65]),
                         reads=[r_pO], writes=[r_rden])
                    P.op(DVE, lambda e, po4=po4, sa=sa: e.tensor_tensor(out=ona2[sa][:].rearrange("p (b h d) -> p b h d", b=2, d=64), in0=po4[:, :, :, 0:64],
                                                                       in1=bc_ap(rden[:], [[4, 2], [1, 4], [0, 64]]), op=ALU.mult),
                         reads=[r_pO, r_rden], writes=[r_ona2[sa]])

                def na_fin_pe(a):
                    sa = a % 2
                    for c in range(4):
                        P.op(PE, lambda e, c=c, sa=sa: e.transpose(out=pTn[:, c * 128:(c + 1) * 128], in_=ona2[sa][:, c * 128:(c + 1) * 128], identity=ident[:]),
                             reads=[r_ona2[sa], r_c], writes=[r_pTn], inc=(c == 3))
                    P.op(ACT, lambda e, a=a: e.activation(out=onaT[:, :, a * 128:(a + 1) * 128], in_=pTn[:, 0:512].rearrange("p (k c) -> p k c", k=4), func=AF.Copy),
                         reads=[r_pTn], writes=[r_onaT])

                na_scores(0)
                pend = None
                for u in range(len(units)):
                    if u + 1 < len(units):
                        na_scores(u + 1)
                    na_pv(u)
                    if pend is not None:
                        na_fin_pe(pend)
                        pend = None
                    a, oi, o, no = units[u]
                    if oi == no - 1:
                        na_fin_dve(a)
                        pend = a
                na_fin_pe(pend)
                P.barrier()

        if _dbg_run():
            _subphase()

        def _subphase():
            with ExitStack() as st:
                sb = lambda name, shape, dt: st.enter_context(nc.sbuf_tensor(uname(tag + name), shape, dt))
                ps = lambda name, shape, dt: st.enter_context(nc.psum_tensor(uname(tag + name), shape, dt))
                wro = sb("wro", [128, 8, D], BF16)
                wno = sb("wno", [128, 4, D], BF16)
                wgr = sb("wgr", [128, 8, D], BF16)
                wgn = sb("wgn", [128, 8, D], BF16)
                wmo = sb("wmo", [128, 8, D], BF16)
                r_w = Res("w")
                gpost = sb("gpost", [128, D], F32)
                mT = sb("mT", [128, 8, 512], BF16)
                r_mT = Res("mT")
                t1 = [sb("t1_%d" % i, [128, 512], F32) for i in range(2)]
                t2 = [sb("t2_%d" % i, [128, 512], F32) for i in range(2)]
                m1 = [sb("m1_0", [128, 512], F32)] * 2
                m2 = [sb("m2_0", [128, 512], F32)] * 2
                r_t1 = [Res("t1") for _ in range(2)]
                r_t2 = [Res("t2") for _ in range(2)]
                r_m1 = [Res("m1")] * 2
                r_m2 = [Res("m2")] * 2
                tmp = [sb("tmp%d" % i, [128, D], F32) for i in range(2)]
                r_tmp = [Res("tmp") for _ in range(2)]
                xr = [sb("xr%d" % i, [128, D], F32) for i in range(2)]
                r_xr = [Res("xr") for _ in range(2)]
                junk = sb("junk", [128, D], BF16)
                r_junk = Res("junk")
                ssq = [sb("ssq%d" % i, [128, 1], F32) for i in range(2)]
                rstd = [sb("rstd%d" % i, [128, 1], F32) for i in range(2)]
                r_ssq = [Res("ssq") for _ in range(2)]
                r_rstd = [Res("rstd") for _ in range(2)]
                pA = [ps("pA%d" % i, [128, 512], F32) for i in range(4)]
                r_pA = [Res("pA") for _ in range(4)]
                pM = [ps("pM%d" % i, [128, D], F32) for i in range(2)]
                r_pM = [Res("pM") for _ in range(2)]

                P.dma(SP, gpost[:], bcast_rows(g_post, D), writes=[r_c])
                P.op(DVE, lambda e: e.tensor_scalar(out=gpost[:], in0=gpost[:], scalar1=0.5, scalar2=None, op0=ALU.mult), reads=[r_c], writes=[r_c])
                r_wro = [Res("wro") for _ in range(8)]
                r_wgr = [Res("wgr") for _ in range(8)]
                r_wno = [Res("wno") for _ in range(8)]
                r_wgn = [Res("wgn") for _ in range(8)]
                r_wmo = [Res("wmo") for _ in range(2)]
                for c in range(8):
                    cs = slice(c * 128, (c + 1) * 128)
                    P.dma(SP, wro[:, :, cs], w_bf["w_ret_out"][:, :, cs], reads=[r_wbf["w_ret_out"]], writes=[r_wro[c]])
                    P.dma(SP, wgr[:, :, cs], wmi[:, :, C_GR + c * 128:C_GR + (c + 1) * 128], reads=[r_wmi], writes=[r_wgr[c]])
                    P.dma(SP, wno[:, :, cs], w_bf["w_na_out"][:, :, cs], reads=[r_wbf["w_na_out"]], writes=[r_wno[c]])
                    P.dma(SP, wgn[:, :, cs], wmi[:, :, C_GN + c * 128:C_GN + (c + 1) * 128], reads=[r_wmi], writes=[r_wgn[c]])
                for half in range(2):
                    hs_ = slice(half * 512, (half + 1) * 512)
                    P.dma(SP, wmo[:, :, hs_], w_bf["w_mix_out"][:, :, hs_], reads=[r_wbf["w_mix_out"]], writes=[r_wmo[half]])
                k2 = 0
                for tb in range(4):
                    tsl = slice(tb * 512, (tb + 1) * 512)
                    for c in range(8):
                        cs = slice(c * 128, (c + 1) * 128)
                        b = k2 % 2
                        k2 += 1
                        for kc in range(8):
                            P.op(PE, lambda e, kc=kc, cs=cs, tsl=tsl: e.matmul(pA[1][:], lhsT=wgr[:, kc, cs], rhs=uT[:, kc, tsl], start=(kc == 0), stop=(kc == 7)),
                                 reads=[r_wgr[c], r_uT], writes=[r_pA[1]], inc=(kc == 7))
                        for kc in range(8):
                            P.op(PE, lambda e, kc=kc, cs=cs, tsl=tsl: e.matmul(pA[3][:], lhsT=wgn[:, kc, cs], rhs=uT[:, kc, tsl], start=(kc == 0), stop=(kc == 7)),
                                 reads=[r_wgn[c], r_uT], writes=[r_pA[3]], inc=(kc == 7))
                        for kc in range(8):
                            P.op(PE, lambda e, kc=kc, cs=cs, tb=tb: e.matmul(pA[0][:], lhsT=wro[:, kc, cs], rhs=VA[:, tb * 4:(tb + 1) * 4, kc * 128:(kc + 1) * 128],
                                                                            start=(kc == 0), stop=(kc == 7)),
                                 reads=[r_wro[c]] + r_VA[tb * 4:(tb + 1) * 4], writes=[r_pA[0]], inc=(kc == 7))
                        for kc in range(4):
                            P.op(PE, lambda e, kc=kc, cs=cs, tsl=tsl: e.matmul(pA[2][:], lhsT=wno[:, kc, cs], rhs=onaT[:, kc, tsl], start=(kc == 0), stop=(kc == 3)),
                                 reads=[r_wno[c], r_onaT], writes=[r_pA[2]], inc=(kc == 3))
                        P.op(ACT, lambda e, b=b: e.activation(out=t1[b][:], in_=pA[1][:], func=AF.Tanh, scale=0.5), reads=[r_pA[1]], writes=[r_t1[b]])
                        P.op(ACT, lambda e, b=b: e.activation(out=t2[b][:], in_=pA[3][:], func=AF.Tanh, scale=0.5), reads=[r_pA[3]], writes=[r_t2[b]])
                        P.op(DVE, lambda e, b=b: e.scalar_tensor_tensor(out=m1[b][:], in0=t1[b][:], scalar=1.0, in1=pA[0][:], op0=ALU.add, op1=ALU.mult),
                             reads=[r_t1[b], r_pA[0]], writes=[r_m1[b]])
                        P.op(DVE, lambda e, b=b: e.scalar_tensor_tensor(out=m2[b][:], in0=t2[b][:], scalar=1.0, in1=pA[2][:], op0=ALU.add, op1=ALU.mult),
                             reads=[r_t2[b], r_pA[2]], writes=[r_m2[b]])
                        P.op(POOL, lambda e, b=b, c=c: e.tensor_tensor(out=mT[:, c, :], in0=m1[b][:], in1=m2[b][:], op=ALU.add),
                             reads=[r_m1[b], r_m2[b]], writes=[r_mT])
                    for tt in range(4):
                        t = tb * 4 + tt
                        ip = t % 2
                        if t == 0:
                            P.dma(SP, xr[ip][:], x_src[t * 128:(t + 1) * 128, :], writes=[r_xr[ip]])
                        for half in range(2):
                            for kc in range(8):
                                P.op(PE, lambda e, ip=ip, half=half, kc=kc, tt=tt: e.matmul(pM[ip][:, half * 512:(half + 1) * 512], lhsT=mT[:, kc, tt * 128:(tt + 1) * 128],
                                                                                            rhs=wmo[:, kc, half * 512:(half + 1) * 512], start=(kc == 0), stop=(kc == 7)),
                                     reads=[r_mT, r_wmo[half]], writes=[r_pM[ip]], inc=(half == 1 and kc == 7))
                        if t + 1 < NT:
                            P.dma(SP, xr[1 - ip][:], x_src[(t + 1) * 128:(t + 2) * 128, :], writes=[r_xr[1 - ip]])
                        P.op(ACT, lambda e, ip=ip: e.activation(out=junk[:], in_=pM[ip][:], func=AF.Square, accum_out=ssq[ip][:]),
                             reads=[r_pM[ip]], writes=[r_junk, r_ssq[ip]])
                        norm_rstd(P, ssq[ip][:], rstd[ip][:], r_ssq[ip], r_rstd[ip], neg_half, r_c, 1, 0.25 / D)
                        P.op(DVE, lambda e, ip=ip: e.scalar_tensor_tensor(out=tmp[ip][:], in0=pM[ip][:], scalar=rstd[ip][:], in1=gpost[:], op0=ALU.mult, op1=ALU.mult),
                             reads=[r_pM[ip], r_rstd[ip], r_c], writes=[r_tmp[ip]])
                        P.op(POOL, lambda e, ip=ip: e.tensor_tensor(out=xr[ip][:], in0=xr[ip][:], in1=tmp[ip][:], op=ALU.add),
                             reads=[r_tmp[ip], r_xr[ip]], writes=[r_xr[ip]])
                        P.dma(SP, x_dst[t * 128:(t + 1) * 128, :], xr[ip][:], reads=[r_xr[ip]])
                P.barrier()

        if _dbg_run():
            _subphase()
```

```python
import math
import numpy as np
import concourse.bass as bass
import concourse.mybir as mybir
from concourse.bass_utils import run_bass_kernel_spmd

F32 = mybir.dt.float32
BF16 = mybir.dt.bfloat16
AF = mybir.ActivationFunctionType
ALU = mybir.AluOpType

D = 1024
S = 2048
NT = S // 128
DFF = 2816
NJ = DFF // 128
EPS = 1e-6
MIXW = 6656
N_CORES = 8
SEQ_PER_CORE = 4

PE, ACT, DVE, POOL, SP = "pe", "act", "dve", "pool", "sp"
ENGS = (PE, ACT, DVE, POOL, SP)


class Res:
    __slots__ = ("name", "w", "r")

    def __init__(self, name):
        self.name = name
        self.w = None
        self.r = {}


class Prog:
    def __init__(self, nc, n_dma_sems=48):
        self.nc = nc
        self.eng = {PE: nc.tensor, ACT: nc.scalar, DVE: nc.vector, POOL: nc.gpsimd, SP: nc.sync}
        self.streams = {e: [] for e in ENGS}
        self.cnt = {e: 0 for e in ENGS}
        self.waited = {e: {} for e in ENGS}
        self.sems = {}
        self.n_dma_sems = n_dma_sems
        self.dma_tot = [0] * n_dma_sems
        self.dma_rr = 0
        self.dma_rr_q = {}
        self.n_ops = 0

    def alloc_sems(self, stack):
        for e in (PE, ACT, DVE, POOL):
            self.sems[e] = stack.enter_context(self.nc.semaphore("s_" + e))
        for i in range(self.n_dma_sems):
            self.sems[("d", i)] = stack.enter_context(self.nc.semaphore("s_d%d" % i))

    def _need(self, e, key, val, waits):
        if key == e and e == PE:
            return
        if self.waited[e].get(key, 0) >= val:
            return
        self.waited[e][key] = val
        waits.append((key, val))

    def _deps(self, e, reads, writes):
        waits = []
        for r in reads:
            if r.w is not None:
                self._need(e, r.w[0], r.w[1], waits)
        for w in writes:
            if w.w is not None:
                self._need(e, w.w[0], w.w[1], waits)
            for k, v in w.r.items():
                self._need(e, k, v, waits)
        return waits

    def _mark(self, key, val, reads, writes):
        for r in reads:
            if r.r.get(key, 0) < val:
                r.r[key] = val
        for w in writes:
            w.w = (key, val)
            w.r = {}

    def op(self, e, fn, reads=(), writes=(), inc=True):
        waits = self._deps(e, reads, writes)
        val = self.cnt[e] + 1
        if inc:
            self.cnt[e] = val
        self.streams[e].append((waits, fn, (e, 1) if inc else None))
        self._mark(e, val, reads, writes)
        self.n_ops += 1

    def dma(self, q, out_ap, in_ap, reads=(), writes=()):
        if q == SP:
            lo, hi = 0, self.n_dma_sems - 24
        else:
            lo, hi = self.n_dma_sems - 24, self.n_dma_sems
        rr = self.dma_rr_q.get(q, lo)
        i = rr
        self.dma_rr_q[q] = lo + (rr + 1 - lo) % (hi - lo)
        key = ("d", i)
        waits = self._deps(q, reads, writes)
        if self.dma_tot[i] > 0:
            self._need(q, key, self.dma_tot[i], waits)
        self.dma_tot[i] += 16
        val = self.dma_tot[i]
        self.streams[q].append((waits, lambda eng: eng.dma_start(out=out_ap, in_=in_ap), (key, 16)))
        self._mark(key, val, reads, writes)
        self.n_ops += 1

    def barrier(self):
        for e in ENGS:
            waits = []
            for k in (PE, ACT, DVE, POOL):
                if self.cnt[k] > 0:
                    self._need(e, k, self.cnt[k], waits)
            for i in range(self.n_dma_sems):
                if self.dma_tot[i] > 0:
                    self._need(e, ("d", i), self.dma_tot[i], waits)
            if waits:
                self.streams[e].append((waits, None, None))

    def check_deadlock(self):
        val = {}
        pos = {e: 0 for e in ENGS}
        progress = True
        while progress:
            progress = False
            for e in ENGS:
                st = self.streams[e]
                while pos[e] < len(st):
                    waits, fn, inc = st[pos[e]]
                    if any(val.get(k, 0) < v for k, v in waits):
                        break
                    if inc is not None:
                        val[inc[0]] = val.get(inc[0], 0) + inc[1]
                    pos[e] += 1
                    progress = True
        stuck = {e: (pos[e], len(self.streams[e]), [(k, v, val.get(k, 0)) for k, v in self.streams[e][pos[e]][0] if val.get(k, 0) < v])
                 for e in ENGS if pos[e] < len(self.streams[e])}
        if stuck:
            raise RuntimeError("deadlock in emitted program: %r" % (stuck,))

    def emit(self, block):
        prog = self
        self.check_deadlock()

        def run(e):
            def body(eng):
                for waits, fn, inc in prog.streams[e]:
                    for key, val in waits:
                        eng.wait_ge(prog.sems[key], val)
                    if fn is not None:
                        ins = fn(eng)
                        if inc is not None:
                            ins.then_inc(prog.sems[inc[0]], inc[1])
            return body

        block.tensor(run(PE))
        block.scalar(run(ACT))
        block.vector(run(DVE))
        block.gpsimd(run(POOL))
        block.sync(run(SP))


WEIGHTS = {
    "ffn1_w_in": (D, 2 * DFF), "ffn1_w_out": (DFF, D),
    "w_mix_in": (D, MIXW), "w_ret_out": (D, D), "w_na_out": (512, D), "w_mix_out": (D, D),
    "ffn2_w_in": (D, 2 * DFF), "ffn2_w_out": (DFF, D),
}
GAINS = ["ffn1_pre_norm", "ffn1_post_norm", "mix_pre_norm", "mix_post_norm", "ffn2_pre_norm", "ffn2_post_norm"]


_UNAME = [0]


def uname(base):
    _UNAME[0] += 1
    return "%s_%d" % (base, _UNAME[0])


def bcast_rows(ap_1xn, n):
    return bass.AP(ap_1xn.tensor, ap_1xn.offset, [[0, 128], [1, n]])


def build(n_seq=SEQ_PER_CORE, stop_after="C", t_ffn=1024):
    from contextlib import ExitStack
    nc = bass.Bass("TRN2", target_bir_lowering=False)
    ntok = n_seq * S
    x_in = nc.dram_tensor("x", [ntok, D], F32, kind="ExternalInput").ap()
    y_out = nc.dram_tensor("y", [ntok, D], F32, kind="ExternalOutput").ap()
    w_in = {k: nc.dram_tensor(k, list(v), F32, kind="ExternalInput").ap() for k, v in WEIGHTS.items()}
    g_in = {k: nc.dram_tensor(k, [1, D], F32, kind="ExternalInput").ap() for k in GAINS}
    ident_in = nc.dram_tensor("ident", [128, 128], F32, kind="ExternalInput").ap()
    cst_in = nc.dram_tensor("cst", [128, NCST], F32, kind="ExternalInput").ap()
    rope_in = nc.dram_tensor("rope", [128, 2 * NT * 64], F32, kind="ExternalInput").ap()
    nab_in = nc.dram_tensor("nab", [NVAR * 128, 1024], F32, kind="ExternalInput").ap()
    dec_in = nc.dram_tensor("dec", [1, 8], F32, kind="ExternalInput").ap()
    sb_d = nc.dram_tensor("sb_scr", [NT * 128, 1024], BF16, kind="Internal").ap()
    nab_bf = nc.dram_tensor("nab_bf", [128, NVAR, 1024], BF16, kind="Internal").ap()
    w_bf = {k: nc.dram_tensor(k + "_bf", [128, v[0] // 128, v[1]], BF16, kind="Internal").ap()
            for k, v in WEIGHTS.items()}
    x1_d = nc.dram_tensor("x1_scr", [ntok, D], F32, kind="Internal").ap()
    x2_d = nc.dram_tensor("x2_scr", [ntok, D], F32, kind="Internal").ap()

    P = Prog(nc)
    with ExitStack() as top:
        P.alloc_sems(top)
        r_wbf = {k: Res("wbf_" + k) for k in WEIGHTS}
        r_nabbf = Res("nabbf")

        def cast_jobs(name, src3, dst3, r_dst, nkc, N, nk_step):
            jobs = []
            for k0 in range(0, nkc, nk_step):
                k1 = min(nkc, k0 + nk_step)
                for c0 in range(0, N, 2048):
                    c1 = min(N, c0 + 2048)
                    jobs.append((dst3[:, k0:k1, c0:c1], src3[:, k0:k1, c0:c1], r_dst))
            return jobs

        def wsrc(name):
            return w_in[name].rearrange("(kc p) n -> p kc n", p=128)

        front = cast_jobs("ffn1_w_in", wsrc("ffn1_w_in"), w_bf["ffn1_w_in"], r_wbf["ffn1_w_in"], 8, 2 * DFF, 1)
        front += cast_jobs("ffn1_w_out", wsrc("ffn1_w_out"), w_bf["ffn1_w_out"], r_wbf["ffn1_w_out"], NJ, D, 2)
        bg_jobs = cast_jobs("w_mix_in", wsrc("w_mix_in"), w_bf["w_mix_in"], r_wbf["w_mix_in"], 8, MIXW, 1)
        bg_jobs += cast_jobs("w_ret_out", wsrc("w_ret_out"), w_bf["w_ret_out"], r_wbf["w_ret_out"], 8, D, 2)
        bg_jobs += cast_jobs("w_na_out", wsrc("w_na_out"), w_bf["w_na_out"], r_wbf["w_na_out"], 4, D, 2)
        bg_jobs += cast_jobs("w_mix_out", wsrc("w_mix_out"), w_bf["w_mix_out"], r_wbf["w_mix_out"], 8, D, 2)
        bg_jobs += cast_jobs("nab", nab_in.rearrange("(v p) n -> p v n", p=128), nab_bf, r_nabbf, NVAR, 1024, 2)
        bg_jobs += cast_jobs("ffn2_w_in", wsrc("ffn2_w_in"), w_bf["ffn2_w_in"], r_wbf["ffn2_w_in"], 8, 2 * DFF, 1)
        bg_jobs += cast_jobs("ffn2_w_out", wsrc("ffn2_w_out"), w_bf["ffn2_w_out"], r_wbf["ffn2_w_out"], NJ, D, 2)
        for dst, src, r_dst in front:
            P.dma(POOL, dst, src, writes=[r_dst])

        MT = top.enter_context(nc.sbuf_tensor("g_MT", [128, 512], F32))
        DFt = top.enter_context(nc.sbuf_tensor("g_DFt", [128, 512], F32))
        DBt = top.enter_context(nc.sbuf_tensor("g_DBt", [128, 512], F32))
        dtok = top.enter_context(nc.sbuf_tensor("g_dtok", [128, 8], F32))
        gC = top.enter_context(nc.sbuf_tensor("g_gC", [128, 8], F32))
        tabs = (MT, DFt, DBt, dtok, gC)
        with ExitStack() as st:
            cst = st.enter_context(nc.sbuf_tensor("g_cst", [128, NCST], F32))
            lgt = st.enter_context(nc.sbuf_tensor("g_lgt", [128, 8], F32))
            tA = st.enter_context(nc.sbuf_tensor("g_tA", [128, 512], F32))
            tB = st.enter_context(nc.sbuf_tensor("g_tB", [128, 512], F32))
            r_c = Res("gconst")
            P.dma(SP, cst[:], cst_in[:, :], writes=[r_c])
            P.dma(SP, lgt[:], bcast_rows(dec_in, 8), writes=[r_c])
            P.op(ACT, lambda e: e.activation(out=lgt[:], in_=lgt[:], func=AF.Exp, scale=-1.0), reads=[r_c], writes=[r_c])
            P.op(ACT, lambda e: e.activation(out=lgt[:], in_=lgt[:], func=AF.Ln, bias=1.0), reads=[r_c], writes=[r_c])
            P.op(DVE, lambda e: e.tensor_scalar(out=lgt[:], in0=lgt[:], scalar1=-1.0, scalar2=None, op0=ALU.mult), reads=[r_c], writes=[r_c])
            for h in range(4):
                hs = slice(h * 128, (h + 1) * 128)
                P.op(ACT, lambda e, h=h: e.activation(out=tA[:, 0:128], in_=cst[:, 128:256], func=AF.Exp, scale=lgt[:, h:h + 1]), reads=[r_c], writes=[r_c])
                P.op(DVE, lambda e, hs=hs: e.tensor_tensor(out=MT[:, hs], in0=tA[:, 0:128], in1=cst[:, 384:512], op=ALU.mult), reads=[r_c], writes=[r_c])
                P.op(ACT, lambda e, h=h: e.activation(out=tB[:, 0:128], in_=cst[:, 256:384], func=AF.Exp, scale=lgt[:, 4 + h:5 + h]), reads=[r_c], writes=[r_c])
                P.op(DVE, lambda e: e.tensor_tensor(out=tB[:, 0:128], in0=tB[:, 0:128], in1=cst[:, 512:640], op=ALU.mult), reads=[r_c], writes=[r_c])
                P.op(DVE, lambda e, hs=hs: e.tensor_tensor(out=MT[:, hs], in0=MT[:, hs], in1=tB[:, 0:128], op=ALU.add), reads=[r_c], writes=[r_c])
                P.op(ACT, lambda e, h=h, hs=hs: e.activation(out=DFt[:, hs], in_=cst[:, 640:768], func=AF.Exp, scale=lgt[:, h:h + 1]), reads=[r_c], writes=[r_c])
                P.op(ACT, lambda e, h=h, hs=hs: e.activation(out=DBt[:, hs], in_=cst[:, 768:896], func=AF.Exp, scale=lgt[:, 4 + h:5 + h]), reads=[r_c], writes=[r_c])
            P.op(DVE, lambda e: e.tensor_scalar(out=DFt[:], in0=DFt[:], scalar1=RET_SCALE, scalar2=None, op0=ALU.mult), reads=[r_c], writes=[r_c])
            P.op(DVE, lambda e: e.tensor_scalar(out=DBt[:], in0=DBt[:], scalar1=RET_SCALE, scalar2=None, op0=ALU.mult), reads=[r_c], writes=[r_c])
            P.op(ACT, lambda e: e.activation(out=dtok[:, 0:4], in_=lgt[:, 0:4], func=AF.Exp, scale=cst[:, 896:897]), reads=[r_c], writes=[r_c])
            P.op(ACT, lambda e: e.activation(out=dtok[:, 4:8], in_=lgt[:, 4:8], func=AF.Exp, scale=cst[:, 897:898]), reads=[r_c], writes=[r_c])
            P.op(ACT, lambda e: e.activation(out=gC[:], in_=lgt[:], func=AF.Exp, scale=128.0), reads=[r_c], writes=[r_c])
            P.barrier()

        ffn_phase(nc, P, x_in, x1_d if stop_after != "A" else y_out,
                  w_bf["ffn1_w_in"], w_bf["ffn1_w_out"], r_wbf["ffn1_w_in"], r_wbf["ffn1_w_out"],
                  g_in["ffn1_pre_norm"], g_in["ffn1_post_norm"], ident_in, t_ffn, "f1", ntok, bg_jobs)
        for dst, src, r_dst in bg_jobs:
            P.dma(POOL, dst, src, writes=[r_dst])
        P.barrier()
        if stop_after != "A":
            for sq in range(n_seq):
                tok0 = sq * S
                mix_phase(nc, P, x1_d[tok0:tok0 + S, :], x2_d[tok0:tok0 + S, :] if stop_after != "B" else y_out[tok0:tok0 + S, :],
                          w_bf, r_wbf, g_in["mix_pre_norm"], g_in["mix_post_norm"], cst_in, rope_in, (nab_bf, r_nabbf), tabs, sb_d, "mxs%d" % sq)
                P.barrier()
            if stop_after != "B":
                ffn_phase(nc, P, x2_d, y_out,
                          w_bf["ffn2_w_in"], w_bf["ffn2_w_out"], r_wbf["ffn2_w_in"], r_wbf["ffn2_w_out"],
                          g_in["ffn2_pre_norm"], g_in["ffn2_post_norm"], ident_in, t_ffn, "f2", ntok)
                P.barrier()

        P.barrier()
        with nc.Block() as block:
            P.emit(block)
    return nc


def norm_rstd(P, pool_ssq, pool_rstd, r_ssq, r_rstd, neg_half, r_consts, n, inv_n):
    P.op(POOL, lambda e: e.tensor_scalar(out=pool_rstd, in0=pool_ssq, scalar1=inv_n, scalar2=EPS, op0=ALU.mult, op1=ALU.add),
         reads=[r_ssq], writes=[r_rstd])
    P.op(POOL, lambda e: e.tensor_tensor(out=pool_rstd, in0=pool_rstd, in1=neg_half[:, 0:n], op=ALU.pow),
         reads=[r_rstd, r_consts], writes=[r_rstd])


def ffn_phase(nc, P, x_src, x_dst, win_bf, wout_bf, r_win, r_wout, g_pre, g_post, ident_in, T, tag, ntok=S, bg_jobs=None):
    from contextlib import ExitStack
    NG = ntok // T
    TT = T // 128
    TB = T // 512
    NWB = 3
    with ExitStack() as st:
        sb = lambda name, shape, dt: st.enter_context(nc.sbuf_tensor(uname(tag + name), shape, dt))
        ps = lambda name, shape, dt: st.enter_context(nc.psum_tensor(uname(tag + name), shape, dt))
        NXN, NXR = 3, 2
        xn = [sb("xn%d" % i, [128, D], F32) for i in range(NXN)]
        r_xn = [Res("xn") for _ in range(NXN)]
        xr = [sb("xr%d" % i, [128, D], F32) for i in range(NXR)]
        r_xr = [Res("xr") for _ in range(NXR)]
        uT = [sb("uT%d" % i, [128, 8, T], BF16) for i in range(2)]
        r_uT = [[Res("uT") for _ in range(TB)] for _ in range(2)]
        hT = sb("hT", [128, NJ, T], BF16)
        r_hT = Res("hT")
        wout = sb("wout", [128, NJ, D], BF16)
        r_wo = Res("wout")
        wg = [sb("wg%d" % i, [128, 8, 256], BF16) for i in range(NWB)]
        wu = [sb("wu%d" % i, [128, 8, 256], BF16) for i in range(NWB)]
        r_w = [Res("w") for _ in range(NWB)]
        gpre = sb("gpre", [128, D], F32)
        gpost = sb("gpost", [128, D], F32)
        ident_f = sb("identf", [128, 128], F32)
        ident = sb("ident", [128, 128], BF16)
        neg_half = sb("nh", [128, 8], F32)
        r_c = Res("consts")
        ub = [sb("ub%d" % i, [128, D], BF16) for i in range(2)]
        r_ub = [Res("ub") for _ in range(2)]
        junk = sb("junk", [128, D], BF16)
        r_junk = Res("junk")
        ssq = [sb("ssq%d" % i, [128, 1], F32) for i in range(4)]
        rstd = [sb("rstd%d" % i, [128, 1], F32) for i in range(4)]
        r_ssq = [Res("ssq") for _ in range(4)]
        r_rstd = [Res("rstd") for _ in range(4)]
        sg = [sb("sg%d" % i, [128, 512], F32) for i in range(2)]
        r_sg = [Res("sg") for _ in range(2)]
        tmp = [sb("tmp%d" % i, [128, D], F32) for i in range(2)]
        r_tmp = [Res("tmp") for _ in range(2)]
        p2 = [ps("p2_%d" % i, [128, 512], F32) for i in range(3)]
        r_p2 = [Res("p2") for _ in range(3)]
        p3 = [ps("p3_%d" % i, [128, D], F32) for i in range(2)]
        r_p3 = [Res("p3") for _ in range(2)]
        pT = ps("pT", [128, D], BF16)
        r_pT = Res("pT")

        P.dma(SP, gpre[:], bcast_rows(g_pre, D), writes=[r_c])
        P.dma(SP, gpost[:], bcast_rows(g_post, D), writes=[r_c])
        P.dma(SP, ident_f[:], ident_in[:, :], writes=[r_c])
        P.op(DVE, lambda e: e.tensor_copy(out=ident[:], in_=ident_f[:]), reads=[r_c], writes=[r_c])
        P.op(DVE, lambda e: e.tensor_scalar(out=gpost[:], in0=gpost[:], scalar1=0.5, scalar2=None, op0=ALU.mult), reads=[r_c], writes=[r_c])
        P.op(DVE, lambda e: e.memset(neg_half[:], -0.5), writes=[r_c])

        cnt = {"n": 0, "p2": 0, "p3": 0, "w": 0, "sg": 0, "tmp": 0, "xn": 0, "xr": 0}

        nrec = {}

        def stage_x(g, t):
            k = cnt["n"] % 4
            kb = cnt["n"] % 2
            cnt["n"] += 1
            ix = cnt["xn"] % NXN
            cnt["xn"] += 1
            nrec[(g, t)] = kb
            r0 = g * T + t * 128
            P.dma(SP, xn[ix][:], x_src[r0:r0 + 128, :], writes=[r_xn[ix]])
            xin = xn[ix][:]
            P.op(ACT, lambda e, xin=xin, k=k: e.activation(out=junk[:], in_=xin, func=AF.Square, accum_out=ssq[k][:]),
                 reads=[r_xn[ix]], writes=[r_junk, r_ssq[k]])
            norm_rstd(P, ssq[k][:], rstd[k][:], r_ssq[k], r_rstd[k], neg_half, r_c, 1, 1.0 / D)
            P.op(DVE, lambda e, xin=xin, k=k, kb=kb: e.scalar_tensor_tensor(out=ub[kb][:], in0=xin, scalar=rstd[k][:], in1=gpre[:], op0=ALU.mult, op1=ALU.mult),
                 reads=[r_xn[ix], r_rstd[k], r_c], writes=[r_ub[kb]])

        def stage_y(g, t):
            b = g % 2
            kb = nrec[(g, t)]
            for kc in range(8):
                P.op(PE, lambda e, kb=kb, kc=kc: e.transpose(out=pT[:, kc * 128:(kc + 1) * 128], in_=ub[kb][:, kc * 128:(kc + 1) * 128], identity=ident[:]),
                     reads=[r_ub[kb], r_c], writes=[r_pT], inc=(kc == 7))
            P.op(ACT, lambda e, b=b, t=t: e.activation(out=uT[b][:, :, t * 128:(t + 1) * 128], in_=pT[:].rearrange("p (k c) -> p k c", k=8), func=AF.Copy),
                 reads=[r_pT], writes=[r_uT[b][t // 4]])

        def norm_group(g):
            stage_x(g, 0)
            for t in range(TT):
                if t + 1 < TT:
                    stage_x(g, t + 1)
                stage_y(g, t)

        def load_w(jb):
            s = cnt["w"] % NWB
            cnt["w"] += 1
            c0 = jb * 256
            P.dma(SP, wg[s][:], win_bf[:, :, c0:c0 + 256], reads=[r_win], writes=[r_w[s]])
            P.dma(SP, wu[s][:], win_bf[:, :, DFF + c0:DFF + c0 + 256], reads=[r_win], writes=[r_w[s]])
            return s

        NJB = NJ // 2
        norm_group(0)
        slots = [load_w(0), load_w(1)]
        for g in range(NG):
            b = g % 2
            for jb in range(NJB):
                s = slots.pop(0)
                nxt = jb + 2
                if nxt < NJB:
                    slots.append(load_w(nxt))
                elif g + 1 < NG:
                    slots.append(load_w(nxt - NJB))
                P.dma(SP, wout[:, 2 * jb:2 * jb + 2, :], wout_bf[:, 2 * jb:2 * jb + 2, :], reads=[r_wout], writes=[r_wo])
                for jj in range(2):
                    j = jb * 2 + jj
                    for tb in range(TB):
                        ig = cnt["p2"] % 3
                        iu = (cnt["p2"] + 1) % 3
                        cnt["p2"] += 2
                        for kc in range(8):
                            P.op(PE, lambda e, ig=ig, s=s, jj=jj, kc=kc, b=b, tb=tb: e.matmul(
                                p2[ig][:], lhsT=wg[s][:, kc, jj * 128:(jj + 1) * 128], rhs=uT[b][:, kc, tb * 512:(tb + 1) * 512],
                                start=(kc == 0), stop=(kc == 7)),
                                reads=[r_w[s], r_uT[b][tb]], writes=[r_p2[ig]], inc=(kc == 7))
                        for kc in range(8):
                            P.op(PE, lambda e, iu=iu, s=s, jj=jj, kc=kc, b=b, tb=tb: e.matmul(
                                p2[iu][:], lhsT=wu[s][:, kc, jj * 128:(jj + 1) * 128], rhs=uT[b][:, kc, tb * 512:(tb + 1) * 512],
                                start=(kc == 0), stop=(kc == 7)),
                                reads=[r_w[s], r_uT[b][tb]], writes=[r_p2[iu]], inc=(kc == 7))
                        k = cnt["sg"] % 2
                        cnt["sg"] += 1
                        P.op(ACT, lambda e, k=k, ig=ig: e.activation(out=sg[k][:], in_=p2[ig][:], func=AF.Silu),
                             reads=[r_p2[ig]], writes=[r_sg[k]])
                        P.op(DVE, lambda e, k=k, iu=iu, j=j, tb=tb: e.tensor_tensor(out=hT[:, j, tb * 512:(tb + 1) * 512], in0=p2[iu][:], in1=sg[k][:], op=ALU.mult),
                             reads=[r_p2[iu], r_sg[k]], writes=[r_hT])
                if bg_jobs:
                    for _ in range(min(1, len(bg_jobs))):
                        dst_, src_, r_dst_ = bg_jobs.pop(0)
                        P.dma(POOL, dst_, src_, writes=[r_dst_])
                if g + 1 < NG:
                    if 1 <= jb <= TT:
                        stage_x(g + 1, jb - 1)
                    if 2 <= jb <= TT + 1:
                        stage_y(g + 1, jb - 2)
            for t in range(TT):
                ip = cnt["p3"] % 2
                cnt["p3"] += 1
                ir = cnt["xr"] % NXR
                cnt["xr"] += 1
                r0 = g * T + t * 128
                if g == 0 and t == 0:
                    P.dma(SP, xr[ir][:], x_src[r0:r0 + 128, :], writes=[r_xr[ir]])
                for half in range(2):
                    for j in range(NJ):
                        P.op(PE, lambda e, ip=ip, half=half, j=j, t=t: e.matmul(
                            p3[ip][:, half * 512:(half + 1) * 512], lhsT=hT[:, j, t * 128:(t + 1) * 128], rhs=wout[:, j, half * 512:(half + 1) * 512],
                            start=(j == 0), stop=(j == NJ - 1)),
                            reads=[r_hT, r_wo], writes=[r_p3[ip]], inc=(half == 1 and j == NJ - 1))
                nt_ = g * TT + t + 1
                if nt_ < NG * TT:
                    irn = cnt["xr"] % NXR
                    P.dma(SP, xr[irn][:], x_src[nt_ * 128:(nt_ + 1) * 128, :], writes=[r_xr[irn]])
                k = cnt["n"] % 4
                cnt["n"] += 1
                kt = cnt["tmp"] % 2
                cnt["tmp"] += 1
                P.op(ACT, lambda e, ip=ip, k=k: e.activation(out=junk[:], in_=p3[ip][:], func=AF.Square, accum_out=ssq[k][:]),
                     reads=[r_p3[ip]], writes=[r_junk, r_ssq[k]])
                norm_rstd(P, ssq[k][:], rstd[k][:], r_ssq[k], r_rstd[k], neg_half, r_c, 1, 1.0 / D)
                P.op(DVE, lambda e, ip=ip, k=k, kt=kt: e.scalar_tensor_tensor(out=tmp[kt][:], in0=p3[ip][:], scalar=rstd[k][:], in1=gpost[:], op0=ALU.mult, op1=ALU.mult),
                     reads=[r_p3[ip], r_rstd[k], r_c], writes=[r_tmp[kt]])
                P.op(POOL, lambda e, ir=ir, kt=kt: e.tensor_tensor(out=xr[ir][:], in0=xr[ir][:], in1=tmp[kt][:], op=ALU.add),
                     reads=[r_tmp[kt], r_xr[ir]], writes=[r_xr[ir]])
                P.dma(SP, x_dst[r0:r0 + 128, :], xr[ir][:], reads=[r_xr[ir]])


def common_inputs(inputs):
    common = {k: np.ascontiguousarray(inputs[k][0], dtype=np.float32) for k in WEIGHTS}
    for k in GAINS:
        common[k] = np.ascontiguousarray(inputs[k], dtype=np.float32).reshape(1, D)
    common["ident"] = np.eye(128, dtype=np.float32)
    common["cst"] = make_cst()
    common["rope"] = make_rope()
    common["nab"] = make_nab(np.asarray(inputs["na_rel_bias"])[0])
    common["dec"] = np.concatenate([np.asarray(inputs["ret_decay_fwd"], np.float32).reshape(4),
                                    np.asarray(inputs["ret_decay_bwd"], np.float32).reshape(4)]).reshape(1, 8)
    return common


def kernel(**inputs):
    x = np.ascontiguousarray(inputs["x"], dtype=np.float32)
    B = x.shape[0]
    assert B == N_CORES * SEQ_PER_CORE
    nc = build()
    common = common_inputs(inputs)
    in_maps = []
    for c in range(N_CORES):
        m = dict(common)
        m["x"] = x[c * SEQ_PER_CORE:(c + 1) * SEQ_PER_CORE].reshape(SEQ_PER_CORE * S, D)
        in_maps.append(m)
    res = run_bass_kernel_spmd(nc, in_maps, core_ids=list(range(N_CORES)))
    out = np.stack([r["y"].reshape(SEQ_PER_CORE, S, D) for r in res.results], axis=0)
    return out.reshape(B, S, D).astype(np.float32)


RET_SCALE = 128.0 ** -0.5
NCST = 898


def make_cst():
    c = np.zeros((128, NCST), np.float32)
    c[:, 0:128] = np.eye(128, dtype=np.float32)
    k = np.arange(128)[:, None]
    q = np.arange(128)[None, :]
    c[:, 128:256] = np.maximum(q - k, 0)
    c[:, 256:384] = np.maximum(k - q, 0)
    c[:, 384:512] = np.where(q >= k, RET_SCALE, 0.0)
    c[:, 512:640] = np.where(k > q, RET_SCALE, 0.0)
    c[:, 640:768] = np.broadcast_to(q + 1, (128, 128))
    c[:, 768:896] = np.broadcast_to(128 - q, (128, 128))
    c[:, 896] = 127 - np.arange(128)
    c[:, 897] = np.arange(128)
    return c


def make_rope():
    inv = (np.float32(1.0) / (np.float32(10000.0) ** np.linspace(0.0, 1.0, 64, dtype=np.float32))).astype(np.float32)
    pos = np.arange(S, dtype=np.float32)
    ang = (pos[:, None] * inv[None, :]).astype(np.float32)
    cos = np.cos(ang).astype(np.float32).reshape(NT, 128, 64).transpose(1, 0, 2)
    sin = np.sin(ang).astype(np.float32).reshape(NT, 128, 64).transpose(1, 0, 2)
    return np.ascontiguousarray(np.concatenate([cos.reshape(128, NT * 64), sin.reshape(128, NT * 64)], axis=1))


def na_offsets(a):
    rows = []
    for qr in (0, 1):
        rq = 2 * a + qr
        rs = min(max(rq - 4, 0), 24)
        rows += [rs, rs + 7]
    return list(range(min(rows) // 2 - a, max(rows) // 2 - a + 1))


def _na_variants():
    var_of, idxs, seen = {}, [], {}
    ck = np.arange(64)[:, None]
    cq = np.arange(64)[None, :]
    ws = np.clip(cq - 8, 0, 48)
    cvalid = (ck >= ws) & (ck < ws + 16)
    ci = np.clip(ck - cq + 15, 0, 30)
    for a in range(16):
        for o in na_offsets(a):
            idx = np.full((128, 128), -1, np.int64)
            for kr in (0, 1):
                for qr in (0, 1):
                    rk = 2 * (a + o) + kr
                    rq = 2 * a + qr
                    rs = min(max(rq - 4, 0), 24)
                    if not (rs <= rk < rs + 8):
                        continue
                    ri = rk - rq + 7
                    idx[kr * 64:(kr + 1) * 64, qr * 64:(qr + 1) * 64] = np.where(cvalid, ri * 31 + ci, -1)
            key = idx.tobytes()
            if key not in seen:
                seen[key] = len(idxs)
                idxs.append(idx)
            var_of[(a, o)] = seen[key]
    return var_of, idxs


NA_VAR_OF, NA_IDX = _na_variants()
NVAR = len(NA_IDX)
NA_NEG = -30000.0


def make_nab(rel_bias):
    rb = np.asarray(rel_bias, np.float32).reshape(8, 15 * 31)
    out = np.empty((NVAR, 128, 8, 128), np.float32)
    for v, idx in enumerate(NA_IDX):
        safe = np.where(idx >= 0, idx, 0)
        for h in range(8):
            out[v, :, h, :] = np.where(idx >= 0, rb[h][safe], np.float32(NA_NEG))
    return np.ascontiguousarray(out.reshape(NVAR * 128, 1024))


def bc_ap(base, dims):
    return bass.AP(base.tensor, base.offset, [list(base.ap[0])] + [list(d) for d in dims])


C_RQ, C_RK, C_RV, C_RG, C_NQ, C_NK, C_NV, C_GR, C_GN = 0, 512, 1024, 2048, 3072, 3584, 4096, 4608, 5632


_DBG = {'n': 0, 'max': 99}


def _dbg_run():
    _DBG['n'] += 1
    return _DBG['n'] <= _DBG['max']


def mix_phase(nc, P, x_src, x_dst, w_bf, r_wbf, g_pre, g_post, cst_in, rope_in, nab_in, tabs, sb_d, tag):
    MT, DFt, DBt, dtok, gC = tabs
    from contextlib import ExitStack
    wmi, r_wmi = w_bf["w_mix_in"], r_wbf["w_mix_in"]
    with ExitStack() as M:
        sbM = lambda name, shape, dt: M.enter_context(nc.sbuf_tensor(uname(tag + name), shape, dt))
        uT = sbM("uT", [128, 8, S], BF16)
        r_uT = Res("uT")
        VA = sbM("VA", [128, NT, 1024], BF16)
        r_VA = [Res("VA") for _ in range(NT)]
        onaT = sbM("onaT", [128, 4, S], BF16)
        r_onaT = Res("onaT")
        cst = sbM("cst", [128, NCST], F32)
        ident = sbM("ident", [128, 128], BF16)
        neg_half = sbM("nh", [128, 8], F32)
        r_c = Res("c")
        P.dma(SP, cst[:], cst_in[:, :], writes=[r_c])
        P.op(DVE, lambda e: e.tensor_copy(out=ident[:], in_=cst[:, 0:128]), reads=[r_c], writes=[r_c])
        P.op(DVE, lambda e: e.memset(neg_half[:], -0.5), writes=[r_c])

        def _subphase():
            with ExitStack() as st:
                sb = lambda name, shape, dt: st.enter_context(nc.sbuf_tensor(uname(tag + name), shape, dt))
                ps = lambda name, shape, dt: st.enter_context(nc.psum_tensor(uname(tag + name), shape, dt))
                gpre = sb("gpre", [128, D], F32)
                xn = [sb("xn%d" % i, [128, D], F32) for i in range(6)]
                r_xn = [Res("xn") for _ in range(6)]
                ub = [sb("ub%d" % i, [128, D], BF16) for i in range(2)]
                r_ub = [Res("ub") for _ in range(2)]
                junk = sb("junk", [128, D], BF16)
                r_junk = Res("junk")
                ssq = [sb("ssq%d" % i, [128, 1], F32) for i in range(4)]
                rstd = [sb("rstd%d" % i, [128, 1], F32) for i in range(4)]
                r_ssq = [Res("ssq") for _ in range(4)]
                r_rstd = [Res("rstd") for _ in range(4)]
                pT = [ps("pT%d" % i, [128, D], BF16) for i in range(2)]
                r_pT = [Res("pT") for _ in range(2)]
                P.dma(SP, gpre[:], bcast_rows(g_pre, D), writes=[r_c])
                def m1_x(t):
                    ix, k, kb = t % 6, t % 4, t % 2
                    P.op(ACT, lambda e, ix=ix, k=k: e.activation(out=junk[:], in_=xn[ix][:], func=AF.Square, accum_out=ssq[k][:]),
                         reads=[r_xn[ix]], writes=[r_junk, r_ssq[k]])
                    norm_rstd(P, ssq[k][:], rstd[k][:], r_ssq[k], r_rstd[k], neg_half, r_c, 1, 1.0 / D)
                    P.op(DVE, lambda e, ix=ix, k=k, kb=kb: e.scalar_tensor_tensor(out=ub[kb][:], in0=xn[ix][:], scalar=rstd[k][:], in1=gpre[:], op0=ALU.mult, op1=ALU.mult),
                         reads=[r_xn[ix], r_rstd[k], r_c], writes=[r_ub[kb]])

                def m1_y(t):
                    kb = t % 2
                    for kc in range(8):
                        P.op(PE, lambda e, kb=kb, kc=kc: e.transpose(out=pT[kb][:, kc * 128:(kc + 1) * 128], in_=ub[kb][:, kc * 128:(kc + 1) * 128], identity=ident[:]),
                             reads=[r_ub[kb], r_c], writes=[r_pT[kb]], inc=(kc == 7))
                    P.op(ACT, lambda e, t=t, kb=kb: e.activation(out=uT[:, :, t * 128:(t + 1) * 128], in_=pT[kb][:].rearrange("p (k c) -> p k c", k=8), func=AF.Copy),
                         reads=[r_pT[kb]], writes=[r_uT])

                for t in range(min(6, NT)):
                    P.dma(SP, xn[t % 6][:], x_src[t * 128:(t + 1) * 128, :], writes=[r_xn[t % 6]])
                m1_x(0)
                for t in range(NT):
                    if t + 1 < NT:
                        m1_x(t + 1)
                    m1_y(t)
                    if t + 6 < NT:
                        P.dma(SP, xn[t % 6][:], x_src[(t + 6) * 128:(t + 7) * 128, :], writes=[r_xn[t % 6]])
                P.barrier()

        if _dbg_run():
            _subphase()

        def _subphase():
            with ExitStack() as st:
                sb = lambda name, shape, dt: st.enter_context(nc.sbuf_tensor(uname(tag + name), shape, dt))
                ps = lambda name, shape, dt: st.enter_context(nc.psum_tensor(uname(tag + name), shape, dt))
                Ktm = sb("Ktm", [128, NT, 512], BF16)
                r_K = [Res("K") for _ in range(NT)]
                wk = sb("wk", [128, 8, 512], BF16)
                wv = sb("wv", [128, 8, 1024], BF16)
                r_wk, r_wv = Res("wk"), Res("wv")
                rope = sb("rope", [128, 2 * NT * 64], F32)
                Sf32 = sb("Sf32", [128, 1024], F32)
                Sb32 = sb("Sb32", [128, 1024], F32)
                Sfb = sb("Sfb", [128, 1024], BF16)
                Sbb = [sb("Sbb%d" % i, [128, 1024], BF16) for i in range(2)]
                r_Sf32, r_Sb32, r_Sfb = Res("Sf32"), Res("Sb32"), Res("Sfb")
                r_Sbb = [Res("Sbb") for _ in range(2)]
                r_sbd = [Res("sbd") for _ in range(NT)]
                tA = sb("tA", [128, 512], F32)
                tB = sb("tB", [128, 512], F32)
                r_tA, r_tB = Res("tA"), Res("tB")
                qrot = sb("qrot", [128, 512], BF16)
                r_qrot = Res("qrot")
                kdec = sb("kdec", [128, 512], BF16)
                r_kdec = Res("kdec")
                qT = sb("qT", [128, 512], BF16)
                qfT = sb("qfT", [128, 512], BF16)
                qbT = sb("qbT", [128, 512], BF16)
                kT = sb("kT", [128, 512], BF16)
                r_qT, r_qfT, r_qbT, r_kT = Res("qT"), Res("qfT"), Res("qbT"), Res("kT")
                Sm = sb("Sm", [128, 512], BF16)
                r_Sm = Res("Sm")
                srg = sb("srg", [128, 1024], F32)
                r_srg = Res("srg")
                Abuf = sb("A", [128, 1024], BF16)
                r_A = Res("A")
                junk = sb("junk", [128, 256], BF16)
                r_junk = Res("junk")
                ssq4 = sb("ssq4", [128, 4], F32)
                rstd4 = sb("rstd4", [128, 4], F32)
                r_ssq4, r_rstd4 = Res("ssq4"), Res("rstd4")
                b0 = ps("b0", [128, 512], F32)
                pR = ps("pR", [128, 1024], F32)
                pTr = ps("pTr", [128, 1024], BF16)
                pY = ps("pY", [128, 1024], F32)
                pU = ps("pU", [128, 1024], F32)
                r_b0, r_pR, r_pTr, r_pY, r_pU = Res("b0"), Res("pR"), Res("pTr"), Res("pY"), Res("pU")

                P.dma(SP, rope[:], rope_in[:, :], writes=[r_c])

                if _DBG.get('rstop') == 'setup':
                    P.barrier()
                    return

                def rotary(dst, n, r_dst):
                    cb = rope[:, n * 64:(n + 1) * 64]
                    sn = rope[:, NT * 64 + n * 64:NT * 64 + (n + 1) * 64]
                    cosb = bc_ap(cb, [[0, 4], [0, 2], [1, 64]])
                    sinb = bc_ap(sn, [[0, 4], [1, 64]])
                    v4 = b0[:].rearrange("p (h t d) -> p h t d", h=4, t=2)
                    a4 = tA[:].rearrange("p (h t d) -> p h t d", h=4, t=2)
                    b4 = tB[:].rearrange("p (h t d) -> p h t d", h=4, t=2)
                    d4 = dst.rearrange("p (h t d) -> p h t d", h=4, t=2)
                    P.op(DVE, lambda e: e.tensor_tensor(out=a4, in0=v4, in1=cosb, op=ALU.mult), reads=[r_b0, r_c], writes=[r_tA])
                    P.op(DVE, lambda e: e.tensor_tensor(out=b4[:, :, 0, :], in0=v4[:, :, 1, :], in1=sinb, op=ALU.mult), reads=[r_b0, r_c], writes=[r_tB])
                    P.op(DVE, lambda e: e.tensor_tensor(out=b4[:, :, 1, :], in0=v4[:, :, 0, :], in1=sinb, op=ALU.mult), reads=[r_b0, r_c], writes=[r_tB])
                    P.op(POOL, lambda e: e.tensor_tensor(out=d4[:, :, 0, :], in0=a4[:, :, 0, :], in1=b4[:, :, 0, :], op=ALU.subtract), reads=[r_tA, r_tB], writes=[r_dst])
                    P.op(POOL, lambda e: e.tensor_tensor(out=d4[:, :, 1, :], in0=a4[:, :, 1, :], in1=b4[:, :, 1, :], op=ALU.add), reads=[r_tA, r_tB], writes=[r_dst])

                def proj_tok(dst_ps, r_dst, wbuf, r_w, n, ncols):
                    for half in range(ncols // 512):
                        for kc in range(8):
                            P.op(PE, lambda e, half=half, kc=kc: e.matmul(dst_ps[:, half * 512:(half + 1) * 512], lhsT=uT[:, kc, n * 128:(n + 1) * 128],
                                                                          rhs=wbuf[:, kc, half * 512:(half + 1) * 512], start=(kc == 0), stop=(kc == 7)),
                                 reads=[r_uT, r_w], writes=[r_dst], inc=(kc == 7 and half == ncols // 512 - 1))

                def state_update(S32, r_S32, goff, kdec_col0):
                    pass

                P.dma(SP, wk[:], wmi[:, :, C_RK:C_RK + 512], reads=[r_wmi], writes=[r_wk])
                P.dma(SP, wv[:], wmi[:, :, C_RV:C_RV + 1024], reads=[r_wmi], writes=[r_wv])
                P.op(DVE, lambda e: e.memset(Sb32[:], 0.0), writes=[r_Sb32])
                for n in range(NT - 1, -1, -1):
                    proj_tok(b0, r_b0, wk, r_wk, n, 512)
                    rotary(Ktm[:, n, :], n, r_K[n])
                    proj_tok(pR, r_pR, wv, r_wv, n, 1024)
                    P.op(ACT, lambda e, n=n: e.activation(out=VA[:, n, :], in_=pR[:], func=AF.Copy), reads=[r_pR], writes=[r_VA[n]])
                    i = n % 2
                    P.op(ACT, lambda e, i=i: e.activation(out=Sbb[i][:], in_=Sb32[:], func=AF.Copy), reads=[r_Sb32], writes=[r_Sbb[i]])
                    P.dma(SP, sb_d[n * 128:(n + 1) * 128, :], Sbb[i][:], reads=[r_Sbb[i]], writes=[r_sbd[n]])
                    if n > 0:
                        P.op(DVE, lambda e, n=n: e.tensor_tensor(out=kdec[:].rearrange("p (h d) -> p h d", h=4), in0=Ktm[:, n, :].rearrange("p (h d) -> p h d", h=4),
                                                                 in1=dtok[:, 4:8].to_broadcast([128, 4, 128]), op=ALU.mult),
                             reads=[r_K[n], r_c], writes=[r_kdec])
                        for h in range(4):
                            P.op(PE, lambda e, h=h, n=n: e.matmul(pU[:, h * 256:(h + 1) * 256], lhsT=kdec[:, h * 128:(h + 1) * 128], rhs=VA[:, n, h * 256:(h + 1) * 256], start=True, stop=True),
                                 reads=[r_kdec, r_VA[n]], writes=[r_pU], inc=(h == 3))
                        for h in range(4):
                            P.op(DVE, lambda e, h=h: e.scalar_tensor_tensor(out=Sb32[:, h * 256:(h + 1) * 256], in0=Sb32[:, h * 256:(h + 1) * 256], scalar=gC[:, 4 + h:5 + h],
                                                                             in1=pU[:, h * 256:(h + 1) * 256], op0=ALU.mult, op1=ALU.add),
                                 reads=[r_pU, r_Sb32, r_c], writes=[r_Sb32])

                if _DBG.get('rstop') == 'pass1':
                    P.barrier()
                    return
                P.dma(SP, wk[:], wmi[:, :, C_RQ:C_RQ + 512], reads=[r_wmi], writes=[r_wk])
                P.dma(SP, wv[:], wmi[:, :, C_RG:C_RG + 1024], reads=[r_wmi], writes=[r_wv])
                P.op(DVE, lambda e: e.memset(Sf32[:], 0.0), writes=[r_Sf32])
                P.op(DVE, lambda e: e.memset(Sfb[:], 0.0), writes=[r_Sfb])
                qrot2 = [qrot, sb("qrot1", [128, 512], BF16)]
                r_qrot2 = [r_qrot, Res("qrot1")]
                srg2 = [srg, sb("srg1", [128, 1024], F32)]
                r_srg2 = [r_srg, Res("srg1")]
                kdec2 = [kdec, sb("kdec1", [128, 512], BF16)]
                r_kdec2 = [r_kdec, Res("kdec1")]
                pU0, pU1 = pU[:, 0:512], pU[:, 512:1024]
                r_pU0, r_pU1 = Res("pU0"), Res("pU1")

                def stage_a(n):
                    i = n % 2
                    P.dma(SP, Sbb[i][:], sb_d[n * 128:(n + 1) * 128, :], reads=[r_sbd[n]], writes=[r_Sbb[i]])
                    proj_tok(b0, r_b0, wk, r_wk, n, 512)
                    rotary(qrot2[i][:], n, r_qrot2[i])
                    proj_tok(pR, r_pR, wv, r_wv, n, 1024)
                    P.op(ACT, lambda e, i=i: e.activation(out=srg2[i][:], in_=pR[:], func=AF.Silu), reads=[r_pR], writes=[r_srg2[i]])
                    P.op(DVE, lambda e, n=n, i=i: e.tensor_tensor(out=kdec2[i][:].rearrange("p (h d) -> p h d", h=4), in0=Ktm[:, n, :].rearrange("p (h d) -> p h d", h=4),
                                                                 in1=dtok[:, 0:4].to_broadcast([128, 4, 128]), op=ALU.mult),
                         reads=[r_K[n], r_c], writes=[r_kdec2[i]])

                def stage_t(n):
                    i = n % 2
                    for h in range(4):
                        P.op(PE, lambda e, h=h, i=i: e.transpose(out=pTr[:, h * 128:(h + 1) * 128], in_=qrot2[i][:, h * 128:(h + 1) * 128], identity=ident[:]),
                             reads=[r_qrot2[i], r_c], writes=[r_pTr], inc=False)
                    for h in range(4):
                        P.op(PE, lambda e, h=h, n=n: e.transpose(out=pTr[:, 512 + h * 128:512 + (h + 1) * 128], in_=Ktm[:, n, h * 128:(h + 1) * 128], identity=ident[:]),
                             reads=[r_K[n], r_c], writes=[r_pTr], inc=(h == 3))
                    P.op(ACT, lambda e: e.activation(out=qT[:], in_=pTr[:, 0:512], func=AF.Copy), reads=[r_pTr], writes=[r_qT])
                    P.op(ACT, lambda e: e.activation(out=kT[:], in_=pTr[:, 512:1024], func=AF.Copy), reads=[r_pTr], writes=[r_kT])
                    P.op(DVE, lambda e: e.tensor_tensor(out=qfT[:], in0=qT[:], in1=DFt[:], op=ALU.mult), reads=[r_qT, r_c], writes=[r_qfT])
                    P.op(DVE, lambda e: e.tensor_tensor(out=qbT[:], in0=qT[:], in1=DBt[:], op=ALU.mult), reads=[r_qT, r_c], writes=[r_qbT])

                def stage_s(n):
                    for h in range(4):
                        P.op(PE, lambda e, h=h: e.matmul(pU1[:, h * 128:(h + 1) * 128], lhsT=kT[:, h * 128:(h + 1) * 128], rhs=qT[:, h * 128:(h + 1) * 128], start=True, stop=True),
                             reads=[r_kT, r_qT], writes=[r_pU1], inc=(h == 3))
                    P.op(DVE, lambda e: e.tensor_tensor(out=Sm[:], in0=pU1, in1=MT[:], op=ALU.mult), reads=[r_pU1, r_c], writes=[r_Sm])

                def stage_u(n, half):
                    i = n % 2
                    for hh in range(2):
                        h = half * 2 + hh
                        P.op(PE, lambda e, h=h, hh=hh, n=n, i=i: e.matmul(pU0[:, hh * 256:(hh + 1) * 256], lhsT=kdec2[i][:, h * 128:(h + 1) * 128], rhs=VA[:, n, h * 256:(h + 1) * 256], start=True, stop=True),
                             reads=[r_kdec2[i], r_VA[n]], writes=[r_pU0], inc=(hh == 1))
                    for hh in range(2):
                        h = half * 2 + hh
                        P.op(DVE, lambda e, h=h, hh=hh: e.scalar_tensor_tensor(out=Sf32[:, h * 256:(h + 1) * 256], in0=Sf32[:, h * 256:(h + 1) * 256], scalar=gC[:, h:h + 1],
                                                                              in1=pU0[:, hh * 256:(hh + 1) * 256], op0=ALU.mult, op1=ALU.add),
                             reads=[r_pU0, r_Sf32, r_c], writes=[r_Sf32])

                def stage_y(n):
                    i = n % 2
                    for h in range(4):
                        vs = slice(h * 256, (h + 1) * 256)
                        hs = slice(h * 128, (h + 1) * 128)
                        P.op(PE, lambda e, vs=vs, hs=hs, n=n: e.matmul(pY[:, vs], lhsT=Sm[:, hs], rhs=VA[:, n, vs], start=True, stop=False),
                             reads=[r_Sm, r_VA[n]], writes=[r_pY], inc=False)
                        P.op(PE, lambda e, vs=vs, hs=hs: e.matmul(pY[:, vs], lhsT=qfT[:, hs], rhs=Sfb[:, vs], start=False, stop=False),
                             reads=[r_qfT, r_Sfb], writes=[r_pY], inc=False)
                        P.op(PE, lambda e, vs=vs, hs=hs, i=i: e.matmul(pY[:, vs], lhsT=qbT[:, hs], rhs=Sbb[i][:, vs], start=False, stop=True),
                             reads=[r_qbT, r_Sbb[i]], writes=[r_pY], inc=(h == 3))

                def stage_c(n):
                    i = n % 2
                    for h in range(4):
                        P.op(ACT, lambda e, h=h: e.activation(out=junk[:], in_=pY[:, h * 256:(h + 1) * 256], func=AF.Square, accum_out=ssq4[:, h:h + 1]),
                             reads=[r_pY], writes=[r_junk, r_ssq4])
                    norm_rstd(P, ssq4[:], rstd4[:], r_ssq4, r_rstd4, neg_half, r_c, 4, 1.0 / 256)
                    for h in range(4):
                        P.op(DVE, lambda e, h=h, i=i: e.scalar_tensor_tensor(out=Abuf[:, h * 256:(h + 1) * 256], in0=pY[:, h * 256:(h + 1) * 256], scalar=rstd4[:, h:h + 1],
                                                                            in1=srg2[i][:, h * 256:(h + 1) * 256], op0=ALU.mult, op1=ALU.mult),
                             reads=[r_pY, r_rstd4, r_srg2[i]], writes=[r_A])
                    P.op(ACT, lambda e: e.activation(out=Sfb[:], in_=Sf32[:], func=AF.Copy), reads=[r_Sf32], writes=[r_Sfb])

                def stage_at(n):
                    for kc in range(8):
                        P.op(PE, lambda e, kc=kc: e.transpose(out=pTr[:, kc * 128:(kc + 1) * 128], in_=Abuf[:, kc * 128:(kc + 1) * 128], identity=ident[:]),
                             reads=[r_A, r_c], writes=[r_pTr], inc=(kc == 7))
                    P.op(ACT, lambda e, n=n: e.activation(out=VA[:, n, :], in_=pTr[:], func=AF.Copy), reads=[r_pTr], writes=[r_VA[n]])

                stage_a(0)
                for n in range(NT):
                    stage_t(n)
                    if n + 1 < NT:
                        stage_a(n + 1)
                    if n >= 1:
                        stage_at(n - 1)
                    stage_s(n)
                    stage_u(n, 0)
                    stage_y(n)
                    stage_u(n, 1)
                    stage_c(n)
                stage_at(NT - 1)
                P.barrier()

        if _dbg_run():
            _subphase()

        def _subphase():
            with ExitStack() as st:
                sb = lambda name, shape, dt: st.enter_context(nc.sbuf_tensor(uname(tag + name), shape, dt))
                ps = lambda name, shape, dt: st.enter_context(nc.psum_tensor(uname(tag + name), shape, dt))
                nqT = sb("nqT", [128, 4, S], BF16)
                nkT = sb("nkT", [128, 4, S], BF16)
                nv = sb("nv", [128, NT, 8, 65], BF16)
                r_nq, r_nk, r_nv = Res("nq"), Res("nk"), Res("nv")
                Bt = sb("Bt", [128, NVAR, 1024], BF16)
                r_Bt = Res("Bt")
                wa = [sb("wa%d" % i, [128, 8, 512], BF16) for i in range(3)]
                r_wa = [Res("wa") for _ in range(3)]
                PT = [sb("PT%d" % i, [128, 1024], BF16) for i in range(2)]
                r_PT = [Res("PT") for _ in range(2)]
                ona = sb("ona", [128, 512], BF16)
                r_ona = Res("ona")
                rden = sb("rden", [128, 8], F32)
                r_rden = Res("rden")
                pS = [ps("pS%d" % i, [128, 1024], F32) for i in range(2)]
                r_pS = [Res("pS") for _ in range(2)]
                pO = ps("pO", [128, 2, 512], F32)
                r_pO = Res("pO")
                pTn = ps("pTn", [128, 1024], BF16)
                r_pTn = Res("pTn")

                for j, c0 in enumerate((C_NQ, C_NK, C_NV)):
                    P.dma(SP, wa[j][:], wmi[:, :, c0:c0 + 512], reads=[r_wmi], writes=[r_wa[j]])
                nabbf_ap, r_nabbf = nab_in
                P.dma(SP, Bt[:, 0:5, :], nabbf_ap[:, 0:5, :], reads=[r_nabbf], writes=[r_Bt])
                P.dma(SP, Bt[:, 5:NVAR, :], nabbf_ap[:, 5:NVAR, :], reads=[r_nabbf], writes=[r_Bt])
                P.op(DVE, lambda e: e.memset(nv[:], 1.0), writes=[r_nv])
                cnt = 0
                for j, (dst, r_dst, scale) in enumerate(((nqT, r_nq, 0.125), (nkT, r_nk, 1.0))):
                    for c in range(4):
                        for tb in range(4):
                            i = cnt % 2
                            cnt += 1
                            for kc in range(8):
                                P.op(PE, lambda e, i=i, j=j, c=c, kc=kc, tb=tb: e.matmul(pS[i][:, 0:512], lhsT=wa[j][:, kc, c * 128:(c + 1) * 128], rhs=uT[:, kc, tb * 512:(tb + 1) * 512],
                                                                                          start=(kc == 0), stop=(kc == 7)),
                                     reads=[r_wa[j], r_uT], writes=[r_pS[i]], inc=(kc == 7))
                            P.op(ACT, lambda e, i=i, dst=dst, c=c, tb=tb, scale=scale: e.activation(out=dst[:, c, tb * 512:(tb + 1) * 512], in_=pS[i][:, 0:512], func=AF.Copy, scale=scale),
                                 reads=[r_pS[i]], writes=[r_dst])
                for t in range(NT):
                    i = cnt % 2
                    cnt += 1
                    for kc in range(8):
                        P.op(PE, lambda e, i=i, kc=kc, t=t: e.matmul(pS[i][:, 0:512], lhsT=uT[:, kc, t * 128:(t + 1) * 128], rhs=wa[2][:, kc, :], start=(kc == 0), stop=(kc == 7)),
                             reads=[r_wa[2], r_uT], writes=[r_pS[i]], inc=(kc == 7))
                    P.op(ACT, lambda e, i=i, t=t: e.activation(out=nv[:, t, :, 0:64], in_=pS[i][:, 0:512].rearrange("p (h d) -> p h d", h=8), func=AF.Copy),
                         reads=[r_pS[i]], writes=[r_nv])
                nqM = [[sb("nqM%d_%d" % (s_, i), [128, 4, 128], BF16) for i in range(2)] for s_ in range(2)]
                r_nqM = [[Res("nqM") for _ in range(2)] for _ in range(2)]
                ona2 = [ona, sb("ona1", [128, 512], BF16)]
                r_ona2 = [r_ona, Res("ona1")]
                for s_ in range(2):
                    for i in range(2):
                        P.op(DVE, lambda e, s_=s_, i=i: e.memset(nqM[s_][i][:], 0.0), writes=[r_nqM[s_][i]])
                units = [(a, oi, o, len(na_offsets(a))) for a in range(NT) for oi, o in enumerate(na_offsets(a))]

                def na_scores(u):
                    a, oi, o, _ = units[u]
                    sa = a % 2
                    if oi == 0:
                        P.op(ACT, lambda e, a=a, sa=sa: e.activation(out=nqM[sa][0][0:64, :, :], in_=nqT[0:64, :, a * 128:(a + 1) * 128], func=AF.Copy),
                             reads=[r_nq], writes=[r_nqM[sa][0]])
                        P.op(ACT, lambda e, a=a, sa=sa: e.activation(out=nqM[sa][1][64:128, :, :], in_=nqT[64:128, :, a * 128:(a + 1) * 128], func=AF.Copy),
                             reads=[r_nq], writes=[r_nqM[sa][1]])
                    kt = a + o
                    var = NA_VAR_OF[(a, o)]
                    i = u % 2
                    for bank in range(2):
                        P.op(PE, lambda e, i=i, bank=bank, var=var: e.matmul(pS[i][:, bank * 512:(bank + 1) * 512], lhsT=ident[:], rhs=Bt[:, var, bank * 512:(bank + 1) * 512], start=True, stop=False),
                             reads=[r_Bt, r_c], writes=[r_pS[i]], inc=False)
                        for hh in range(4):
                            h = bank * 4 + hh
                            c = h // 2
                            P.op(PE, lambda e, i=i, h=h, c=c, kt=kt, sa=sa, hh=hh: e.matmul(
                                pS[i][:, h * 128:(h + 1) * 128], lhsT=nkT[:, c, kt * 128:(kt + 1) * 128], rhs=nqM[sa][h % 2][:, c, :],
                                start=False, stop=(hh == 3)),
                                reads=[r_nk, r_nqM[sa][h % 2]], writes=[r_pS[i]], inc=(bank == 1 and hh == 3))

                def na_pv(u):
                    a, oi, o, no = units[u]
                    kt = a + o
                    i = u % 2
                    P.op(ACT, lambda e, i=i: e.activation(out=PT[i][:], in_=pS[i][:], func=AF.Exp), reads=[r_pS[i]], writes=[r_PT[i]])
                    for h in range(8):
                        c0 = (h % 4) * 65
                        P.op(PE, lambda e, i=i, h=h, c0=c0, kt=kt, oi=oi, last=(oi == no - 1): e.matmul(
                            pO[:, h // 4, c0:c0 + 65], lhsT=PT[i][:, h * 128:(h + 1) * 128], rhs=nv[:, kt, h, :],
                            start=(oi == 0 and h % 4 == 0), stop=last),
                            reads=[r_PT[i], r_nv], writes=[r_pO], inc=(h == 7))

                def na_fin_dve(a):
                    sa = a % 2
                    po4 = pO[:, :, 0:260].rearrange("p b (h e) -> p b h e", e=65)
                    P.op(DVE, lambda e, po4=po4: e.reciprocal(out=rden[:].rearrange("p (b h e) -> p b h e", b=2, e=1), in_=po4[:, :, :, 64:65]),
                         reads=[r_pO], writes=[r_rden])
                    P.op(DVE, lambda e, po4=po4, sa=sa: e.tensor_tensor(out=ona2[sa][:].rearrange("p (b h d) -> p b h d", b=2, d=64), in0=po4[:, :, :, 0:64],
                                                                       in1=bc_ap(rden[:], [[4, 2], [1, 4], [0, 64]]), op=ALU.mult),
                         reads=[r_pO, r_rden], writes=[r_ona2[sa]])

                def na_fin_pe(a):
                    sa = a % 2
                    for c in range(4):
                        P.op(PE, lambda e, c=c, sa=sa: e.transpose(out=pTn[:, c * 128:(c + 1) * 128], in_=ona2[sa][:, c * 128:(c + 1) * 128], identity=ident[:]),
                             reads=[r_ona2[sa], r_c], writes=[r_pTn], inc=(c == 3))
                    P.op(ACT, lambda e, a=a: e.activation(out=onaT[:, :, a * 128:(a + 1) * 128], in_=pTn[:, 0:512].rearrange("p (k c) -> p k c", k=4), func=AF.Copy),
                         reads=[r_pTn], writes=[r_onaT])

                na_scores(0)
                pend = None
                for u in range(len(units)):
                    if u + 1 < len(units):
                        na_scores(u + 1)
                    na_pv(u)
                    if pend is not None:
                        na_fin_pe(pend)
                        pend = None
                    a, oi, o, no = units[u]
                    if oi == no - 1:
                        na_fin_dve(a)
                        pend = a
                na_fin_pe(pend)
                P.barrier()

        if _dbg_run():
            _subphase()

        def _subphase():
            with ExitStack() as st:
                sb = lambda name, shape, dt: st.enter_context(nc.sbuf_tensor(uname(tag + name), shape, dt))
                ps = lambda name, shape, dt: st.enter_context(nc.psum_tensor(uname(tag + name), shape, dt))
                wro = sb("wro", [128, 8, D], BF16)
                wno = sb("wno", [128, 4, D], BF16)
                wgr = sb("wgr", [128, 8, D], BF16)
                wgn = sb("wgn", [128, 8, D], BF16)
                wmo = sb("wmo", [128, 8, D], BF16)
                r_w = Res("w")
                gpost = sb("gpost", [128, D], F32)
                mT = sb("mT", [128, 8, 512], BF16)
                r_mT = Res("mT")
                t1 = [sb("t1_%d" % i, [128, 512], F32) for i in range(2)]
                t2 = [sb("t2_%d" % i, [128, 512], F32) for i in range(2)]
                m1 = [sb("m1_0", [128, 512], F32)] * 2
                m2 = [sb("m2_0", [128, 512], F32)] * 2
                r_t1 = [Res("t1") for _ in range(2)]
                r_t2 = [Res("t2") for _ in range(2)]
                r_m1 = [Res("m1")] * 2
                r_m2 = [Res("m2")] * 2
                tmp = [sb("tmp%d" % i, [128, D], F32) for i in range(2)]
                r_tmp = [Res("tmp") for _ in range(2)]
                xr = [sb("xr%d" % i, [128, D], F32) for i in range(2)]
                r_xr = [Res("xr") for _ in range(2)]
                junk = sb("junk", [128, D], BF16)
                r_junk = Res("junk")
                ssq = [sb("ssq%d" % i, [128, 1], F32) for i in range(2)]
                rstd = [sb("rstd%d" % i, [128, 1], F32) for i in range(2)]
                r_ssq = [Res("ssq") for _ in range(2)]
                r_rstd = [Res("rstd") for _ in range(2)]
                pA = [ps("pA%d" % i, [128, 512], F32) for i in range(4)]
                r_pA = [Res("pA") for _ in range(4)]
                pM = [ps("pM%d" % i, [128, D], F32) for i in range(2)]
                r_pM = [Res("pM") for _ in range(2)]

                P.dma(SP, gpost[:], bcast_rows(g_post, D), writes=[r_c])
                P.op(DVE, lambda e: e.tensor_scalar(out=gpost[:], in0=gpost[:], scalar1=0.5, scalar2=None, op0=ALU.mult), reads=[r_c], writes=[r_c])
                r_wro = [Res("wro") for _ in range(8)]
                r_wgr = [Res("wgr") for _ in range(8)]
                r_wno = [Res("wno") for _ in range(8)]
                r_wgn = [Res("wgn") for _ in range(8)]
                r_wmo = [Res("wmo") for _ in range(2)]
                for c in range(8):
                    cs = slice(c * 128, (c + 1) * 128)
                    P.dma(SP, wro[:, :, cs], w_bf["w_ret_out"][:, :, cs], reads=[r_wbf["w_ret_out"]], writes=[r_wro[c]])
                    P.dma(SP, wgr[:, :, cs], wmi[:, :, C_GR + c * 128:C_GR + (c + 1) * 128], reads=[r_wmi], writes=[r_wgr[c]])
                    P.dma(SP, wno[:, :, cs], w_bf["w_na_out"][:, :, cs], reads=[r_wbf["w_na_out"]], writes=[r_wno[c]])
                    P.dma(SP, wgn[:, :, cs], wmi[:, :, C_GN + c * 128:C_GN + (c + 1) * 128], reads=[r_wmi], writes=[r_wgn[c]])
                for half in range(2):
                    hs_ = slice(half * 512, (half + 1) * 512)
                    P.dma(SP, wmo[:, :, hs_], w_bf["w_mix_out"][:, :, hs_], reads=[r_wbf["w_mix_out"]], writes=[r_wmo[half]])
                k2 = 0
                for tb in range(4):
                    tsl = slice(tb * 512, (tb + 1) * 512)
                    for c in range(8):
                        cs = slice(c * 128, (c + 1) * 128)
                        b = k2 % 2
                        k2 += 1
                        for kc in range(8):
                            P.op(PE, lambda e, kc=kc, cs=cs, tsl=tsl: e.matmul(pA[1][:], lhsT=wgr[:, kc, cs], rhs=uT[:, kc, tsl], start=(kc == 0), stop=(kc == 7)),
                                 reads=[r_wgr[c], r_uT], writes=[r_pA[1]], inc=(kc == 7))
                        for kc in range(8):
                            P.op(PE, lambda e, kc=kc, cs=cs, tsl=tsl: e.matmul(pA[3][:], lhsT=wgn[:, kc, cs], rhs=uT[:, kc, tsl], start=(kc == 0), stop=(kc == 7)),
                                 reads=[r_wgn[c], r_uT], writes=[r_pA[3]], inc=(kc == 7))
                        for kc in range(8):
                            P.op(PE, lambda e, kc=kc, cs=cs, tb=tb: e.matmul(pA[0][:], lhsT=wro[:, kc, cs], rhs=VA[:, tb * 4:(tb + 1) * 4, kc * 128:(kc + 1) * 128],
                                                                            start=(kc == 0), stop=(kc == 7)),
                                 reads=[r_wro[c]] + r_VA[tb * 4:(tb + 1) * 4], writes=[r_pA[0]], inc=(kc == 7))
                        for kc in range(4):
                            P.op(PE, lambda e, kc=kc, cs=cs, tsl=tsl: e.matmul(pA[2][:], lhsT=wno[:, kc, cs], rhs=onaT[:, kc, tsl], start=(kc == 0), stop=(kc == 3)),
                                 reads=[r_wno[c], r_onaT], writes=[r_pA[2]], inc=(kc == 3))
                        P.op(ACT, lambda e, b=b: e.activation(out=t1[b][:], in_=pA[1][:], func=AF.Tanh, scale=0.5), reads=[r_pA[1]], writes=[r_t1[b]])
                        P.op(ACT, lambda e, b=b: e.activation(out=t2[b][:], in_=pA[3][:], func=AF.Tanh, scale=0.5), reads=[r_pA[3]], writes=[r_t2[b]])
                        P.op(DVE, lambda e, b=b: e.scalar_tensor_tensor(out=m1[b][:], in0=t1[b][:], scalar=1.0, in1=pA[0][:], op0=ALU.add, op1=ALU.mult),
                             reads=[r_t1[b], r_pA[0]], writes=[r_m1[b]])
                        P.op(DVE, lambda e, b=b: e.scalar_tensor_tensor(out=m2[b][:], in0=t2[b][:], scalar=1.0, in1=pA[2][:], op0=ALU.add, op1=ALU.mult),
                             reads=[r_t2[b], r_pA[2]], writes=[r_m2[b]])
                        P.op(POOL, lambda e, b=b, c=c: e.tensor_tensor(out=mT[:, c, :], in0=m1[b][:], in1=m2[b][:], op=ALU.add),
                             reads=[r_m1[b], r_m2[b]], writes=[r_mT])
                    for tt in range(4):
                        t = tb * 4 + tt
                        ip = t % 2
                        if t == 0:
                            P.dma(SP, xr[ip][:], x_src[t * 128:(t + 1) * 128, :], writes=[r_xr[ip]])
                        for half in range(2):
                            for kc in range(8):
                                P.op(PE, lambda e, ip=ip, half=half, kc=kc, tt=tt: e.matmul(pM[ip][:, half * 512:(half + 1) * 512], lhsT=mT[:, kc, tt * 128:(tt + 1) * 128],
                                                                                            rhs=wmo[:, kc, half * 512:(half + 1) * 512], start=(kc == 0), stop=(kc == 7)),
                                     reads=[r_mT, r_wmo[half]], writes=[r_pM[ip]], inc=(half == 1 and kc == 7))
                        if t + 1 < NT:
                            P.dma(SP, xr[1 - ip][:], x_src[(t + 1) * 128:(t + 2) * 128, :], writes=[r_xr[1 - ip]])
                        P.op(ACT, lambda e, ip=ip: e.activation(out=junk[:], in_=pM[ip][:], func=AF.Square, accum_out=ssq[ip][:]),
                             reads=[r_pM[ip]], writes=[r_junk, r_ssq[ip]])
                        norm_rstd(P, ssq[ip][:], rstd[ip][:], r_ssq[ip], r_rstd[ip], neg_half, r_c, 1, 0.25 / D)
                        P.op(DVE, lambda e, ip=ip: e.scalar_tensor_tensor(out=tmp[ip][:], in0=pM[ip][:], scalar=rstd[ip][:], in1=gpost[:], op0=ALU.mult, op1=ALU.mult),
                             reads=[r_pM[ip], r_rstd[ip], r_c], writes=[r_tmp[ip]])
                        P.op(POOL, lambda e, ip=ip: e.tensor_tensor(out=xr[ip][:], in0=xr[ip][:], in1=tmp[ip][:], op=ALU.add),
                             reads=[r_tmp[ip], r_xr[ip]], writes=[r_xr[ip]])
                        P.dma(SP, x_dst[t * 128:(t + 1) * 128, :], xr[ip][:], reads=[r_xr[ip]])
                P.barrier()

        if _dbg_run():
            _subphase()
```

```python
import math
import numpy as np
import concourse.bass as bass
import concourse.mybir as mybir
from concourse.bass_utils import run_bass_kernel_spmd

F32 = mybir.dt.float32
BF16 = mybir.dt.bfloat16
AF = mybir.ActivationFunctionType
ALU = mybir.AluOpType

D = 1024
S = 2048
NT = S // 128
DFF = 2816
NJ = DFF // 128
EPS = 1e-6
MIXW = 6656
N_CORES = 8
SEQ_PER_CORE = 4

PE, ACT, DVE, POOL, SP = "pe", "act", "dve", "pool", "sp"
ENGS = (PE, ACT, DVE, POOL, SP)


class Res:
    __slots__ = ("name", "w", "r")

    def __init__(self, name):
        self.name = name
        self.w = None
        self.r = {}


class Prog:
    def __init__(self, nc, n_dma_sems=48):
        self.nc = nc
        self.eng = {PE: nc.tensor, ACT: nc.scalar, DVE: nc.vector, POOL: nc.gpsimd, SP: nc.sync}
        self.streams = {e: [] for e in ENGS}
        self.cnt = {e: 0 for e in ENGS}
        self.waited = {e: {} for e in ENGS}
        self.sems = {}
        self.n_dma_sems = n_dma_sems
        self.dma_tot = [0] * n_dma_sems
        self.dma_rr = 0
        self.dma_rr_q = {}
        self.n_ops = 0

    def alloc_sems(self, stack):
        for e in (PE, ACT, DVE, POOL):
            self.sems[e] = stack.enter_context(self.nc.semaphore("s_" + e))
        for i in range(self.n_dma_sems):
            self.sems[("d", i)] = stack.enter_context(self.nc.semaphore("s_d%d" % i))

    def _need(self, e, key, val, waits):
        if key == e and e == PE:
            return
        if self.waited[e].get(key, 0) >= val:
            return
        self.waited[e][key] = val
        waits.append((key, val))

    def _deps(self, e, reads, writes):
        waits = []
        for r in reads:
            if r.w is not None:
                self._need(e, r.w[0], r.w[1], waits)
        for w in writes:
            if w.w is not None:
                self._need(e, w.w[0], w.w[1], waits)
            for k, v in w.r.items():
                self._need(e, k, v, waits)
        return waits

    def _mark(self, key, val, reads, writes):
        for r in reads:
            if r.r.get(key, 0) < val:
                r.r[key] = val
        for w in writes:
            w.w = (key, val)
            w.r = {}

    def op(self, e, fn, reads=(), writes=(), inc=True):
        waits = self._deps(e, reads, writes)
        val = self.cnt[e] + 1
        if inc:
            self.cnt[e] = val
        self.streams[e].append((waits, fn, (e, 1) if inc else None))
        self._mark(e, val, reads, writes)
        self.n_ops += 1

    def dma(self, q, out_ap, in_ap, reads=(), writes=()):
        if q == SP:
            lo, hi = 0, self.n_dma_sems - 24
        else:
            lo, hi = self.n_dma_sems - 24, self.n_dma_sems
        rr = self.dma_rr_q.get(q, lo)
        i = rr
        self.dma_rr_q[q] = lo + (rr + 1 - lo) % (hi - lo)
        key = ("d", i)
        waits = self._deps(q, reads, writes)
        if self.dma_tot[i] > 0:
            self._need(q, key, self.dma_tot[i], waits)
        self.dma_tot[i] += 16
        val = self.dma_tot[i]
        self.streams[q].append((waits, lambda eng: eng.dma_start(out=out_ap, in_=in_ap), (key, 16)))
        self._mark(key, val, reads, writes)
        self.n_ops += 1

    def barrier(self):
        for e in ENGS:
            waits = []
            for k in (PE, ACT, DVE, POOL):
                if self.cnt[k] > 0:
                    self._need(e, k, self.cnt[k], waits)
            for i in range(self.n_dma_sems):
                if self.dma_tot[i] > 0:
                    self._need(e, ("d", i), self.dma_tot[i], waits)
            if waits:
                self.streams[e].append((waits, None, None))

    def check_deadlock(self):
        val = {}
        pos = {e: 0 for e in ENGS}
        progress = True
        while progress:
            progress = False
            for e in ENGS:
                st = self.streams[e]
                while pos[e] < len(st):
                    waits, fn, inc = st[pos[e]]
                    if any(val.get(k, 0) < v for k, v in waits):
                        break
                    if inc is not None:
                        val[inc[0]] = val.get(inc[0], 0) + inc[1]
                    pos[e] += 1
                    progress = True
        stuck = {e: (pos[e], len(self.streams[e]), [(k, v, val.get(k, 0)) for k, v in self.streams[e][pos[e]][0] if val.get(k, 0) < v])
                 for e in ENGS if pos[e] < len(self.streams[e])}
        if stuck:
            raise RuntimeError("deadlock in emitted program: %r" % (stuck,))

    def emit(self, block):
        prog = self
        self.check_deadlock()

        def run(e):
            def body(eng):
                for waits, fn, inc in prog.streams[e]:
                    for key, val in waits:
                        eng.wait_ge(prog.sems[key], val)
                    if fn is not None:
                        ins = fn(eng)
                        if inc is not None:
                            ins.then_inc(prog.sems[inc[0]], inc[1])
            return body

        block.tensor(run(PE))
        block.scalar(run(ACT))
        block.vector(run(DVE))
        block.gpsimd(run(POOL))
        block.sync(run(SP))


WEIGHTS = {
    "ffn1_w_in": (D, 2 * DFF), "ffn1_w_out": (DFF, D),
    "w_mix_in": (D, MIXW), "w_ret_out": (D, D), "w_na_out": (512, D), "w_mix_out": (D, D),
    "ffn2_w_in": (D, 2 * DFF), "ffn2_w_out": (DFF, D),
}
GAINS = ["ffn1_pre_norm", "ffn1_post_norm", "mix_pre_norm", "mix_post_norm", "ffn2_pre_norm", "ffn2_post_norm"]


_UNAME = [0]


def uname(base):
    _UNAME[0] += 1
    return "%s_%d" % (base, _UNAME[0])


def bcast_rows(ap_1xn, n):
    return bass.AP(ap_1xn.tensor, ap_1xn.offset, [[0, 128], [1, n]])


def build(n_seq=SEQ_PER_CORE, stop_after="C", t_ffn=1024):
    from contextlib import ExitStack
    nc = bass.Bass("TRN2", target_bir_lowering=False)
    ntok = n_seq * S
    x_in = nc.dram_tensor("x", [ntok, D], F32, kind="ExternalInput").ap()
    y_out = nc.dram_tensor("y", [ntok, D], F32, kind="ExternalOutput").ap()
    w_in = {k: nc.dram_tensor(k, list(v), F32, kind="ExternalInput").ap() for k, v in WEIGHTS.items()}
    g_in = {k: nc.dram_tensor(k, [1, D], F32, kind="ExternalInput").ap() for k in GAINS}
    ident_in = nc.dram_tensor("ident", [128, 128], F32, kind="ExternalInput").ap()
    cst_in = nc.dram_tensor("cst", [128, NCST], F32, kind="ExternalInput").ap()
    rope_in = nc.dram_tensor("rope", [128, 2 * NT * 64], F32, kind="ExternalInput").ap()
    nab_in = nc.dram_tensor("nab", [NVAR * 128, 1024], F32, kind="ExternalInput").ap()
    dec_in = nc.dram_tensor("dec", [1, 8], F32, kind="ExternalInput").ap()
    sb_d = nc.dram_tensor("sb_scr", [NT * 128, 1024], BF16, kind="Internal").ap()
    nab_bf = nc.dram_tensor("nab_bf", [128, NVAR, 1024], BF16, kind="Internal").ap()
    w_bf = {k: nc.dram_tensor(k + "_bf", [128, v[0] // 128, v[1]], BF16, kind="Internal").ap()
            for k, v in WEIGHTS.items()}
    x1_d = nc.dram_tensor("x1_scr", [ntok, D], F32, kind="Internal").ap()
    x2_d = nc.dram_tensor("x2_scr", [ntok, D], F32, kind="Internal").ap()

    P = Prog(nc)
    with ExitStack() as top:
        P.alloc_sems(top)
        r_wbf = {k: Res("wbf_" + k) for k in WEIGHTS}
        r_nabbf = Res("nabbf")

        def cast_jobs(name, src3, dst3, r_dst, nkc, N, nk_step):
            jobs = []
            for k0 in range(0, nkc, nk_step):
                k1 = min(nkc, k0 + nk_step)
                for c0 in range(0, N, 2048):
                    c1 = min(N, c0 + 2048)
                    jobs.append((dst3[:, k0:k1, c0:c1], src3[:, k0:k1, c0:c1], r_dst))
            return jobs

        def wsrc(name):
            return w_in[name].rearrange("(kc p) n -> p kc n", p=128)

        front = cast_jobs("ffn1_w_in", wsrc("ffn1_w_in"), w_bf["ffn1_w_in"], r_wbf["ffn1_w_in"], 8, 2 * DFF, 1)
        front += cast_jobs("ffn1_w_out", wsrc("ffn1_w_out"), w_bf["ffn1_w_out"], r_wbf["ffn1_w_out"], NJ, D, 2)
        bg_jobs = cast_jobs("w_mix_in", wsrc("w_mix_in"), w_bf["w_mix_in"], r_wbf["w_mix_in"], 8, MIXW, 1)
        bg_jobs += cast_jobs("w_ret_out", wsrc("w_ret_out"), w_bf["w_ret_out"], r_wbf["w_ret_out"], 8, D, 2)
        bg_jobs += cast_jobs("w_na_out", wsrc("w_na_out"), w_bf["w_na_out"], r_wbf["w_na_out"], 4, D, 2)
        bg_jobs += cast_jobs("w_mix_out", wsrc("w_mix_out"), w_bf["w_mix_out"], r_wbf["w_mix_out"], 8, D, 2)
        bg_jobs += cast_jobs("nab", nab_in.rearrange("(v p) n -> p v n", p=128), nab_bf, r_nabbf, NVAR, 1024, 2)
        bg_jobs += cast_jobs("ffn2_w_in", wsrc("ffn2_w_in"), w_bf["ffn2_w_in"], r_wbf["ffn2_w_in"], 8, 2 * DFF, 1)
        bg_jobs += cast_jobs("ffn2_w_out", wsrc("ffn2_w_out"), w_bf["ffn2_w_out"], r_wbf["ffn2_w_out"], NJ, D, 2)
        for dst, src, r_dst in front:
            P.dma(POOL, dst, src, writes=[r_dst])

        MT = top.enter_context(nc.sbuf_tensor("g_MT", [128, 512], F32))
        DFt = top.enter_context(nc.sbuf_tensor("g_DFt", [128, 512], F32))
        DBt = top.enter_context(nc.sbuf_tensor("g_DBt", [128, 512], F32))
        dtok = top.enter_context(nc.sbuf_tensor("g_dtok", [128, 8], F32))
        gC = top.enter_context(nc.sbuf_tensor("g_gC", [128, 8], F32))
        tabs = (MT, DFt, DBt, dtok, gC)
        with ExitStack() as st:
            cst = st.enter_context(nc.sbuf_tensor("g_cst", [128, NCST], F32))
            lgt = st.enter_context(nc.sbuf_tensor("g_lgt", [128, 8], F32))
            tA = st.enter_context(nc.sbuf_tensor("g_tA", [128, 512], F32))
            tB = st.enter_context(nc.sbuf_tensor("g_tB", [128, 512], F32))
            r_c = Res("gconst")
            P.dma(SP, cst[:], cst_in[:, :], writes=[r_c])
            P.dma(SP, lgt[:], bcast_rows(dec_in, 8), writes=[r_c])
            P.op(ACT, lambda e: e.activation(out=lgt[:], in_=lgt[:], func=AF.Exp, scale=-1.0), reads=[r_c], writes=[r_c])
            P.op(ACT, lambda e: e.activation(out=lgt[:], in_=lgt[:], func=AF.Ln, bias=1.0), reads=[r_c], writes=[r_c])
            P.op(DVE, lambda e: e.tensor_scalar(out=lgt[:], in0=lgt[:], scalar1=-1.0, scalar2=None, op0=ALU.mult), reads=[r_c], writes=[r_c])
            for h in range(4):
                hs = slice(h * 128, (h + 1) * 128)
                P.op(ACT, lambda e, h=h: e.activation(out=tA[:, 0:128], in_=cst[:, 128:256], func=AF.Exp, scale=lgt[:, h:h + 1]), reads=[r_c], writes=[r_c])
                P.op(DVE, lambda e, hs=hs: e.tensor_tensor(out=MT[:, hs], in0=tA[:, 0:128], in1=cst[:, 384:512], op=ALU.mult), reads=[r_c], writes=[r_c])
                P.op(ACT, lambda e, h=h: e.activation(out=tB[:, 0:128], in_=cst[:, 256:384], func=AF.Exp, scale=lgt[:, 4 + h:5 + h]), reads=[r_c], writes=[r_c])
                P.op(DVE, lambda e: e.tensor_tensor(out=tB[:, 0:128], in0=tB[:, 0:128], in1=cst[:, 512:640], op=ALU.mult), reads=[r_c], writes=[r_c])
                P.op(DVE, lambda e, hs=hs: e.tensor_tensor(out=MT[:, hs], in0=MT[:, hs], in1=tB[:, 0:128], op=ALU.add), reads=[r_c], writes=[r_c])
                P.op(ACT, lambda e, h=h, hs=hs: e.activation(out=DFt[:, hs], in_=cst[:, 640:768], func=AF.Exp, scale=lgt[:, h:h + 1]), reads=[r_c], writes=[r_c])
                P.op(ACT, lambda e, h=h, hs=hs: e.activation(out=DBt[:, hs], in_=cst[:, 768:896], func=AF.Exp, scale=lgt[:, 4 + h:5 + h]), reads=[r_c], writes=[r_c])
            P.op(DVE, lambda e: e.tensor_scalar(out=DFt[:], in0=DFt[:], scalar1=RET_SCALE, scalar2=None, op0=ALU.mult), reads=[r_c], writes=[r_c])
            P.op(DVE, lambda e: e.tensor_scalar(out=DBt[:], in0=DBt[:], scalar1=RET_SCALE, scalar2=None, op0=ALU.mult), reads=[r_c], writes=[r_c])
            P.op(ACT, lambda e: e.activation(out=dtok[:, 0:4], in_=lgt[:, 0:4], func=AF.Exp, scale=cst[:, 896:897]), reads=[r_c], writes=[r_c])
            P.op(ACT, lambda e: e.activation(out=dtok[:, 4:8], in_=lgt[:, 4:8], func=AF.Exp, scale=cst[:, 897:898]), reads=[r_c], writes=[r_c])
            P.op(ACT, lambda e: e.activation(out=gC[:], in_=lgt[:], func=AF.Exp, scale=128.0), reads=[r_c], writes=[r_c])
            P.barrier()

        ffn_phase(nc, P, x_in, x1_d if stop_after != "A" else y_out,
                  w_bf["ffn1_w_in"], w_bf["ffn1_w_out"], r_wbf["ffn1_w_in"], r_wbf["ffn1_w_out"],
                  g_in["ffn1_pre_norm"], g_in["ffn1_post_norm"], ident_in, t_ffn, "f1", ntok, bg_jobs)
        for dst, src, r_dst in bg_jobs:
            P.dma(POOL, dst, src, writes=[r_dst])
        P.barrier()
        if stop_after != "A":
            for sq in range(n_seq):
                tok0 = sq * S
                mix_phase(nc, P, x1_d[tok0:tok0 + S, :], x2_d[tok0:tok0 + S, :] if stop_after != "B" else y_out[tok0:tok0 + S, :],
                          w_bf, r_wbf, g_in["mix_pre_norm"], g_in["mix_post_norm"], cst_in, rope_in, (nab_bf, r_nabbf), tabs, sb_d, "mxs%d" % sq)
                P.barrier()
            if stop_after != "B":
                ffn_phase(nc, P, x2_d, y_out,
                          w_bf["ffn2_w_in"], w_bf["ffn2_w_out"], r_wbf["ffn2_w_in"], r_wbf["ffn2_w_out"],
                          g_in["ffn2_pre_norm"], g_in["ffn2_post_norm"], ident_in, t_ffn, "f2", ntok)
                P.barrier()

        P.barrier()
        with nc.Block() as block:
            P.emit(block)
    return nc


def norm_rstd(P, pool_ssq, pool_rstd, r_ssq, r_rstd, neg_half, r_consts, n, inv_n):
    P.op(POOL, lambda e: e.tensor_scalar(out=pool_rstd, in0=pool_ssq, scalar1=inv_n, scalar2=EPS, op0=ALU.mult, op1=ALU.add),
         reads=[r_ssq], writes=[r_rstd])
    P.op(POOL, lambda e: e.tensor_tensor(out=pool_rstd, in0=pool_rstd, in1=neg_half[:, 0:n], op=ALU.pow),
         reads=[r_rstd, r_consts], writes=[r_rstd])


def ffn_phase(nc, P, x_src, x_dst, win_bf, wout_bf, r_win, r_wout, g_pre, g_post, ident_in, T, tag, ntok=S, bg_jobs=None):
    from contextlib import ExitStack
    NG = ntok // T
    TT = T // 128
    TB = T // 512
    NWB = 3
    with ExitStack() as st:
        sb = lambda name, shape, dt: st.enter_context(nc.sbuf_tensor(uname(tag + name), shape, dt))
        ps = lambda name, shape, dt: st.enter_context(nc.psum_tensor(uname(tag + name), shape, dt))
        NXN, NXR = 3, 2
        xn = [sb("xn%d" % i, [128, D], F32) for i in range(NXN)]
        r_xn = [Res("xn") for _ in range(NXN)]
        xr = [sb("xr%d" % i, [128, D], F32) for i in range(NXR)]
        r_xr = [Res("xr") for _ in range(NXR)]
        uT = [sb("uT%d" % i, [128, 8, T], BF16) for i in range(2)]
        r_uT = [[Res("uT") for _ in range(TB)] for _ in range(2)]
        hT = sb("hT", [128, NJ, T], BF16)
        r_hT = Res("hT")
        wout = sb("wout", [128, NJ, D], BF16)
        r_wo = Res("wout")
        wg = [sb("wg%d" % i, [128, 8, 256], BF16) for i in range(NWB)]
        wu = [sb("wu%d" % i, [128, 8, 256], BF16) for i in range(NWB)]
        r_w = [Res("w") for _ in range(NWB)]
        gpre = sb("gpre", [128, D], F32)
        gpost = sb("gpost", [128, D], F32)
        ident_f = sb("identf", [128, 128], F32)
        ident = sb("ident", [128, 128], BF16)
        neg_half = sb("nh", [128, 8], F32)
        r_c = Res("consts")
        ub = [sb("ub%d" % i, [128, D], BF16) for i in range(2)]
        r_ub = [Res("ub") for _ in range(2)]
        junk = sb("junk", [128, D], BF16)
        r_junk = Res("junk")
        ssq = [sb("ssq%d" % i, [128, 1], F32) for i in range(4)]
        rstd = [sb("rstd%d" % i, [128, 1], F32) for i in range(4)]
        r_ssq = [Res("ssq") for _ in range(4)]
        r_rstd = [Res("rstd") for _ in range(4)]
        sg = [sb("sg%d" % i, [128, 512], F32) for i in range(2)]
        r_sg = [Res("sg") for _ in range(2)]
        tmp = [sb("tmp%d" % i, [128, D], F32) for i in range(2)]
        r_tmp = [Res("tmp") for _ in range(2)]
        p2 = [ps("p2_%d" % i, [128, 512], F32) for i in range(3)]
        r_p2 = [Res("p2") for _ in range(3)]
        p3 = [ps("p3_%d" % i, [128, D], F32) for i in range(2)]
        r_p3 = [Res("p3") for _ in range(2)]
        pT = ps("pT", [128, D], BF16)
        r_pT = Res("pT")

        P.dma(SP, gpre[:], bcast_rows(g_pre, D), writes=[r_c])
        P.dma(SP, gpost[:], bcast_rows(g_post, D), writes=[r_c])
        P.dma(SP, ident_f[:], ident_in[:, :], writes=[r_c])
        P.op(DVE, lambda e: e.tensor_copy(out=ident[:], in_=ident_f[:]), reads=[r_c], writes=[r_c])
        P.op(DVE, lambda e: e.tensor_scalar(out=gpost[:], in0=gpost[:], scalar1=0.5, scalar2=None, op0=ALU.mult), reads=[r_c], writes=[r_c])
        P.op(DVE, lambda e: e.memset(neg_half[:], -0.5), writes=[r_c])

        cnt = {"n": 0, "p2": 0, "p3": 0, "w": 0, "sg": 0, "tmp": 0, "xn": 0, "xr": 0}

        nrec = {}

        def stage_x(g, t):
            k = cnt["n"] % 4
            kb = cnt["n"] % 2
            cnt["n"] += 1
            ix = cnt["xn"] % NXN
            cnt["xn"] += 1
            nrec[(g, t)] = kb
            r0 = g * T + t * 128
            P.dma(SP, xn[ix][:], x_src[r0:r0 + 128, :], writes=[r_xn[ix]])
            xin = xn[ix][:]
            P.op(ACT, lambda e, xin=xin, k=k: e.activation(out=junk[:], in_=xin, func=AF.Square, accum_out=ssq[k][:]),
                 reads=[r_xn[ix]], writes=[r_junk, r_ssq[k]])
            norm_rstd(P, ssq[k][:], rstd[k][:], r_ssq[k], r_rstd[k], neg_half, r_c, 1, 1.0 / D)
            P.op(DVE, lambda e, xin=xin, k=k, kb=kb: e.scalar_tensor_tensor(out=ub[kb][:], in0=xin, scalar=rstd[k][:], in1=gpre[:], op0=ALU.mult, op1=ALU.mult),
                 reads=[r_xn[ix], r_rstd[k], r_c], writes=[r_ub[kb]])

        def stage_y(g, t):
            b = g % 2
            kb = nrec[(g, t)]
            for kc in range(8):
                P.op(PE, lambda e, kb=kb, kc=kc: e.transpose(out=pT[:, kc * 128:(kc + 1) * 128], in_=ub[kb][:, kc * 128:(kc + 1) * 128], identity=ident[:]),
                     reads=[r_ub[kb], r_c], writes=[r_pT], inc=(kc == 7))
            P.op(ACT, lambda e, b=b, t=t: e.activation(out=uT[b][:, :, t * 128:(t + 1) * 128], in_=pT[:].rearrange("p (k c) -> p k c", k=8), func=AF.Copy),
                 reads=[r_pT], writes=[r_uT[b][t // 4]])

        def norm_group(g):
            stage_x(g, 0)
            for t in range(TT):
                if t + 1 < TT:
                    stage_x(g, t + 1)
                stage_y(g, t)

        def load_w(jb):
            s = cnt["w"] % NWB
            cnt["w"] += 1
            c0 = jb * 256
            P.dma(SP, wg[s][:], win_bf[:, :, c0:c0 + 256], reads=[r_win], writes=[r_w[s]])
            P.dma(SP, wu[s][:], win_bf[:, :, DFF + c0:DFF + c0 + 256], reads=[r_win], writes=[r_w[s]])
            return s

        NJB = NJ // 2
        norm_group(0)
        slots = [load_w(0), load_w(1)]
        for g in range(NG):
            b = g % 2
            for jb in range(NJB):
                s = slots.pop(0)
                nxt = jb + 2
                if nxt < NJB:
                    slots.append(load_w(nxt))
                elif g + 1 < NG:
                    slots.append(load_w(nxt - NJB))
                P.dma(SP, wout[:, 2 * jb:2 * jb + 2, :], wout_bf[:, 2 * jb:2 * jb + 2, :], reads=[r_wout], writes=[r_wo])
                for jj in range(2):
                    j = jb * 2 + jj
                    for tb in range(TB):
                        ig = cnt["p2"] % 3
                        iu = (cnt["p2"] + 1) % 3
                        cnt["p2"] += 2
                        for kc in range(8):
                            P.op(PE, lambda e, ig=ig, s=s, jj=jj, kc=kc, b=b, tb=tb: e.matmul(
                                p2[ig][:], lhsT=wg[s][:, kc, jj * 128:(jj + 1) * 128], rhs=uT[b][:, kc, tb * 512:(tb + 1) * 512],
                                start=(kc == 0), stop=(kc == 7)),
                                reads=[r_w[s], r_uT[b][tb]], writes=[r_p2[ig]], inc=(kc == 7))
                        for kc in range(8):
                            P.op(PE, lambda e, iu=iu, s=s, jj=jj, kc=kc, b=b, tb=tb: e.matmul(
                                p2[iu][:], lhsT=wu[s][:, kc, jj * 128:(jj + 1) * 128], rhs=uT[b][:, kc, tb * 512:(tb + 1) * 512],
                                start=(kc == 0), stop=(kc == 7)),
                                reads=[r_w[s], r_uT[b][tb]], writes=[r_p2[iu]], inc=(kc == 7))
                        k = cnt["sg"] % 2
                        cnt["sg"] += 1
                        P.op(ACT, lambda e, k=k, ig=ig: e.activation(out=sg[k][:], in_=p2[ig][:], func=AF.Silu),
                             reads=[r_p2[ig]], writes=[r_sg[k]])
                        P.op(DVE, lambda e, k=k, iu=iu, j=j, tb=tb: e.tensor_tensor(out=hT[:, j, tb * 512:(tb + 1) * 512], in0=p2[iu][:], in1=sg[k][:], op=ALU.mult),
                             reads=[r_p2[iu], r_sg[k]], writes=[r_hT])
                if bg_jobs:
                    for _ in range(min(1, len(bg_jobs))):
                        dst_, src_, r_dst_ = bg_jobs.pop(0)
                        P.dma(POOL, dst_, src_, writes=[r_dst_])
                if g + 1 < NG:
                    if 1 <= jb <= TT:
                        stage_x(g + 1, jb - 1)
                    if 2 <= jb <= TT + 1:
                        stage_y(g + 1, jb - 2)
            for t in range(TT):
                ip = cnt["p3"] % 2
                cnt["p3"] += 1
                ir = cnt["xr"] % NXR
                cnt["xr"] += 1
                r0 = g * T + t * 128
                if g == 0 and t == 0:
                    P.dma(SP, xr[ir][:], x_src[r0:r0 + 128, :], writes=[r_xr[ir]])
                for half in range(2):
                    for j in range(NJ):
                        P.op(PE, lambda e, ip=ip, half=half, j=j, t=t: e.matmul(
                            p3[ip][:, half * 512:(half + 1) * 512], lhsT=hT[:, j, t * 128:(t + 1) * 128], rhs=wout[:, j, half * 512:(half + 1) * 512],
                            start=(j == 0), stop=(j == NJ - 1)),
                            reads=[r_hT, r_wo], writes=[r_p3[ip]], inc=(half == 1 and j == NJ - 1))
                nt_ = g * TT + t + 1
                if nt_ < NG * TT:
                    irn = cnt["xr"] % NXR
                    P.dma(SP, xr[irn][:], x_src[nt_ * 128:(nt_ + 1) * 128, :], writes=[r_xr[irn]])
                k = cnt["n"] % 4
                cnt["n"] += 1
                kt = cnt["tmp"] % 2
                cnt["tmp"] += 1
                P.op(ACT, lambda e, ip=ip, k=k: e.activation(out=junk[:], in_=p3[ip][:], func=AF.Square, accum_out=ssq[k][:]),
                     reads=[r_p3[ip]], writes=[r_junk, r_ssq[k]])
                norm_rstd(P, ssq[k][:], rstd[k][:], r_ssq[k], r_rstd[k], neg_half, r_c, 1, 1.0 / D)
                P.op(DVE, lambda e, ip=ip, k=k, kt=kt: e.scalar_tensor_tensor(out=tmp[kt][:], in0=p3[ip][:], scalar=rstd[k][:], in1=gpost[:], op0=ALU.mult, op1=ALU.mult),
                     reads=[r_p3[ip], r_rstd[k], r_c], writes=[r_tmp[kt]])
                P.op(POOL, lambda e, ir=ir, kt=kt: e.tensor_tensor(out=xr[ir][:], in0=xr[ir][:], in1=tmp[kt][:], op=ALU.add),
                     reads=[r_tmp[kt], r_xr[ir]], writes=[r_xr[ir]])
                P.dma(SP, x_dst[r0:r0 + 128, :], xr[ir][:], reads=[r_xr[ir]])


def common_inputs(inputs):
    common = {k: np.ascontiguousarray(inputs[k][0], dtype=np.float32) for k in WEIGHTS}
    for k in GAINS:
        common[k] = np.ascontiguousarray(inputs[k], dtype=np.float32).reshape(1, D)
    common["ident"] = np.eye(128, dtype=np.float32)
    common["cst"] = make_cst()
    common["rope"] = make_rope()
    common["nab"] = make_nab(np.asarray(inputs["na_rel_bias"])[0])
    common["dec"] = np.concatenate([np.asarray(inputs["ret_decay_fwd"], np.float32).reshape(4),
                                    np.asarray(inputs["ret_decay_bwd"], np.float32).reshape(4)]).reshape(1, 8)
    return common


def kernel(**inputs):
    x = np.ascontiguousarray(inputs["x"], dtype=np.float32)
    B = x.shape[0]
    assert B == N_CORES * SEQ_PER_CORE
    nc = build()
    common = common_inputs(inputs)
    in_maps = []
    for c in range(N_CORES):
        m = dict(common)
        m["x"] = x[c * SEQ_PER_CORE:(c + 1) * SEQ_PER_CORE].reshape(SEQ_PER_CORE * S, D)
        in_maps.append(m)
    res = run_bass_kernel_spmd(nc, in_maps, core_ids=list(range(N_CORES)))
    out = np.stack([r["y"].reshape(SEQ_PER_CORE, S, D) for r in res.results], axis=0)
    return out.reshape(B, S, D).astype(np.float32)


RET_SCALE = 128.0 ** -0.5
NCST = 898


def make_cst():
    c = np.zeros((128, NCST), np.float32)
    c[:, 0:128] = np.eye(128, dtype=np.float32)
    k = np.arange(128)[:, None]
    q = np.arange(128)[None, :]
    c[:, 128:256] = np.maximum(q - k, 0)
    c[:, 256:384] = np.maximum(k - q, 0)
    c[:, 384:512] = np.where(q >= k, RET_SCALE, 0.0)
    c[:, 512:640] = np.where(k > q, RET_SCALE, 0.0)
    c[:, 640:768] = np.broadcast_to(q + 1, (128, 128))
    c[:, 768:896] = np.broadcast_to(128 - q, (128, 128))
    c[:, 896] = 127 - np.arange(128)
    c[:, 897] = np.arange(128)
    return c


def make_rope():
    inv = (np.float32(1.0) / (np.float32(10000.0) ** np.linspace(0.0, 1.0, 64, dtype=np.float32))).astype(np.float32)
    pos = np.arange(S, dtype=np.float32)
    ang = (pos[:, None] * inv[None, :]).astype(np.float32)
    cos = np.cos(ang).astype(np.float32).reshape(NT, 128, 64).transpose(1, 0, 2)
    sin = np.sin(ang).astype(np.float32).reshape(NT, 128, 64).transpose(1, 0, 2)
    return np.ascontiguousarray(np.concatenate([cos.reshape(128, NT * 64), sin.reshape(128, NT * 64)], axis=1))


def na_offsets(a):
    rows = []
    for qr in (0, 1):
        rq = 2 * a + qr
        rs = min(max(rq - 4, 0), 24)
        rows += [rs, rs + 7]
    return list(range(min(rows) // 2 - a, max(rows) // 2 - a + 1))


def _na_variants():
    var_of, idxs, seen = {}, [], {}
    ck = np.arange(64)[:, None]
    cq = np.arange(64)[None, :]
    ws = np.clip(cq - 8, 0, 48)
    cvalid = (ck >= ws) & (ck < ws + 16)
    ci = np.clip(ck - cq + 15, 0, 30)
    for a in range(16):
        for o in na_offsets(a):
            idx = np.full((128, 128), -1, np.int64)
            for kr in (0, 1):
                for qr in (0, 1):
                    rk = 2 * (a + o) + kr
                    rq = 2 * a + qr
                    rs = min(max(rq - 4, 0), 24)
                    if not (rs <= rk < rs + 8):
                        continue
                    ri = rk - rq + 7
                    idx[kr * 64:(kr + 1) * 64, qr * 64:(qr + 1) * 64] = np.where(cvalid, ri * 31 + ci, -1)
            key = idx.tobytes()
            if key not in seen:
                seen[key] = len(idxs)
                idxs.append(idx)
            var_of[(a, o)] = seen[key]
    return var_of, idxs


NA_VAR_OF, NA_IDX = _na_variants()
NVAR = len(NA_IDX)
NA_NEG = -30000.0


def make_nab(rel_bias):
    rb = np.asarray(rel_bias, np.float32).reshape(8, 15 * 31)
    out = np.empty((NVAR, 128, 8, 128), np.float32)
    for v, idx in enumerate(NA_IDX):
        safe = np.where(idx >= 0, idx, 0)
        for h in range(8):
            out[v, :, h, :] = np.where(idx >= 0, rb[h][safe], np.float32(NA_NEG))
    return np.ascontiguousarray(out.reshape(NVAR * 128, 1024))


def bc_ap(base, dims):
    return bass.AP(base.tensor, base.offset, [list(base.ap[0])] + [list(d) for d in dims])


C_RQ, C_RK, C_RV, C_RG, C_NQ, C_NK, C_NV, C_GR, C_GN = 0, 512, 1024, 2048, 3072, 3584, 4096, 4608, 5632


_DBG = {'n': 0, 'max': 99}


def _dbg_run():
    _DBG['n'] += 1
    return _DBG['n'] <= _DBG['max']


def mix_phase(nc, P, x_src, x_dst, w_bf, r_wbf, g_pre, g_post, cst_in, rope_in, nab_in, tabs, sb_d, tag):
    MT, DFt, DBt, dtok, gC = tabs
    from contextlib import ExitStack
    wmi, r_wmi = w_bf["w_mix_in"], r_wbf["w_mix_in"]
    with ExitStack() as M:
        sbM = lambda name, shape, dt: M.enter_context(nc.sbuf_tensor(uname(tag + name), shape, dt))
        uT = sbM("uT", [128, 8, S], BF16)
        r_uTt = [Res("uT") for _ in range(NT)]
        VA = sbM("VA", [128, NT, 1024], BF16)
        r_VA = [Res("VA") for _ in range(NT)]
        onaT = sbM("onaT", [128, 4, S], BF16)
        r_onaT = Res("onaT")
        cst = sbM("cst", [128, NCST], F32)
        ident = sbM("ident", [128, 128], BF16)
        neg_half = sbM("nh", [128, 8], F32)
        r_c = Res("c")
        P.dma(SP, cst[:], cst_in[:, :], writes=[r_c])
        P.op(DVE, lambda e: e.tensor_copy(out=ident[:], in_=cst[:, 0:128]), reads=[r_c], writes=[r_c])
        P.op(DVE, lambda e: e.memset(neg_half[:], -0.5), writes=[r_c])

        def _subphase():
            with ExitStack() as st:
                sb = lambda name, shape, dt: st.enter_context(nc.sbuf_tensor(uname(tag + name), shape, dt))
                ps = lambda name, shape, dt: st.enter_context(nc.psum_tensor(uname(tag + name), shape, dt))
                Ktm = sb("Ktm", [128, NT, 512], BF16)
                r_K = [Res("K") for _ in range(NT)]
                wk = sb("wk", [128, 8, 512], BF16)
                wv = sb("wv", [128, 8, 1024], BF16)
                r_wk, r_wv = Res("wk"), Res("wv")
                rope = sb("rope", [128, 2 * NT * 64], F32)
                Sf32 = sb("Sf32", [128, 1024], F32)
                Sb32 = sb("Sb32", [128, 1024], F32)
                Sfb = sb("Sfb", [128, 1024], BF16)
                Sbb = [sb("Sbb%d" % i, [128, 1024], BF16) for i in range(2)]
                r_Sf32, r_Sb32, r_Sfb = Res("Sf32"), Res("Sb32"), Res("Sfb")
                r_Sbb = [Res("Sbb") for _ in range(2)]
                r_sbd = [Res("sbd") for _ in range(NT)]
                tA = sb("tA", [128, 512], F32)
                tB = sb("tB", [128, 512], F32)
                r_tA, r_tB = Res("tA"), Res("tB")
                qrot = sb("qrot", [128, 512], BF16)
                r_qrot = Res("qrot")
                kdec = sb("kdec", [128, 512], BF16)
                r_kdec = Res("kdec")
                qT = sb("qT", [128, 512], BF16)
                qfT = sb("qfT", [128, 512], BF16)
                qbT = sb("qbT", [128, 512], BF16)
                kT = sb("kT", [128, 512], BF16)
                r_qT, r_qfT, r_qbT, r_kT = Res("qT"), Res("qfT"), Res("qbT"), Res("kT")
                Sm = sb("Sm", [128, 512], BF16)
                r_Sm = Res("Sm")
                srg = sb("srg", [128, 1024], F32)
                r_srg = Res("srg")
                Abuf = sb("A", [128, 1024], BF16)
                r_A = Res("A")
                junk = sb("junk", [128, 256], BF16)
                r_junk = Res("junk")
                ssq4 = sb("ssq4", [128, 4], F32)
                rstd4 = sb("rstd4", [128, 4], F32)
                r_ssq4, r_rstd4 = Res("ssq4"), Res("rstd4")
                b0 = ps("b0", [128, 512], F32)
                pR = ps("pR", [128, 1024], F32)
                pTr = ps("pTr", [128, 1024], BF16)
                pY = ps("pY", [128, 1024], F32)
                pU = ps("pU", [128, 1024], F32)
                r_b0, r_pR, r_pTr, r_pY, r_pU = Res("b0"), Res("pR"), Res("pTr"), Res("pY"), Res("pU")

                P.dma(SP, rope[:], rope_in[:, :], writes=[r_c])

                if _DBG.get('rstop') == 'setup':
                    P.barrier()
                    return

                def rotary(dst, n, r_dst):
                    cb = rope[:, n * 64:(n + 1) * 64]
                    sn = rope[:, NT * 64 + n * 64:NT * 64 + (n + 1) * 64]
                    cosb = bc_ap(cb, [[0, 4], [0, 2], [1, 64]])
                    sinb = bc_ap(sn, [[0, 4], [1, 64]])
                    v4 = b0[:].rearrange("p (h t d) -> p h t d", h=4, t=2)
                    a4 = tA[:].rearrange("p (h t d) -> p h t d", h=4, t=2)
                    b4 = tB[:].rearrange("p (h t d) -> p h t d", h=4, t=2)
                    d4 = dst.rearrange("p (h t d) -> p h t d", h=4, t=2)
                    P.op(DVE, lambda e: e.tensor_tensor(out=a4, in0=v4, in1=cosb, op=ALU.mult), reads=[r_b0, r_c], writes=[r_tA])
                    P.op(DVE, lambda e: e.tensor_tensor(out=b4[:, :, 0, :], in0=v4[:, :, 1, :], in1=sinb, op=ALU.mult), reads=[r_b0, r_c], writes=[r_tB])
                    P.op(DVE, lambda e: e.tensor_tensor(out=b4[:, :, 1, :], in0=v4[:, :, 0, :], in1=sinb, op=ALU.mult), reads=[r_b0, r_c], writes=[r_tB])
                    P.op(POOL, lambda e: e.tensor_tensor(out=d4[:, :, 0, :], in0=a4[:, :, 0, :], in1=b4[:, :, 0, :], op=ALU.subtract), reads=[r_tA, r_tB], writes=[r_dst])
                    P.op(POOL, lambda e: e.tensor_tensor(out=d4[:, :, 1, :], in0=a4[:, :, 1, :], in1=b4[:, :, 1, :], op=ALU.add), reads=[r_tA, r_tB], writes=[r_dst])

                def proj_tok(dst_ps, r_dst, wbuf, r_w, n, ncols):
                    for half in range(ncols // 512):
                        for kc in range(8):
                            P.op(PE, lambda e, half=half, kc=kc: e.matmul(dst_ps[:, half * 512:(half + 1) * 512], lhsT=uT[:, kc, n * 128:(n + 1) * 128],
                                                                          rhs=wbuf[:, kc, half * 512:(half + 1) * 512], start=(kc == 0), stop=(kc == 7)),
                                 reads=[r_uTt[n], r_w], writes=[r_dst], inc=(kc == 7 and half == ncols // 512 - 1))

                def state_update(S32, r_S32, goff, kdec_col0):
                    pass

                P.dma(SP, wk[:], wmi[:, :, C_RK:C_RK + 512], reads=[r_wmi], writes=[r_wk])
                P.dma(SP, wv[:], wmi[:, :, C_RV:C_RV + 1024], reads=[r_wmi], writes=[r_wv])
                P.op(DVE, lambda e: e.memset(Sb32[:], 0.0), writes=[r_Sb32])
                gpre = sb("gpre", [128, D], F32)
                xn = [sb("xn%d" % i, [128, D], F32) for i in range(4)]
                r_xn = [Res("xn") for _ in range(4)]
                ub = [sb("ub%d" % i, [128, D], BF16) for i in range(2)]
                r_ub = [Res("ub") for _ in range(2)]
                junkM = sb("junkM", [128, D], BF16)
                r_junkM = Res("junkM")
                ssq = [sb("ssq%d" % i, [128, 1], F32) for i in range(4)]
                rstd = [sb("rstd%d" % i, [128, 1], F32) for i in range(4)]
                r_ssq = [Res("ssq") for _ in range(4)]
                r_rstd = [Res("rstd") for _ in range(4)]
                P.dma(SP, gpre[:], bcast_rows(g_pre, D), writes=[r_c])

                def m1_dma(t):
                    P.dma(SP, xn[t % 4][:], x_src[t * 128:(t + 1) * 128, :], writes=[r_xn[t % 4]])

                def m1_x(t):
                    ix, k, kb = t % 4, t % 4, t % 2
                    P.op(ACT, lambda e, ix=ix, k=k: e.activation(out=junkM[:], in_=xn[ix][:], func=AF.Square, accum_out=ssq[k][:]),
                         reads=[r_xn[ix]], writes=[r_junkM, r_ssq[k]])
                    norm_rstd(P, ssq[k][:], rstd[k][:], r_ssq[k], r_rstd[k], neg_half, r_c, 1, 1.0 / D)
                    P.op(DVE, lambda e, ix=ix, k=k, kb=kb: e.scalar_tensor_tensor(out=ub[kb][:], in0=xn[ix][:], scalar=rstd[k][:], in1=gpre[:], op0=ALU.mult, op1=ALU.mult),
                         reads=[r_xn[ix], r_rstd[k], r_c], writes=[r_ub[kb]])

                def m1_y(t):
                    kb = t % 2
                    for kc in range(8):
                        P.op(PE, lambda e, kb=kb, kc=kc: e.transpose(out=pTr[:, kc * 128:(kc + 1) * 128], in_=ub[kb][:, kc * 128:(kc + 1) * 128], identity=ident[:]),
                             reads=[r_ub[kb], r_c], writes=[r_pTr], inc=(kc == 7))
                    P.op(ACT, lambda e, t=t: e.activation(out=uT[:, :, t * 128:(t + 1) * 128], in_=pTr[:].rearrange("p (k c) -> p k c", k=8), func=AF.Copy),
                         reads=[r_pTr], writes=[r_uTt[t]])

                for t in (NT - 1, NT - 2, NT - 3):
                    m1_dma(t)
                m1_x(NT - 1)
                m1_x(NT - 2)
                m1_y(NT - 1)
                for n in range(NT - 1, -1, -1):
                    if n - 3 >= 0:
                        m1_dma(n - 3)
                    if n - 2 >= 0:
                        m1_x(n - 2)
                    if n - 1 >= 0:
                        m1_y(n - 1)
                    proj_tok(b0, r_b0, wk, r_wk, n, 512)
                    rotary(Ktm[:, n, :], n, r_K[n])
                    proj_tok(pR, r_pR, wv, r_wv, n, 1024)
                    P.op(ACT, lambda e, n=n: e.activation(out=VA[:, n, :], in_=pR[:], func=AF.Copy), reads=[r_pR], writes=[r_VA[n]])
                    i = n % 2
                    P.op(ACT, lambda e, i=i: e.activation(out=Sbb[i][:], in_=Sb32[:], func=AF.Copy), reads=[r_Sb32], writes=[r_Sbb[i]])
                    P.dma(SP, sb_d[n * 128:(n + 1) * 128, :], Sbb[i][:], reads=[r_Sbb[i]], writes=[r_sbd[n]])
                    if n > 0:
                        P.op(DVE, lambda e, n=n: e.tensor_tensor(out=kdec[:].rearrange("p (h d) -> p h d", h=4), in0=Ktm[:, n, :].rearrange("p (h d) -> p h d", h=4),
                                                                 in1=dtok[:, 4:8].to_broadcast([128, 4, 128]), op=ALU.mult),
                             reads=[r_K[n], r_c], writes=[r_kdec])
                        for h in range(4):
                            P.op(PE, lambda e, h=h, n=n: e.matmul(pU[:, h * 256:(h + 1) * 256], lhsT=kdec[:, h * 128:(h + 1) * 128], rhs=VA[:, n, h * 256:(h + 1) * 256], start=True, stop=True),
                                 reads=[r_kdec, r_VA[n]], writes=[r_pU], inc=(h == 3))
                        for h in range(4):
                            P.op(DVE, lambda e, h=h: e.scalar_tensor_tensor(out=Sb32[:, h * 256:(h + 1) * 256], in0=Sb32[:, h * 256:(h + 1) * 256], scalar=gC[:, 4 + h:5 + h],
                                                                             in1=pU[:, h * 256:(h + 1) * 256], op0=ALU.mult, op1=ALU.add),
                                 reads=[r_pU, r_Sb32, r_c], writes=[r_Sb32])

                if _DBG.get('rstop') == 'pass1':
                    P.barrier()
                    return
                P.dma(SP, wk[:], wmi[:, :, C_RQ:C_RQ + 512], reads=[r_wmi], writes=[r_wk])
                P.dma(SP, wv[:], wmi[:, :, C_RG:C_RG + 1024], reads=[r_wmi], writes=[r_wv])
                P.op(DVE, lambda e: e.memset(Sf32[:], 0.0), writes=[r_Sf32])
                P.op(DVE, lambda e: e.memset(Sfb[:], 0.0), writes=[r_Sfb])
                qrot2 = [qrot, sb("qrot1", [128, 512], BF16)]
                r_qrot2 = [r_qrot, Res("qrot1")]
                srg2 = [srg, sb("srg1", [128, 1024], F32)]
                r_srg2 = [r_srg, Res("srg1")]
                kdec2 = [kdec, sb("kdec1", [128, 512], BF16)]
                r_kdec2 = [r_kdec, Res("kdec1")]
                pU0, pU1 = pU[:, 0:512], pU[:, 512:1024]
                r_pU0, r_pU1 = Res("pU0"), Res("pU1")

                def stage_a(n):
                    i = n % 2
                    P.dma(SP, Sbb[i][:], sb_d[n * 128:(n + 1) * 128, :], reads=[r_sbd[n]], writes=[r_Sbb[i]])
                    proj_tok(b0, r_b0, wk, r_wk, n, 512)
                    rotary(qrot2[i][:], n, r_qrot2[i])
                    proj_tok(pR, r_pR, wv, r_wv, n, 1024)
                    P.op(ACT, lambda e, i=i: e.activation(out=srg2[i][:], in_=pR[:], func=AF.Silu), reads=[r_pR], writes=[r_srg2[i]])
                    P.op(DVE, lambda e, n=n, i=i: e.tensor_tensor(out=kdec2[i][:].rearrange("p (h d) -> p h d", h=4), in0=Ktm[:, n, :].rearrange("p (h d) -> p h d", h=4),
                                                                 in1=dtok[:, 0:4].to_broadcast([128, 4, 128]), op=ALU.mult),
                         reads=[r_K[n], r_c], writes=[r_kdec2[i]])

                def stage_t(n):
                    i = n % 2
                    for h in range(4):
                        P.op(PE, lambda e, h=h, i=i: e.transpose(out=pTr[:, h * 128:(h + 1) * 128], in_=qrot2[i][:, h * 128:(h + 1) * 128], identity=ident[:]),
                             reads=[r_qrot2[i], r_c], writes=[r_pTr], inc=False)
                    for h in range(4):
                        P.op(PE, lambda e, h=h, n=n: e.transpose(out=pTr[:, 512 + h * 128:512 + (h + 1) * 128], in_=Ktm[:, n, h * 128:(h + 1) * 128], identity=ident[:]),
                             reads=[r_K[n], r_c], writes=[r_pTr], inc=(h == 3))
                    P.op(ACT, lambda e: e.activation(out=qT[:], in_=pTr[:, 0:512], func=AF.Copy), reads=[r_pTr], writes=[r_qT])
                    P.op(ACT, lambda e: e.activation(out=kT[:], in_=pTr[:, 512:1024], func=AF.Copy), reads=[r_pTr], writes=[r_kT])
                    P.op(DVE, lambda e: e.tensor_tensor(out=qfT[:], in0=qT[:], in1=DFt[:], op=ALU.mult), reads=[r_qT, r_c], writes=[r_qfT])
                    P.op(DVE, lambda e: e.tensor_tensor(out=qbT[:], in0=qT[:], in1=DBt[:], op=ALU.mult), reads=[r_qT, r_c], writes=[r_qbT])

                def stage_s(n):
                    for h in range(4):
                        P.op(PE, lambda e, h=h: e.matmul(pU1[:, h * 128:(h + 1) * 128], lhsT=kT[:, h * 128:(h + 1) * 128], rhs=qT[:, h * 128:(h + 1) * 128], start=True, stop=True),
                             reads=[r_kT, r_qT], writes=[r_pU1], inc=(h == 3))
                    P.op(DVE, lambda e: e.tensor_tensor(out=Sm[:], in0=pU1, in1=MT[:], op=ALU.mult), reads=[r_pU1, r_c], writes=[r_Sm])

                def stage_u(n, half):
                    i = n % 2
                    for hh in range(2):
                        h = half * 2 + hh
                        P.op(PE, lambda e, h=h, hh=hh, n=n, i=i: e.matmul(pU0[:, hh * 256:(hh + 1) * 256], lhsT=kdec2[i][:, h * 128:(h + 1) * 128], rhs=VA[:, n, h * 256:(h + 1) * 256], start=True, stop=True),
                             reads=[r_kdec2[i], r_VA[n]], writes=[r_pU0], inc=(hh == 1))
                    for hh in range(2):
                        h = half * 2 + hh
                        P.op(DVE, lambda e, h=h, hh=hh: e.scalar_tensor_tensor(out=Sf32[:, h * 256:(h + 1) * 256], in0=Sf32[:, h * 256:(h + 1) * 256], scalar=gC[:, h:h + 1],
                                                                              in1=pU0[:, hh * 256:(hh + 1) * 256], op0=ALU.mult, op1=ALU.add),
                             reads=[r_pU0, r_Sf32, r_c], writes=[r_Sf32])

                def stage_y(n):
                    i = n % 2
                    for h in range(4):
                        vs = slice(h * 256, (h + 1) * 256)
                        hs = slice(h * 128, (h + 1) * 128)
                        P.op(PE, lambda e, vs=vs, hs=hs, n=n: e.matmul(pY[:, vs], lhsT=Sm[:, hs], rhs=VA[:, n, vs], start=True, stop=False),
                             reads=[r_Sm, r_VA[n]], writes=[r_pY], inc=False)
                        P.op(PE, lambda e, vs=vs, hs=hs: e.matmul(pY[:, vs], lhsT=qfT[:, hs], rhs=Sfb[:, vs], start=False, stop=False),
                             reads=[r_qfT, r_Sfb], writes=[r_pY], inc=False)
                        P.op(PE, lambda e, vs=vs, hs=hs, i=i: e.matmul(pY[:, vs], lhsT=qbT[:, hs], rhs=Sbb[i][:, vs], start=False, stop=True),
                             reads=[r_qbT, r_Sbb[i]], writes=[r_pY], inc=(h == 3))

                def stage_c(n):
                    i = n % 2
                    for h in range(4):
                        P.op(ACT, lambda e, h=h: e.activation(out=junk[:], in_=pY[:, h * 256:(h + 1) * 256], func=AF.Square, accum_out=ssq4[:, h:h + 1]),
                             reads=[r_pY], writes=[r_junk, r_ssq4])
                    norm_rstd(P, ssq4[:], rstd4[:], r_ssq4, r_rstd4, neg_half, r_c, 4, 1.0 / 256)
                    for h in range(4):
                        P.op(DVE, lambda e, h=h, i=i: e.scalar_tensor_tensor(out=Abuf[:, h * 256:(h + 1) * 256], in0=pY[:, h * 256:(h + 1) * 256], scalar=rstd4[:, h:h + 1],
                                                                            in1=srg2[i][:, h * 256:(h + 1) * 256], op0=ALU.mult, op1=ALU.mult),
                             reads=[r_pY, r_rstd4, r_srg2[i]], writes=[r_A])
                    P.op(ACT, lambda e: e.activation(out=Sfb[:], in_=Sf32[:], func=AF.Copy), reads=[r_Sf32], writes=[r_Sfb])

                def stage_at(n):
                    for kc in range(8):
                        P.op(PE, lambda e, kc=kc: e.transpose(out=pTr[:, kc * 128:(kc + 1) * 128], in_=Abuf[:, kc * 128:(kc + 1) * 128], identity=ident[:]),
                             reads=[r_A, r_c], writes=[r_pTr], inc=(kc == 7))
                    P.op(ACT, lambda e, n=n: e.activation(out=VA[:, n, :], in_=pTr[:], func=AF.Copy), reads=[r_pTr], writes=[r_VA[n]])

                stage_a(0)
                for n in range(NT):
                    stage_t(n)
                    if n + 1 < NT:
                        stage_a(n + 1)
                    if n >= 1:
                        stage_at(n - 1)
                    stage_s(n)
                    stage_u(n, 0)
                    stage_y(n)
                    stage_u(n, 1)
                    stage_c(n)
                stage_at(NT - 1)
                P.barrier()

        if _dbg_run():
            _subphase()

        def _subphase():
            with ExitStack() as st:
                sb = lambda name, shape, dt: st.enter_context(nc.sbuf_tensor(uname(tag + name), shape, dt))
                ps = lambda name, shape, dt: st.enter_context(nc.psum_tensor(uname(tag + name), shape, dt))
                nqT = sb("nqT", [128, 4, S], BF16)
                nkT = sb("nkT", [128, 4, S], BF16)
                nv = sb("nv", [128, NT, 8, 65], BF16)
                r_nq, r_nk, r_nv = Res("nq"), Res("nk"), Res("nv")
                Bt = sb("Bt", [128, NVAR, 1024], BF16)
                r_Bt = Res("Bt")
                wa = [sb("wa%d" % i, [128, 8, 512], BF16) for i in range(3)]
                r_wa = [Res("wa") for _ in range(3)]
                PT = [sb("PT%d" % i, [128, 1024], BF16) for i in range(2)]
                r_PT = [Res("PT") for _ in range(2)]
                ona = sb("ona", [128, 512], BF16)
                r_ona = Res("ona")
                rden = sb("rden", [128, 8], F32)
                r_rden = Res("rden")
                pS = [ps("pS%d" % i, [128, 1024], F32) for i in range(2)]
                r_pS = [Res("pS") for _ in range(2)]
                pO = ps("pO", [128, 2, 512], F32)
                r_pO = Res("pO")
                pTn = ps("pTn", [128, 1024], BF16)
                r_pTn = Res("pTn")

                for j, c0 in enumerate((C_NQ, C_NK, C_NV)):
                    P.dma(SP, wa[j][:], wmi[:, :, c0:c0 + 512], reads=[r_wmi], writes=[r_wa[j]])
                nabbf_ap, r_nabbf = nab_in
                P.dma(SP, Bt[:, 0:5, :], nabbf_ap[:, 0:5, :], reads=[r_nabbf], writes=[r_Bt])
                P.dma(SP, Bt[:, 5:NVAR, :], nabbf_ap[:, 5:NVAR, :], reads=[r_nabbf], writes=[r_Bt])
                P.op(DVE, lambda e: e.memset(nv[:], 1.0), writes=[r_nv])
                cnt = 0
                for j, (dst, r_dst, scale) in enumerate(((nqT, r_nq, 0.125), (nkT, r_nk, 1.0))):
                    for c in range(4):
                        for tb in range(4):
                            i = cnt % 2
                            cnt += 1
                            for kc in range(8):
                                P.op(PE, lambda e, i=i, j=j, c=c, kc=kc, tb=tb: e.matmul(pS[i][:, 0:512], lhsT=wa[j][:, kc, c * 128:(c + 1) * 128], rhs=uT[:, kc, tb * 512:(tb + 1) * 512],
                                                                                          start=(kc == 0), stop=(kc == 7)),
                                     reads=[r_wa[j]] + r_uTt[tb * 4:tb * 4 + 4], writes=[r_pS[i]], inc=(kc == 7))
                            P.op(ACT, lambda e, i=i, dst=dst, c=c, tb=tb, scale=scale: e.activation(out=dst[:, c, tb * 512:(tb + 1) * 512], in_=pS[i][:, 0:512], func=AF.Copy, scale=scale),
                                 reads=[r_pS[i]], writes=[r_dst])
                for t in range(NT):
                    i = cnt % 2
                    cnt += 1
                    for kc in range(8):
                        P.op(PE, lambda e, i=i, kc=kc, t=t: e.matmul(pS[i][:, 0:512], lhsT=uT[:, kc, t * 128:(t + 1) * 128], rhs=wa[2][:, kc, :], start=(kc == 0), stop=(kc == 7)),
                             reads=[r_wa[2], r_uTt[t]], writes=[r_pS[i]], inc=(kc == 7))
                    P.op(ACT, lambda e, i=i, t=t: e.activation(out=nv[:, t, :, 0:64], in_=pS[i][:, 0:512].rearrange("p (h d) -> p h d", h=8), func=AF.Copy),
                         reads=[r_pS[i]], writes=[r_nv])
                nqM = [[sb("nqM%d_%d" % (s_, i), [128, 4, 128], BF16) for i in range(2)] for s_ in range(2)]
                r_nqM = [[Res("nqM") for _ in range(2)] for _ in range(2)]
                ona2 = [ona, sb("ona1", [128, 512], BF16)]
                r_ona2 = [r_ona, Res("ona1")]
                for s_ in range(2):
                    for i in range(2):
                        P.op(DVE, lambda e, s_=s_, i=i: e.memset(nqM[s_][i][:], 0.0), writes=[r_nqM[s_][i]])
                units = [(a, oi, o, len(na_offsets(a))) for a in range(NT) for oi, o in enumerate(na_offsets(a))]

                def na_scores(u):
                    a, oi, o, _ = units[u]
                    sa = a % 2
                    if oi == 0:
                        P.op(ACT, lambda e, a=a, sa=sa: e.activation(out=nqM[sa][0][0:64, :, :], in_=nqT[0:64, :, a * 128:(a + 1) * 128], func=AF.Copy),
                             reads=[r_nq], writes=[r_nqM[sa][0]])
                        P.op(ACT, lambda e, a=a, sa=sa: e.activation(out=nqM[sa][1][64:128, :, :], in_=nqT[64:128, :, a * 128:(a + 1) * 128], func=AF.Copy),
                             reads=[r_nq], writes=[r_nqM[sa][1]])
                    kt = a + o
                    var = NA_VAR_OF[(a, o)]
                    i = u % 2
                    for bank in range(2):
                        P.op(PE, lambda e, i=i, bank=bank, var=var: e.matmul(pS[i][:, bank * 512:(bank + 1) * 512], lhsT=ident[:], rhs=Bt[:, var, bank * 512:(bank + 1) * 512], start=True, stop=False),
                             reads=[r_Bt, r_c], writes=[r_pS[i]], inc=False)
                        for hh in range(4):
                            h = bank * 4 + hh
                            c = h // 2
                            P.op(PE, lambda e, i=i, h=h, c=c, kt=kt, sa=sa, hh=hh: e.matmul(
                                pS[i][:, h * 128:(h + 1) * 128], lhsT=nkT[:, c, kt * 128:(kt + 1) * 128], rhs=nqM[sa][h % 2][:, c, :],
                                start=False, stop=(hh == 3)),
                                reads=[r_nk, r_nqM[sa][h % 2]], writes=[r_pS[i]], inc=(bank == 1 and hh == 3))

                def na_pv(u):
                    a, oi, o, no = units[u]
                    kt = a + o
                    i = u % 2
                    P.op(ACT, lambda e, i=i: e.activation(out=PT[i][:], in_=pS[i][:], func=AF.Exp), reads=[r_pS[i]], writes=[r_PT[i]])
                    for h in range(8):
                        c0 = (h % 4) * 65
                        P.op(PE, lambda e, i=i, h=h, c0=c0, kt=kt, oi=oi, last=(oi == no - 1): e.matmul(
                            pO[:, h // 4, c0:c0 + 65], lhsT=PT[i][:, h * 128:(h + 1) * 128], rhs=nv[:, kt, h, :],
                            start=(oi == 0 and h % 4 == 0), stop=last),
                            reads=[r_PT[i], r_nv], writes=[r_pO], inc=(h == 7))

                def na_fin_dve(a):
                    sa = a % 2
                    po4 = pO[:, :, 0:260].rearrange("p b (h e) -> p b h e", e=65)
                    P.op(DVE, lambda e, po4=po4: e.reciprocal(out=rden[:].rearrange("p (b h e) -> p b h e", b=2, e=1), in_=po4[:, :, :, 64:65]),
                         reads=[r_pO], writes=[r_rden])
                    P.op(DVE, lambda e, po4=po4, sa=sa: e.tensor_tensor(out=ona2[sa][:].rearrange("p (b h d) -> p b h d", b=2, d=64), in0=po4[:, :, :, 0:64],
                                                                       in1=bc_ap(rden[:], [[4, 2], [1, 4], [0, 64]]), op=ALU.mult),
                         reads=[r_pO, r_rden], writes=[r_ona2[sa]])

                def na_fin_pe(a):
                    sa = a % 2
                    for c in range(4):
                        P.op(PE, lambda e, c=c, sa=sa: e.transpose(out=pTn[:, c * 128:(c + 1) * 128], in_=ona2[sa][:, c * 128:(c + 1) * 128], identity=ident[:]),
                             reads=[r_ona2[sa], r_c], writes=[r_pTn], inc=(c == 3))
                    P.op(ACT, lambda e, a=a: e.activation(out=onaT[:, :, a * 128:(a + 1) * 128], in_=pTn[:, 0:512].rearrange("p (k c) -> p k c", k=4), func=AF.Copy),
                         reads=[r_pTn], writes=[r_onaT])

                na_scores(0)
                pend = None
                for u in range(len(units)):
                    if u + 1 < len(units):
                        na_scores(u + 1)
                    na_pv(u)
                    if pend is not None:
                        na_fin_pe(pend)
                        pend = None
                    a, oi, o, no = units[u]
                    if oi == no - 1:
                        na_fin_dve(a)
                        pend = a
                na_fin_pe(pend)
                P.barrier()

        if _dbg_run():
            _subphase()

        def _subphase():
            with ExitStack() as st:
                sb = lambda name, shape, dt: st.enter_context(nc.sbuf_tensor(uname(tag + name), shape, dt))
                ps = lambda name, shape, dt: st.enter_context(nc.psum_tensor(uname(tag + name), shape, dt))
                wro = sb("wro", [128, 8, D], BF16)
                wno = sb("wno", [128, 4, D], BF16)
                wgr = sb("wgr", [128, 8, D], BF16)
                wgn = sb("wgn", [128, 8, D], BF16)
                wmo = sb("wmo", [128, 8, D], BF16)
                r_w = Res("w")
                gpost = sb("gpost", [128, D], F32)
                mT = sb("mT", [128, 8, 512], BF16)
                r_mT = Res("mT")
                t1 = [sb("t1_%d" % i, [128, 512], F32) for i in range(2)]
                t2 = [sb("t2_%d" % i, [128, 512], F32) for i in range(2)]
                m1 = [sb("m1_0", [128, 512], F32)] * 2
                m2 = [sb("m2_0", [128, 512], F32)] * 2
                r_t1 = [Res("t1") for _ in range(2)]
                r_t2 = [Res("t2") for _ in range(2)]
                r_m1 = [Res("m1")] * 2
                r_m2 = [Res("m2")] * 2
                tmp = [sb("tmp%d" % i, [128, D], F32) for i in range(2)]
                r_tmp = [Res("tmp") for _ in range(2)]
                xr = [sb("xr%d" % i, [128, D], F32) for i in range(2)]
                r_xr = [Res("xr") for _ in range(2)]
                junk = sb("junk", [128, D], BF16)
                r_junk = Res("junk")
                ssq = [sb("ssq%d" % i, [128, 1], F32) for i in range(2)]
                rstd = [sb("rstd%d" % i, [128, 1], F32) for i in range(2)]
                r_ssq = [Res("ssq") for _ in range(2)]
                r_rstd = [Res("rstd") for _ in range(2)]
                pA = [ps("pA%d" % i, [128, 512], F32) for i in range(4)]
                r_pA = [Res("pA") for _ in range(4)]
                pM = [ps("pM%d" % i, [128, D], F32) for i in range(2)]
                r_pM = [Res("pM") for _ in range(2)]

                P.dma(SP, gpost[:], bcast_rows(g_post, D), writes=[r_c])
                P.op(DVE, lambda e: e.tensor_scalar(out=gpost[:], in0=gpost[:], scalar1=0.5, scalar2=None, op0=ALU.mult), reads=[r_c], writes=[r_c])
                r_wro = [Res("wro") for _ in range(8)]
                r_wgr = [Res("wgr") for _ in range(8)]
                r_wno = [Res("wno") for _ in range(8)]
                r_wgn = [Res("wgn") for _ in range(8)]
                r_wmo = [Res("wmo") for _ in range(2)]
                for c in range(8):
                    cs = slice(c * 128, (c + 1) * 128)
                    P.dma(SP, wro[:, :, cs], w_bf["w_ret_out"][:, :, cs], reads=[r_wbf["w_ret_out"]], writes=[r_wro[c]])
                    P.dma(SP, wgr[:, :, cs], wmi[:, :, C_GR + c * 128:C_GR + (c + 1) * 128], reads=[r_wmi], writes=[r_wgr[c]])
                    P.dma(SP, wno[:, :, cs], w_bf["w_na_out"][:, :, cs], reads=[r_wbf["w_na_out"]], writes=[r_wno[c]])
                    P.dma(SP, wgn[:, :, cs], wmi[:, :, C_GN + c * 128:C_GN + (c + 1) * 128], reads=[r_wmi], writes=[r_wgn[c]])
                for half in range(2):
                    hs_ = slice(half * 512, (half + 1) * 512)
                    P.dma(SP, wmo[:, :, hs_], w_bf["w_mix_out"][:, :, hs_], reads=[r_wbf["w_mix_out"]], writes=[r_wmo[half]])
                k2 = 0
                for tb in range(4):
                    tsl = slice(tb * 512, (tb + 1) * 512)
                    for c in range(8):
                        cs = slice(c * 128, (c + 1) * 128)
                        b = k2 % 2
                        k2 += 1
                        for kc in range(8):
                            P.op(PE, lambda e, kc=kc, cs=cs, tsl=tsl: e.matmul(pA[1][:], lhsT=wgr[:, kc, cs], rhs=uT[:, kc, tsl], start=(kc == 0), stop=(kc == 7)),
                                 reads=[r_wgr[c]] + r_uTt[tb * 4:tb * 4 + 4], writes=[r_pA[1]], inc=(kc == 7))
                        for kc in range(8):
                            P.op(PE, lambda e, kc=kc, cs=cs, tsl=tsl: e.matmul(pA[3][:], lhsT=wgn[:, kc, cs], rhs=uT[:, kc, tsl], start=(kc == 0), stop=(kc == 7)),
                                 reads=[r_wgn[c]] + r_uTt[tb * 4:tb * 4 + 4], writes=[r_pA[3]], inc=(kc == 7))
                        for kc in range(8):
                            P.op(PE, lambda e, kc=kc, cs=cs, tb=tb: e.matmul(pA[0][:], lhsT=wro[:, kc, cs], rhs=VA[:, tb * 4:(tb + 1) * 4, kc * 128:(kc + 1) * 128],
                                                                            start=(kc == 0), stop=(kc == 7)),
                                 reads=[r_wro[c]] + r_VA[tb * 4:(tb + 1) * 4], writes=[r_pA[0]], inc=(kc == 7))
                        for kc in range(4):
                            P.op(PE, lambda e, kc=kc, cs=cs, tsl=tsl: e.matmul(pA[2][:], lhsT=wno[:, kc, cs], rhs=onaT[:, kc, tsl], start=(kc == 0), stop=(kc == 3)),
                                 reads=[r_wno[c], r_onaT], writes=[r_pA[2]], inc=(kc == 3))
                        P.op(ACT, lambda e, b=b: e.activation(out=t1[b][:], in_=pA[1][:], func=AF.Tanh, scale=0.5), reads=[r_pA[1]], writes=[r_t1[b]])
                        P.op(ACT, lambda e, b=b: e.activation(out=t2[b][:], in_=pA[3][:], func=AF.Tanh, scale=0.5), reads=[r_pA[3]], writes=[r_t2[b]])
                        P.op(DVE, lambda e, b=b: e.scalar_tensor_tensor(out=m1[b][:], in0=t1[b][:], scalar=1.0, in1=pA[0][:], op0=ALU.add, op1=ALU.mult),
                             reads=[r_t1[b], r_pA[0]], writes=[r_m1[b]])
                        P.op(DVE, lambda e, b=b: e.scalar_tensor_tensor(out=m2[b][:], in0=t2[b][:], scalar=1.0, in1=pA[2][:], op0=ALU.add, op1=ALU.mult),
                             reads=[r_t2[b], r_pA[2]], writes=[r_m2[b]])
                        P.op(POOL, lambda e, b=b, c=c: e.tensor_tensor(out=mT[:, c, :], in0=m1[b][:], in1=m2[b][:], op=ALU.add),
                             reads=[r_m1[b], r_m2[b]], writes=[r_mT])
                    for tt in range(4):
                        t = tb * 4 + tt
                        ip = t % 2
                        if t == 0:
                            P.dma(SP, xr[ip][:], x_src[t * 128:(t + 1) * 128, :], writes=[r_xr[ip]])
                        for half in range(2):
                            for kc in range(8):
                                P.op(PE, lambda e, ip=ip, half=half, kc=kc, tt=tt: e.matmul(pM[ip][:, half * 512:(half + 1) * 512], lhsT=mT[:, kc, tt * 128:(tt + 1) * 128],
                                                                                            rhs=wmo[:, kc, half * 512:(half + 1) * 512], start=(kc == 0), stop=(kc == 7)),
                                     reads=[r_mT, r_wmo[half]], writes=[r_pM[ip]], inc=(half == 1 and kc == 7))
                        if t + 1 < NT:
                            P.dma(SP, xr[1 - ip][:], x_src[(t + 1) * 128:(t + 2) * 128, :], writes=[r_xr[1 - ip]])
                        P.op(ACT, lambda e, ip=ip: e.activation(out=junk[:], in_=pM[ip][:], func=AF.Square, accum_out=ssq[ip][:]),
                             reads=[r_pM[ip]], writes=[r_junk, r_ssq[ip]])
                        norm_rstd(P, ssq[ip][:], rstd[ip][:], r_ssq[ip], r_rstd[ip], neg_half, r_c, 1, 0.25 / D)
                        P.op(DVE, lambda e, ip=ip: e.scalar_tensor_tensor(out=tmp[ip][:], in0=pM[ip][:], scalar=rstd[ip][:], in1=gpost[:], op0=ALU.mult, op1=ALU.mult),
                             reads=[r_pM[ip], r_rstd[ip], r_c], writes=[r_tmp[ip]])
                        P.op(POOL, lambda e, ip=ip: e.tensor_tensor(out=xr[ip][:], in0=xr[ip][:], in1=tmp[ip][:], op=ALU.add),
                             reads=[r_tmp[ip], r_xr[ip]], writes=[r_xr[ip]])
                        P.dma(SP, x_dst[t * 128:(t + 1) * 128, :], xr[ip][:], reads=[r_xr[ip]])
                P.barrier()

        if _dbg_run():
            _subphase()
```

```python
import math
import numpy as np
import concourse.bass as bass
import concourse.mybir as mybir
from concourse.bass_utils import run_bass_kernel_spmd

F32 = mybir.dt.float32
BF16 = mybir.dt.bfloat16
AF = mybir.ActivationFunctionType
ALU = mybir.AluOpType

D = 1024
S = 2048
NT = S // 128
DFF = 2816
NJ = DFF // 128
EPS = 1e-6
MIXW = 6656
N_CORES = 8
SEQ_PER_CORE = 4

PE, ACT, DVE, POOL, SP = "pe", "act", "dve", "pool", "sp"
ENGS = (PE, ACT, DVE, POOL, SP)


class Res:
    __slots__ = ("name", "w", "r")

    def __init__(self, name):
        self.name = name
        self.w = None
        self.r = {}


class Prog:
    def __init__(self, nc, n_dma_sems=48):
        self.nc = nc
        self.eng = {PE: nc.tensor, ACT: nc.scalar, DVE: nc.vector, POOL: nc.gpsimd, SP: nc.sync}
        self.streams = {e: [] for e in ENGS}
        self.cnt = {e: 0 for e in ENGS}
        self.waited = {e: {} for e in ENGS}
        self.sems = {}
        self.n_dma_sems = n_dma_sems
        self.dma_tot = [0] * n_dma_sems
        self.dma_rr = 0
        self.dma_rr_q = {}
        self.n_ops = 0

    def alloc_sems(self, stack):
        for e in (PE, ACT, DVE, POOL):
            self.sems[e] = stack.enter_context(self.nc.semaphore("s_" + e))
        for i in range(self.n_dma_sems):
            self.sems[("d", i)] = stack.enter_context(self.nc.semaphore("s_d%d" % i))

    def _need(self, e, key, val, waits):
        if key == e and e == PE:
            return
        if self.waited[e].get(key, 0) >= val:
            return
        self.waited[e][key] = val
        waits.append((key, val))

    def _deps(self, e, reads, writes):
        waits = []
        for r in reads:
            if r.w is not None:
                self._need(e, r.w[0], r.w[1], waits)
        for w in writes:
            if w.w is not None:
                self._need(e, w.w[0], w.w[1], waits)
            for k, v in w.r.items():
                self._need(e, k, v, waits)
        return waits

    def _mark(self, key, val, reads, writes):
        for r in reads:
            if r.r.get(key, 0) < val:
                r.r[key] = val
        for w in writes:
            w.w = (key, val)
            w.r = {}

    def op(self, e, fn, reads=(), writes=(), inc=True):
        waits = self._deps(e, reads, writes)
        val = self.cnt[e] + 1
        if inc:
            self.cnt[e] = val
        self.streams[e].append((waits, fn, (e, 1) if inc else None))
        self._mark(e, val, reads, writes)
        self.n_ops += 1

    def dma(self, q, out_ap, in_ap, reads=(), writes=()):
        if q == SP:
            lo, hi = 0, self.n_dma_sems - 24
        else:
            lo, hi = self.n_dma_sems - 24, self.n_dma_sems
        rr = self.dma_rr_q.get(q, lo)
        i = rr
        self.dma_rr_q[q] = lo + (rr + 1 - lo) % (hi - lo)
        key = ("d", i)
        waits = self._deps(q, reads, writes)
        if self.dma_tot[i] > 0:
            self._need(q, key, self.dma_tot[i], waits)
        self.dma_tot[i] += 16
        val = self.dma_tot[i]
        self.streams[q].append((waits, lambda eng: eng.dma_start(out=out_ap, in_=in_ap), (key, 16)))
        self._mark(key, val, reads, writes)
        self.n_ops += 1

    def barrier(self):
        for e in ENGS:
            waits = []
            for k in (PE, ACT, DVE, POOL):
                if self.cnt[k] > 0:
                    self._need(e, k, self.cnt[k], waits)
            for i in range(self.n_dma_sems):
                if self.dma_tot[i] > 0:
                    self._need(e, ("d", i), self.dma_tot[i], waits)
            if waits:
                self.streams[e].append((waits, None, None))

    def check_deadlock(self):
        val = {}
        pos = {e: 0 for e in ENGS}
        progress = True
        while progress:
            progress = False
            for e in ENGS:
                st = self.streams[e]
                while pos[e] < len(st):
                    waits, fn, inc = st[pos[e]]
                    if any(val.get(k, 0) < v for k, v in waits):
                        break
                    if inc is not None:
                        val[inc[0]] = val.get(inc[0], 0) + inc[1]
                    pos[e] += 1
                    progress = True
        stuck = {e: (pos[e], len(self.streams[e]), [(k, v, val.get(k, 0)) for k, v in self.streams[e][pos[e]][0] if val.get(k, 0) < v])
                 for e in ENGS if pos[e] < len(self.streams[e])}
        if stuck:
            raise RuntimeError("deadlock in emitted program: %r" % (stuck,))

    def emit(self, block):
        prog = self
        self.check_deadlock()

        def run(e):
            def body(eng):
                for waits, fn, inc in prog.streams[e]:
                    for key, val in waits:
                        eng.wait_ge(prog.sems[key], val)
                    if fn is not None:
                        ins = fn(eng)
                        if inc is not None:
                            ins.then_inc(prog.sems[inc[0]], inc[1])
            return body

        block.tensor(run(PE))
        block.scalar(run(ACT))
        block.vector(run(DVE))
        block.gpsimd(run(POOL))
        block.sync(run(SP))


WEIGHTS = {
    "ffn1_w_in": (D, 2 * DFF), "ffn1_w_out": (DFF, D),
    "w_mix_in": (D, MIXW), "w_ret_out": (D, D), "w_na_out": (512, D), "w_mix_out": (D, D),
    "ffn2_w_in": (D, 2 * DFF), "ffn2_w_out": (DFF, D),
}
GAINS = ["ffn1_pre_norm", "ffn1_post_norm", "mix_pre_norm", "mix_post_norm", "ffn2_pre_norm", "ffn2_post_norm"]


_UNAME = [0]


def uname(base):
    _UNAME[0] += 1
    return "%s_%d" % (base, _UNAME[0])


def bcast_rows(ap_1xn, n):
    return bass.AP(ap_1xn.tensor, ap_1xn.offset, [[0, 128], [1, n]])


def build(n_seq=SEQ_PER_CORE, stop_after="C", t_ffn=1024):
    from contextlib import ExitStack
    nc = bass.Bass("TRN2", target_bir_lowering=False)
    ntok = n_seq * S
    x_in = nc.dram_tensor("x", [ntok, D], F32, kind="ExternalInput").ap()
    y_out = nc.dram_tensor("y", [ntok, D], F32, kind="ExternalOutput").ap()
    w_in = {k: nc.dram_tensor(k, list(v), F32, kind="ExternalInput").ap() for k, v in WEIGHTS.items()}
    g_in = {k: nc.dram_tensor(k, [1, D], F32, kind="ExternalInput").ap() for k in GAINS}
    ident_in = nc.dram_tensor("ident", [128, 128], F32, kind="ExternalInput").ap()
    cst_in = nc.dram_tensor("cst", [128, NCST], F32, kind="ExternalInput").ap()
    rope_in = nc.dram_tensor("rope", [128, 2 * NT * 64], F32, kind="ExternalInput").ap()
    nab_in = nc.dram_tensor("nab", [NVAR * 128, 1024], F32, kind="ExternalInput").ap()
    dec_in = nc.dram_tensor("dec", [1, 8], F32, kind="ExternalInput").ap()
    sb_d = nc.dram_tensor("sb_scr", [NT * 128, 1024], BF16, kind="Internal").ap()
    nab_bf = nc.dram_tensor("nab_bf", [128, NVAR, 1024], BF16, kind="Internal").ap()
    w_bf = {k: nc.dram_tensor(k + "_bf", [128, v[0] // 128, v[1]], BF16, kind="Internal").ap()
            for k, v in WEIGHTS.items()}
    x1_d = nc.dram_tensor("x1_scr", [ntok, D], F32, kind="Internal").ap()
    x2_d = nc.dram_tensor("x2_scr", [ntok, D], F32, kind="Internal").ap()

    P = Prog(nc)
    with ExitStack() as top:
        P.alloc_sems(top)
        r_wbf = {k: Res("wbf_" + k) for k in WEIGHTS}
        r_nabbf = Res("nabbf")

        def cast_jobs(name, src3, dst3, r_dst, nkc, N, nk_step):
            jobs = []
            for k0 in range(0, nkc, nk_step):
                k1 = min(nkc, k0 + nk_step)
                for c0 in range(0, N, 2048):
                    c1 = min(N, c0 + 2048)
                    jobs.append((dst3[:, k0:k1, c0:c1], src3[:, k0:k1, c0:c1], r_dst))
            return jobs

        def wsrc(name):
            return w_in[name].rearrange("(kc p) n -> p kc n", p=128)

        r_win1_seg = [Res("win1seg") for _ in range(11)]
        seg_order = [0, 5, 6, 1, 7, 2, 8, 3, 9, 4, 10]
        src1 = wsrc("ffn1_w_in")
        front = [(w_bf["ffn1_w_in"][:, :, sg * 512:(sg + 1) * 512], src1[:, :, sg * 512:(sg + 1) * 512], r_win1_seg[sg]) for sg in seg_order]
        front += cast_jobs("ffn1_w_out", wsrc("ffn1_w_out"), w_bf["ffn1_w_out"], r_wbf["ffn1_w_out"], NJ, D, 2)

        def win1_res(jb):
            return [r_win1_seg[(jb * 256) // 512], r_win1_seg[(DFF + jb * 256) // 512]]
        bg_jobs = cast_jobs("w_mix_in", wsrc("w_mix_in"), w_bf["w_mix_in"], r_wbf["w_mix_in"], 8, MIXW, 1)
        bg_jobs += cast_jobs("w_ret_out", wsrc("w_ret_out"), w_bf["w_ret_out"], r_wbf["w_ret_out"], 8, D, 2)
        bg_jobs += cast_jobs("w_na_out", wsrc("w_na_out"), w_bf["w_na_out"], r_wbf["w_na_out"], 4, D, 2)
        bg_jobs += cast_jobs("w_mix_out", wsrc("w_mix_out"), w_bf["w_mix_out"], r_wbf["w_mix_out"], 8, D, 2)
        bg_jobs += cast_jobs("nab", nab_in.rearrange("(v p) n -> p v n", p=128), nab_bf, r_nabbf, NVAR, 1024, 2)
        bg_jobs += cast_jobs("ffn2_w_in", wsrc("ffn2_w_in"), w_bf["ffn2_w_in"], r_wbf["ffn2_w_in"], 8, 2 * DFF, 1)
        bg_jobs += cast_jobs("ffn2_w_out", wsrc("ffn2_w_out"), w_bf["ffn2_w_out"], r_wbf["ffn2_w_out"], NJ, D, 2)
        for dst, src, r_dst in front:
            P.dma(POOL, dst, src, writes=[r_dst])

        MT = top.enter_context(nc.sbuf_tensor("g_MT", [128, 512], F32))
        DFt = top.enter_context(nc.sbuf_tensor("g_DFt", [128, 512], F32))
        DBt = top.enter_context(nc.sbuf_tensor("g_DBt", [128, 512], F32))
        dtok = top.enter_context(nc.sbuf_tensor("g_dtok", [128, 8], F32))
        gC = top.enter_context(nc.sbuf_tensor("g_gC", [128, 8], F32))
        tabs = (MT, DFt, DBt, dtok, gC)
        with ExitStack() as st:
            cst = st.enter_context(nc.sbuf_tensor("g_cst", [128, NCST], F32))
            lgt = st.enter_context(nc.sbuf_tensor("g_lgt", [128, 8], F32))
            tA = st.enter_context(nc.sbuf_tensor("g_tA", [128, 512], F32))
            tB = st.enter_context(nc.sbuf_tensor("g_tB", [128, 512], F32))
            r_c = Res("gconst")
            P.dma(SP, cst[:], cst_in[:, :], writes=[r_c])
            P.dma(SP, lgt[:], bcast_rows(dec_in, 8), writes=[r_c])
            P.op(ACT, lambda e: e.activation(out=lgt[:], in_=lgt[:], func=AF.Exp, scale=-1.0), reads=[r_c], writes=[r_c])
            P.op(ACT, lambda e: e.activation(out=lgt[:], in_=lgt[:], func=AF.Ln, bias=1.0), reads=[r_c], writes=[r_c])
            P.op(DVE, lambda e: e.tensor_scalar(out=lgt[:], in0=lgt[:], scalar1=-1.0, scalar2=None, op0=ALU.mult), reads=[r_c], writes=[r_c])
            for h in range(4):
                hs = slice(h * 128, (h + 1) * 128)
                P.op(ACT, lambda e, h=h: e.activation(out=tA[:, 0:128], in_=cst[:, 128:256], func=AF.Exp, scale=lgt[:, h:h + 1]), reads=[r_c], writes=[r_c])
                P.op(DVE, lambda e, hs=hs: e.tensor_tensor(out=MT[:, hs], in0=tA[:, 0:128], in1=cst[:, 384:512], op=ALU.mult), reads=[r_c], writes=[r_c])
                P.op(ACT, lambda e, h=h: e.activation(out=tB[:, 0:128], in_=cst[:, 256:384], func=AF.Exp, scale=lgt[:, 4 + h:5 + h]), reads=[r_c], writes=[r_c])
                P.op(DVE, lambda e: e.tensor_tensor(out=tB[:, 0:128], in0=tB[:, 0:128], in1=cst[:, 512:640], op=ALU.mult), reads=[r_c], writes=[r_c])
                P.op(DVE, lambda e, hs=hs: e.tensor_tensor(out=MT[:, hs], in0=MT[:, hs], in1=tB[:, 0:128], op=ALU.add), reads=[r_c], writes=[r_c])
                P.op(ACT, lambda e, h=h, hs=hs: e.activation(out=DFt[:, hs], in_=cst[:, 640:768], func=AF.Exp, scale=lgt[:, h:h + 1]), reads=[r_c], writes=[r_c])
                P.op(ACT, lambda e, h=h, hs=hs: e.activation(out=DBt[:, hs], in_=cst[:, 768:896], func=AF.Exp, scale=lgt[:, 4 + h:5 + h]), reads=[r_c], writes=[r_c])
            P.op(DVE, lambda e: e.tensor_scalar(out=DFt[:], in0=DFt[:], scalar1=RET_SCALE, scalar2=None, op0=ALU.mult), reads=[r_c], writes=[r_c])
            P.op(DVE, lambda e: e.tensor_scalar(out=DBt[:], in0=DBt[:], scalar1=RET_SCALE, scalar2=None, op0=ALU.mult), reads=[r_c], writes=[r_c])
            P.op(ACT, lambda e: e.activation(out=dtok[:, 0:4], in_=lgt[:, 0:4], func=AF.Exp, scale=cst[:, 896:897]), reads=[r_c], writes=[r_c])
            P.op(ACT, lambda e: e.activation(out=dtok[:, 4:8], in_=lgt[:, 4:8], func=AF.Exp, scale=cst[:, 897:898]), reads=[r_c], writes=[r_c])
            P.op(ACT, lambda e: e.activation(out=gC[:], in_=lgt[:], func=AF.Exp, scale=128.0), reads=[r_c], writes=[r_c])
            P.barrier()

        ffn_phase(nc, P, x_in, x1_d if stop_after != "A" else y_out,
                  w_bf["ffn1_w_in"], w_bf["ffn1_w_out"], win1_res, r_wbf["ffn1_w_out"],
                  g_in["ffn1_pre_norm"], g_in["ffn1_post_norm"], ident_in, t_ffn, "f1", ntok, bg_jobs)
        for dst, src, r_dst in bg_jobs:
            P.dma(POOL, dst, src, writes=[r_dst])
        P.barrier()
        if stop_after != "A":
            for sq in range(n_seq):
                tok0 = sq * S
                mix_phase(nc, P, x1_d[tok0:tok0 + S, :], x2_d[tok0:tok0 + S, :] if stop_after != "B" else y_out[tok0:tok0 + S, :],
                          w_bf, r_wbf, g_in["mix_pre_norm"], g_in["mix_post_norm"], cst_in, rope_in, (nab_bf, r_nabbf), tabs, sb_d, "mxs%d" % sq)
                P.barrier()
            if stop_after != "B":
                ffn_phase(nc, P, x2_d, y_out,
                          w_bf["ffn2_w_in"], w_bf["ffn2_w_out"], r_wbf["ffn2_w_in"], r_wbf["ffn2_w_out"],
                          g_in["ffn2_pre_norm"], g_in["ffn2_post_norm"], ident_in, t_ffn, "f2", ntok)
                P.barrier()

        P.barrier()
        with nc.Block() as block:
            P.emit(block)
    return nc


def norm_rstd(P, pool_ssq, pool_rstd, r_ssq, r_rstd, neg_half, r_consts, n, inv_n):
    P.op(POOL, lambda e: e.tensor_scalar(out=pool_rstd, in0=pool_ssq, scalar1=inv_n, scalar2=EPS, op0=ALU.mult, op1=ALU.add),
         reads=[r_ssq], writes=[r_rstd])
    P.op(POOL, lambda e: e.tensor_tensor(out=pool_rstd, in0=pool_rstd, in1=neg_half[:, 0:n], op=ALU.pow),
         reads=[r_rstd, r_consts], writes=[r_rstd])


def ffn_phase(nc, P, x_src, x_dst, win_bf, wout_bf, r_win, r_wout, g_pre, g_post, ident_in, T, tag, ntok=S, bg_jobs=None):
    from contextlib import ExitStack
    NG = ntok // T
    TT = T // 128
    TB = T // 512
    NWB = 3
    with ExitStack() as st:
        sb = lambda name, shape, dt: st.enter_context(nc.sbuf_tensor(uname(tag + name), shape, dt))
        ps = lambda name, shape, dt: st.enter_context(nc.psum_tensor(uname(tag + name), shape, dt))
        NXN, NXR = 3, 2
        xn = [sb("xn%d" % i, [128, D], F32) for i in range(NXN)]
        r_xn = [Res("xn") for _ in range(NXN)]
        xr = [sb("xr%d" % i, [128, D], F32) for i in range(NXR)]
        r_xr = [Res("xr") for _ in range(NXR)]
        uT = [sb("uT%d" % i, [128, 8, T], BF16) for i in range(2)]
        r_uT = [[Res("uT") for _ in range(TB)] for _ in range(2)]
        hT = sb("hT", [128, NJ, T], BF16)
        r_hT = Res("hT")
        wout = sb("wout", [128, NJ, D], BF16)
        r_wo = Res("wout")
        wg = [sb("wg%d" % i, [128, 8, 256], BF16) for i in range(NWB)]
        wu = [sb("wu%d" % i, [128, 8, 256], BF16) for i in range(NWB)]
        r_w = [Res("w") for _ in range(NWB)]
        gpre = sb("gpre", [128, D], F32)
        gpost = sb("gpost", [128, D], F32)
        ident_f = sb("identf", [128, 128], F32)
        ident = sb("ident", [128, 128], BF16)
        neg_half = sb("nh", [128, 8], F32)
        r_c = Res("consts")
        ub = [sb("ub%d" % i, [128, D], BF16) for i in range(2)]
        r_ub = [Res("ub") for _ in range(2)]
        junk = sb("junk", [128, D], BF16)
        r_junk = Res("junk")
        ssq = [sb("ssq%d" % i, [128, 1], F32) for i in range(4)]
        rstd = [sb("rstd%d" % i, [128, 1], F32) for i in range(4)]
        r_ssq = [Res("ssq") for _ in range(4)]
        r_rstd = [Res("rstd") for _ in range(4)]
        sg = [sb("sg%d" % i, [128, 512], F32) for i in range(2)]
        r_sg = [Res("sg") for _ in range(2)]
        tmp = [sb("tmp%d" % i, [128, D], F32) for i in range(2)]
        r_tmp = [Res("tmp") for _ in range(2)]
        p2 = [ps("p2_%d" % i, [128, 512], F32) for i in range(3)]
        r_p2 = [Res("p2") for _ in range(3)]
        p3 = [ps("p3_%d" % i, [128, D], F32) for i in range(2)]
        r_p3 = [Res("p3") for _ in range(2)]
        pT = ps("pT", [128, D], BF16)
        r_pT = Res("pT")

        P.dma(SP, gpre[:], bcast_rows(g_pre, D), writes=[r_c])
        P.dma(SP, gpost[:], bcast_rows(g_post, D), writes=[r_c])
        P.dma(SP, ident_f[:], ident_in[:, :], writes=[r_c])
        P.op(DVE, lambda e: e.tensor_copy(out=ident[:], in_=ident_f[:]), reads=[r_c], writes=[r_c])
        P.op(DVE, lambda e: e.tensor_scalar(out=gpost[:], in0=gpost[:], scalar1=0.5, scalar2=None, op0=ALU.mult), reads=[r_c], writes=[r_c])
        P.op(DVE, lambda e: e.memset(neg_half[:], -0.5), writes=[r_c])

        cnt = {"n": 0, "p2": 0, "p3": 0, "w": 0, "sg": 0, "tmp": 0, "xn": 0, "xr": 0}

        nrec = {}

        def stage_x(g, t):
            k = cnt["n"] % 4
            kb = cnt["n"] % 2
            cnt["n"] += 1
            ix = cnt["xn"] % NXN
            cnt["xn"] += 1
            nrec[(g, t)] = kb
            r0 = g * T + t * 128
            P.dma(SP, xn[ix][:], x_src[r0:r0 + 128, :], writes=[r_xn[ix]])
            xin = xn[ix][:]
            P.op(ACT, lambda e, xin=xin, k=k: e.activation(out=junk[:], in_=xin, func=AF.Square, accum_out=ssq[k][:]),
                 reads=[r_xn[ix]], writes=[r_junk, r_ssq[k]])
            norm_rstd(P, ssq[k][:], rstd[k][:], r_ssq[k], r_rstd[k], neg_half, r_c, 1, 1.0 / D)
            P.op(DVE, lambda e, xin=xin, k=k, kb=kb: e.scalar_tensor_tensor(out=ub[kb][:], in0=xin, scalar=rstd[k][:], in1=gpre[:], op0=ALU.mult, op1=ALU.mult),
                 reads=[r_xn[ix], r_rstd[k], r_c], writes=[r_ub[kb]])

        def stage_y(g, t):
            b = g % 2
            kb = nrec[(g, t)]
            for kc in range(8):
                P.op(PE, lambda e, kb=kb, kc=kc: e.transpose(out=pT[:, kc * 128:(kc + 1) * 128], in_=ub[kb][:, kc * 128:(kc + 1) * 128], identity=ident[:]),
                     reads=[r_ub[kb], r_c], writes=[r_pT], inc=(kc == 7))
            P.op(ACT, lambda e, b=b, t=t: e.activation(out=uT[b][:, :, t * 128:(t + 1) * 128], in_=pT[:].rearrange("p (k c) -> p k c", k=8), func=AF.Copy),
                 reads=[r_pT], writes=[r_uT[b][t // 4]])

        def norm_group(g):
            stage_x(g, 0)
            for t in range(TT):
                if t + 1 < TT:
                    stage_x(g, t + 1)
                stage_y(g, t)

        def load_w(jb):
            s = cnt["w"] % NWB
            cnt["w"] += 1
            c0 = jb * 256
            rr = r_win(jb) if callable(r_win) else [r_win]
            P.dma(SP, wg[s][:], win_bf[:, :, c0:c0 + 256], reads=rr, writes=[r_w[s]])
            P.dma(SP, wu[s][:], win_bf[:, :, DFF + c0:DFF + c0 + 256], reads=rr, writes=[r_w[s]])
            return s

        NJB = NJ // 2
        norm_group(0)
        slots = [load_w(0), load_w(1)]
        for g in range(NG):
            b = g % 2
            for jb in range(NJB):
                s = slots.pop(0)
                nxt = jb + 2
                if nxt < NJB:
                    slots.append(load_w(nxt))
                elif g + 1 < NG:
                    slots.append(load_w(nxt - NJB))
                P.dma(SP, wout[:, 2 * jb:2 * jb + 2, :], wout_bf[:, 2 * jb:2 * jb + 2, :], reads=[r_wout], writes=[r_wo])
                for jj in range(2):
                    j = jb * 2 + jj
                    for tb in range(TB):
                        ig = cnt["p2"] % 3
                        iu = (cnt["p2"] + 1) % 3
                        cnt["p2"] += 2
                        for kc in range(8):
                            P.op(PE, lambda e, ig=ig, s=s, jj=jj, kc=kc, b=b, tb=tb: e.matmul(
                                p2[ig][:], lhsT=wg[s][:, kc, jj * 128:(jj + 1) * 128], rhs=uT[b][:, kc, tb * 512:(tb + 1) * 512],
                                start=(kc == 0), stop=(kc == 7)),
                                reads=[r_w[s], r_uT[b][tb]], writes=[r_p2[ig]], inc=(kc == 7))
                        for kc in range(8):
                            P.op(PE, lambda e, iu=iu, s=s, jj=jj, kc=kc, b=b, tb=tb: e.matmul(
                                p2[iu][:], lhsT=wu[s][:, kc, jj * 128:(jj + 1) * 128], rhs=uT[b][:, kc, tb * 512:(tb + 1) * 512],
                                start=(kc == 0), stop=(kc == 7)),
                                reads=[r_w[s], r_uT[b][tb]], writes=[r_p2[iu]], inc=(kc == 7))
                        k = cnt["sg"] % 2
                        cnt["sg"] += 1
                        P.op(ACT, lambda e, k=k, ig=ig: e.activation(out=sg[k][:], in_=p2[ig][:], func=AF.Silu),
                             reads=[r_p2[ig]], writes=[r_sg[k]])
                        P.op(DVE, lambda e, k=k, iu=iu, j=j, tb=tb: e.tensor_tensor(out=hT[:, j, tb * 512:(tb + 1) * 512], in0=p2[iu][:], in1=sg[k][:], op=ALU.mult),
                             reads=[r_p2[iu], r_sg[k]], writes=[r_hT])
                if bg_jobs:
                    for _ in range(min(1, len(bg_jobs))):
                        dst_, src_, r_dst_ = bg_jobs.pop(0)
                        P.dma(POOL, dst_, src_, writes=[r_dst_])
                if g + 1 < NG:
                    if 1 <= jb <= TT:
                        stage_x(g + 1, jb - 1)
                    if 2 <= jb <= TT + 1:
                        stage_y(g + 1, jb - 2)
            for t in range(TT):
                ip = cnt["p3"] % 2
                cnt["p3"] += 1
                ir = cnt["xr"] % NXR
                cnt["xr"] += 1
                r0 = g * T + t * 128
                if g == 0 and t == 0:
                    P.dma(SP, xr[ir][:], x_src[r0:r0 + 128, :], writes=[r_xr[ir]])
                for half in range(2):
                    for j in range(NJ):
                        P.op(PE, lambda e, ip=ip, half=half, j=j, t=t: e.matmul(
                            p3[ip][:, half * 512:(half + 1) * 512], lhsT=hT[:, j, t * 128:(t + 1) * 128], rhs=wout[:, j, half * 512:(half + 1) * 512],
                            start=(j == 0), stop=(j == NJ - 1)),
                            reads=[r_hT, r_wo], writes=[r_p3[ip]], inc=(half == 1 and j == NJ - 1))
                nt_ = g * TT + t + 1
                if nt_ < NG * TT:
                    irn = cnt["xr"] % NXR
                    P.dma(SP, xr[irn][:], x_src[nt_ * 128:(nt_ + 1) * 128, :], writes=[r_xr[irn]])
                k = cnt["n"] % 4
                cnt["n"] += 1
                kt = cnt["tmp"] % 2
                cnt["tmp"] += 1
                P.op(ACT, lambda e, ip=ip, k=k: e.activation(out=junk[:], in_=p3[ip][:], func=AF.Square, accum_out=ssq[k][:]),
                     reads=[r_p3[ip]], writes=[r_junk, r_ssq[k]])
                norm_rstd(P, ssq[k][:], rstd[k][:], r_ssq[k], r_rstd[k], neg_half, r_c, 1, 1.0 / D)
                P.op(DVE, lambda e, ip=ip, k=k, kt=kt: e.scalar_tensor_tensor(out=tmp[kt][:], in0=p3[ip][:], scalar=rstd[k][:], in1=gpost[:], op0=ALU.mult, op1=ALU.mult),
                     reads=[r_p3[ip], r_rstd[k], r_c], writes=[r_tmp[kt]])
                P.op(POOL, lambda e, ir=ir, kt=kt: e.tensor_tensor(out=xr[ir][:], in0=xr[ir][:], in1=tmp[kt][:], op=ALU.add),
                     reads=[r_tmp[kt], r_xr[ir]], writes=[r_xr[ir]])
                P.dma(SP, x_dst[r0:r0 + 128, :], xr[ir][:], reads=[r_xr[ir]])


def common_inputs(inputs):
    common = {k: np.ascontiguousarray(inputs[k][0], dtype=np.float32) for k in WEIGHTS}
    for k in GAINS:
        common[k] = np.ascontiguousarray(inputs[k], dtype=np.float32).reshape(1, D)
    common["ident"] = np.eye(128, dtype=np.float32)
    common["cst"] = make_cst()
    common["rope"] = make_rope()
    common["nab"] = make_nab(np.asarray(inputs["na_rel_bias"])[0])
    common["dec"] = np.concatenate([np.asarray(inputs["ret_decay_fwd"], np.float32).reshape(4),
                                    np.asarray(inputs["ret_decay_bwd"], np.float32).reshape(4)]).reshape(1, 8)
    return common


def kernel(**inputs):
    x = np.ascontiguousarray(inputs["x"], dtype=np.float32)
    B = x.shape[0]
    assert B == N_CORES * SEQ_PER_CORE
    nc = build()
    common = common_inputs(inputs)
    in_maps = []
    for c in range(N_CORES):
        m = dict(common)
        m["x"] = x[c * SEQ_PER_CORE:(c + 1) * SEQ_PER_CORE].reshape(SEQ_PER_CORE * S, D)
        in_maps.append(m)
    res = run_bass_kernel_spmd(nc, in_maps, core_ids=list(range(N_CORES)))
    out = np.stack([r["y"].reshape(SEQ_PER_CORE, S, D) for r in res.results], axis=0)
    return out.reshape(B, S, D).astype(np.float32)


RET_SCALE = 128.0 ** -0.5
NCST = 898


def make_cst():
    c = np.zeros((128, NCST), np.float32)
    c[:, 0:128] = np.eye(128, dtype=np.float32)
    k = np.arange(128)[:, None]
    q = np.arange(128)[None, :]
    c[:, 128:256] = np.maximum(q - k, 0)
    c[:, 256:384] = np.maximum(k - q, 0)
    c[:, 384:512] = np.where(q >= k, RET_SCALE, 0.0)
    c[:, 512:640] = np.where(k > q, RET_SCALE, 0.0)
    c[:, 640:768] = np.broadcast_to(q + 1, (128, 128))
    c[:, 768:896] = np.broadcast_to(128 - q, (128, 128))
    c[:, 896] = 127 - np.arange(128)
    c[:, 897] = np.arange(128)
    return c


def make_rope():
    inv = (np.float32(1.0) / (np.float32(10000.0) ** np.linspace(0.0, 1.0, 64, dtype=np.float32))).astype(np.float32)
    pos = np.arange(S, dtype=np.float32)
    ang = (pos[:, None] * inv[None, :]).astype(np.float32)
    cos = np.cos(ang).astype(np.float32).reshape(NT, 128, 64).transpose(1, 0, 2)
    sin = np.sin(ang).astype(np.float32).reshape(NT, 128, 64).transpose(1, 0, 2)
    return np.ascontiguousarray(np.concatenate([cos.reshape(128, NT * 64), sin.reshape(128, NT * 64)], axis=1))


def na_offsets(a):
    rows = []
    for qr in (0, 1):
        rq = 2 * a + qr
        rs = min(max(rq - 4, 0), 24)
        rows += [rs, rs + 7]
    return list(range(min(rows) // 2 - a, max(rows) // 2 - a + 1))


def _na_variants():
    var_of, idxs, seen = {}, [], {}
    ck = np.arange(64)[:, None]
    cq = np.arange(64)[None, :]
    ws = np.clip(cq - 8, 0, 48)
    cvalid = (ck >= ws) & (ck < ws + 16)
    ci = np.clip(ck - cq + 15, 0, 30)
    for a in range(16):
        for o in na_offsets(a):
            idx = np.full((128, 128), -1, np.int64)
            for kr in (0, 1):
                for qr in (0, 1):
                    rk = 2 * (a + o) + kr
                    rq = 2 * a + qr
                    rs = min(max(rq - 4, 0), 24)
                    if not (rs <= rk < rs + 8):
                        continue
                    ri = rk - rq + 7
                    idx[kr * 64:(kr + 1) * 64, qr * 64:(qr + 1) * 64] = np.where(cvalid, ri * 31 + ci, -1)
            key = idx.tobytes()
            if key not in seen:
                seen[key] = len(idxs)
                idxs.append(idx)
            var_of[(a, o)] = seen[key]
    return var_of, idxs


NA_VAR_OF, NA_IDX = _na_variants()
NVAR = len(NA_IDX)
NA_NEG = -30000.0


def make_nab(rel_bias):
    rb = np.asarray(rel_bias, np.float32).reshape(8, 15 * 31)
    out = np.empty((NVAR, 128, 8, 128), np.float32)
    for v, idx in enumerate(NA_IDX):
        safe = np.where(idx >= 0, idx, 0)
        for h in range(8):
            out[v, :, h, :] = np.where(idx >= 0, rb[h][safe], np.float32(NA_NEG))
    return np.ascontiguousarray(out.reshape(NVAR * 128, 1024))


def bc_ap(base, dims):
    return bass.AP(base.tensor, base.offset, [list(base.ap[0])] + [list(d) for d in dims])


C_RQ, C_RK, C_RV, C_RG, C_NQ, C_NK, C_NV, C_GR, C_GN = 0, 512, 1024, 2048, 3072, 3584, 4096, 4608, 5632


_DBG = {'n': 0, 'max': 99}


def _dbg_run():
    _DBG['n'] += 1
    return _DBG['n'] <= _DBG['max']


def mix_phase(nc, P, x_src, x_dst, w_bf, r_wbf, g_pre, g_post, cst_in, rope_in, nab_in, tabs, sb_d, tag):
    MT, DFt, DBt, dtok, gC = tabs
    from contextlib import ExitStack
    wmi, r_wmi = w_bf["w_mix_in"], r_wbf["w_mix_in"]
    with ExitStack() as M:
        sbM = lambda name, shape, dt: M.enter_context(nc.sbuf_tensor(uname(tag + name), shape, dt))
        uT = sbM("uT", [128, 8, S], BF16)
        r_uTt = [Res("uT") for _ in range(NT)]
        VA = sbM("VA", [128, NT, 1024], BF16)
        r_VA = [Res("VA") for _ in range(NT)]
        onaT = sbM("onaT", [128, 4, S], BF16)
        r_onaT = Res("onaT")
        cst = sbM("cst", [128, NCST], F32)
        ident = sbM("ident", [128, 128], BF16)
        neg_half = sbM("nh", [128, 8], F32)
        r_c = Res("c")
        P.dma(SP, cst[:], cst_in[:, :], writes=[r_c])
        P.op(DVE, lambda e: e.tensor_copy(out=ident[:], in_=cst[:, 0:128]), reads=[r_c], writes=[r_c])
        P.op(DVE, lambda e: e.memset(neg_half[:], -0.5), writes=[r_c])

        def _subphase():
            with ExitStack() as st:
                sb = lambda name, shape, dt: st.enter_context(nc.sbuf_tensor(uname(tag + name), shape, dt))
                ps = lambda name, shape, dt: st.enter_context(nc.psum_tensor(uname(tag + name), shape, dt))
                Ktm = sb("Ktm", [128, NT, 512], BF16)
                r_K = [Res("K") for _ in range(NT)]
                wk = sb("wk", [128, 8, 512], BF16)
                wv = sb("wv", [128, 8, 1024], BF16)
                r_wk, r_wv = Res("wk"), Res("wv")
                rope = sb("rope", [128, 2 * NT * 64], F32)
                Sf32 = sb("Sf32", [128, 1024], F32)
                Sb32 = sb("Sb32", [128, 1024], F32)
                Sfb = sb("Sfb", [128, 1024], BF16)
                Sbb = [sb("Sbb%d" % i, [128, 1024], BF16) for i in range(2)]
                r_Sf32, r_Sb32, r_Sfb = Res("Sf32"), Res("Sb32"), Res("Sfb")
                r_Sbb = [Res("Sbb") for _ in range(2)]
                r_sbd = [Res("sbd") for _ in range(NT)]
                tA = sb("tA", [128, 512], F32)
                tB = sb("tB", [128, 512], F32)
                r_tA, r_tB = Res("tA"), Res("tB")
                qrot = sb("qrot", [128, 512], BF16)
                r_qrot = Res("qrot")
                kdec = sb("kdec", [128, 512], BF16)
                r_kdec = Res("kdec")
                qT = sb("qT", [128, 512], BF16)
                qfT = sb("qfT", [128, 512], BF16)
                qbT = sb("qbT", [128, 512], BF16)
                kT = sb("kT", [128, 512], BF16)
                r_qT, r_qfT, r_qbT, r_kT = Res("qT"), Res("qfT"), Res("qbT"), Res("kT")
                Sm = sb("Sm", [128, 512], BF16)
                r_Sm = Res("Sm")
                srg = sb("srg", [128, 1024], F32)
                r_srg = Res("srg")
                Abuf = sb("A", [128, 1024], BF16)
                r_A = Res("A")
                junk = sb("junk", [128, 256], BF16)
                r_junk = Res("junk")
                ssq4 = sb("ssq4", [128, 4], F32)
                rstd4 = sb("rstd4", [128, 4], F32)
                r_ssq4, r_rstd4 = Res("ssq4"), Res("rstd4")
                b0 = ps("b0", [128, 512], F32)
                pR = ps("pR", [128, 1024], F32)
                pTr = ps("pTr", [128, 1024], BF16)
                pY = ps("pY", [128, 1024], F32)
                pU = ps("pU", [128, 1024], F32)
                r_b0, r_pR, r_pTr, r_pY, r_pU = Res("b0"), Res("pR"), Res("pTr"), Res("pY"), Res("pU")

                gpre = sb("gpre", [128, D], F32)
                xn = [sb("xn%d" % i, [128, D], F32) for i in range(4)]
                r_xn = [Res("xn") for _ in range(4)]
                ub = [sb("ub%d" % i, [128, D], BF16) for i in range(2)]
                r_ub = [Res("ub") for _ in range(2)]
                junkM = sb("junkM", [128, D], BF16)
                r_junkM = Res("junkM")
                ssq = [sb("ssq%d" % i, [128, 1], F32) for i in range(4)]
                rstd = [sb("rstd%d" % i, [128, 1], F32) for i in range(4)]
                r_ssq = [Res("ssq") for _ in range(4)]
                r_rstd = [Res("rstd") for _ in range(4)]
                P.dma(SP, gpre[:], bcast_rows(g_pre, D), writes=[r_c])

                def m1_dma(t):
                    P.dma(SP, xn[t % 4][:], x_src[t * 128:(t + 1) * 128, :], writes=[r_xn[t % 4]])

                def m1_x(t):
                    ix, k, kb = t % 4, t % 4, t % 2
                    P.op(ACT, lambda e, ix=ix, k=k: e.activation(out=junkM[:], in_=xn[ix][:], func=AF.Square, accum_out=ssq[k][:]),
                         reads=[r_xn[ix]], writes=[r_junkM, r_ssq[k]])
                    norm_rstd(P, ssq[k][:], rstd[k][:], r_ssq[k], r_rstd[k], neg_half, r_c, 1, 1.0 / D)
                    P.op(DVE, lambda e, ix=ix, k=k, kb=kb: e.scalar_tensor_tensor(out=ub[kb][:], in0=xn[ix][:], scalar=rstd[k][:], in1=gpre[:], op0=ALU.mult, op1=ALU.mult),
                         reads=[r_xn[ix], r_rstd[k], r_c], writes=[r_ub[kb]])

                def m1_y(t):
                    kb = t % 2
                    for kc in range(8):
                        P.op(PE, lambda e, kb=kb, kc=kc: e.transpose(out=pTr[:, kc * 128:(kc + 1) * 128], in_=ub[kb][:, kc * 128:(kc + 1) * 128], identity=ident[:]),
                             reads=[r_ub[kb], r_c], writes=[r_pTr], inc=(kc == 7))
                    P.op(ACT, lambda e, t=t: e.activation(out=uT[:, :, t * 128:(t + 1) * 128], in_=pTr[:].rearrange("p (k c) -> p k c", k=8), func=AF.Copy),
                         reads=[r_pTr], writes=[r_uTt[t]])

                for t in (NT - 1, NT - 2, NT - 3):
                    m1_dma(t)
                P.dma(SP, rope[:], rope_in[:, :], writes=[r_c])

                if _DBG.get('rstop') == 'setup':
                    P.barrier()
                    return

                def rotary(dst, n, r_dst):
                    cb = rope[:, n * 64:(n + 1) * 64]
                    sn = rope[:, NT * 64 + n * 64:NT * 64 + (n + 1) * 64]
                    cosb = bc_ap(cb, [[0, 4], [0, 2], [1, 64]])
                    sinb = bc_ap(sn, [[0, 4], [1, 64]])
                    v4 = b0[:].rearrange("p (h t d) -> p h t d", h=4, t=2)
                    a4 = tA[:].rearrange("p (h t d) -> p h t d", h=4, t=2)
                    b4 = tB[:].rearrange("p (h t d) -> p h t d", h=4, t=2)
                    d4 = dst.rearrange("p (h t d) -> p h t d", h=4, t=2)
                    P.op(DVE, lambda e: e.tensor_tensor(out=a4, in0=v4, in1=cosb, op=ALU.mult), reads=[r_b0, r_c], writes=[r_tA])
                    P.op(DVE, lambda e: e.tensor_tensor(out=b4[:, :, 0, :], in0=v4[:, :, 1, :], in1=sinb, op=ALU.mult), reads=[r_b0, r_c], writes=[r_tB])
                    P.op(DVE, lambda e: e.tensor_tensor(out=b4[:, :, 1, :], in0=v4[:, :, 0, :], in1=sinb, op=ALU.mult), reads=[r_b0, r_c], writes=[r_tB])
                    P.op(POOL, lambda e: e.tensor_tensor(out=d4[:, :, 0, :], in0=a4[:, :, 0, :], in1=b4[:, :, 0, :], op=ALU.subtract), reads=[r_tA, r_tB], writes=[r_dst])
                    P.op(POOL, lambda e: e.tensor_tensor(out=d4[:, :, 1, :], in0=a4[:, :, 1, :], in1=b4[:, :, 1, :], op=ALU.add), reads=[r_tA, r_tB], writes=[r_dst])

                def proj_tok(dst_ps, r_dst, wbuf, r_w, n, ncols):
                    for half in range(ncols // 512):
                        for kc in range(8):
                            P.op(PE, lambda e, half=half, kc=kc: e.matmul(dst_ps[:, half * 512:(half + 1) * 512], lhsT=uT[:, kc, n * 128:(n + 1) * 128],
                                                                          rhs=wbuf[:, kc, half * 512:(half + 1) * 512], start=(kc == 0), stop=(kc == 7)),
                                 reads=[r_uTt[n], r_w], writes=[r_dst], inc=(kc == 7 and half == ncols // 512 - 1))

                def state_update(S32, r_S32, goff, kdec_col0):
                    pass

                P.dma(SP, wk[:], wmi[:, :, C_RK:C_RK + 512], reads=[r_wmi], writes=[r_wk])
                P.dma(SP, wv[:], wmi[:, :, C_RV:C_RV + 1024], reads=[r_wmi], writes=[r_wv])
                P.op(DVE, lambda e: e.memset(Sb32[:], 0.0), writes=[r_Sb32])
                m1_x(NT - 1)
                m1_x(NT - 2)
                m1_y(NT - 1)
                for n in range(NT - 1, -1, -1):
                    if n - 3 >= 0:
                        m1_dma(n - 3)
                    if n - 2 >= 0:
                        m1_x(n - 2)
                    if n - 1 >= 0:
                        m1_y(n - 1)
                    proj_tok(b0, r_b0, wk, r_wk, n, 512)
                    rotary(Ktm[:, n, :], n, r_K[n])
                    proj_tok(pR, r_pR, wv, r_wv, n, 1024)
                    P.op(ACT, lambda e, n=n: e.activation(out=VA[:, n, :], in_=pR[:], func=AF.Copy), reads=[r_pR], writes=[r_VA[n]])
                    i = n % 2
                    P.op(ACT, lambda e, i=i: e.activation(out=Sbb[i][:], in_=Sb32[:], func=AF.Copy), reads=[r_Sb32], writes=[r_Sbb[i]])
                    P.dma(SP, sb_d[n * 128:(n + 1) * 128, :], Sbb[i][:], reads=[r_Sbb[i]], writes=[r_sbd[n]])
                    if n > 0:
                        P.op(DVE, lambda e, n=n: e.tensor_tensor(out=kdec[:].rearrange("p (h d) -> p h d", h=4), in0=Ktm[:, n, :].rearrange("p (h d) -> p h d", h=4),
                                                                 in1=dtok[:, 4:8].to_broadcast([128, 4, 128]), op=ALU.mult),
                             reads=[r_K[n], r_c], writes=[r_kdec])
                        for h in range(4):
                            P.op(PE, lambda e, h=h, n=n: e.matmul(pU[:, h * 256:(h + 1) * 256], lhsT=kdec[:, h * 128:(h + 1) * 128], rhs=VA[:, n, h * 256:(h + 1) * 256], start=True, stop=True),
                                 reads=[r_kdec, r_VA[n]], writes=[r_pU], inc=(h == 3))
                        for h in range(4):
                            P.op(DVE, lambda e, h=h: e.scalar_tensor_tensor(out=Sb32[:, h * 256:(h + 1) * 256], in0=Sb32[:, h * 256:(h + 1) * 256], scalar=gC[:, 4 + h:5 + h],
                                                                             in1=pU[:, h * 256:(h + 1) * 256], op0=ALU.mult, op1=ALU.add),
                                 reads=[r_pU, r_Sb32, r_c], writes=[r_Sb32])

                if _DBG.get('rstop') == 'pass1':
                    P.barrier()
                    return
                P.dma(SP, wk[:], wmi[:, :, C_RQ:C_RQ + 512], reads=[r_wmi], writes=[r_wk])
                P.dma(SP, wv[:], wmi[:, :, C_RG:C_RG + 1024], reads=[r_wmi], writes=[r_wv])
                P.op(DVE, lambda e: e.memset(Sf32[:], 0.0), writes=[r_Sf32])
                P.op(DVE, lambda e: e.memset(Sfb[:], 0.0), writes=[r_Sfb])
                qrot2 = [qrot, sb("qrot1", [128, 512], BF16)]
                r_qrot2 = [r_qrot, Res("qrot1")]
                srg2 = [srg, sb("srg1", [128, 1024], F32)]
                r_srg2 = [r_srg, Res("srg1")]
                kdec2 = [kdec, sb("kdec1", [128, 512], BF16)]
                r_kdec2 = [r_kdec, Res("kdec1")]
                pU0, pU1 = pU[:, 0:512], pU[:, 512:1024]
                r_pU0, r_pU1 = Res("pU0"), Res("pU1")

                def stage_a(n):
                    i = n % 2
                    P.dma(SP, Sbb[i][:], sb_d[n * 128:(n + 1) * 128, :], reads=[r_sbd[n]], writes=[r_Sbb[i]])
                    proj_tok(b0, r_b0, wk, r_wk, n, 512)
                    rotary(qrot2[i][:], n, r_qrot2[i])
                    proj_tok(pR, r_pR, wv, r_wv, n, 1024)
                    P.op(ACT, lambda e, i=i: e.activation(out=srg2[i][:], in_=pR[:], func=AF.Silu), reads=[r_pR], writes=[r_srg2[i]])
                    P.op(DVE, lambda e, n=n, i=i: e.tensor_tensor(out=kdec2[i][:].rearrange("p (h d) -> p h d", h=4), in0=Ktm[:, n, :].rearrange("p (h d) -> p h d", h=4),
                                                                 in1=dtok[:, 0:4].to_broadcast([128, 4, 128]), op=ALU.mult),
                         reads=[r_K[n], r_c], writes=[r_kdec2[i]])

                def stage_t(n):
                    i = n % 2
                    for h in range(4):
                        P.op(PE, lambda e, h=h, i=i: e.transpose(out=pTr[:, h * 128:(h + 1) * 128], in_=qrot2[i][:, h * 128:(h + 1) * 128], identity=ident[:]),
                             reads=[r_qrot2[i], r_c], writes=[r_pTr], inc=False)
                    for h in range(4):
                        P.op(PE, lambda e, h=h, n=n: e.transpose(out=pTr[:, 512 + h * 128:512 + (h + 1) * 128], in_=Ktm[:, n, h * 128:(h + 1) * 128], identity=ident[:]),
                             reads=[r_K[n], r_c], writes=[r_pTr], inc=(h == 3))
                    P.op(ACT, lambda e: e.activation(out=qT[:], in_=pTr[:, 0:512], func=AF.Copy), reads=[r_pTr], writes=[r_qT])
                    P.op(ACT, lambda e: e.activation(out=kT[:], in_=pTr[:, 512:1024], func=AF.Copy), reads=[r_pTr], writes=[r_kT])
                    P.op(DVE, lambda e: e.tensor_tensor(out=qfT[:], in0=qT[:], in1=DFt[:], op=ALU.mult), reads=[r_qT, r_c], writes=[r_qfT])
                    P.op(DVE, lambda e: e.tensor_tensor(out=qbT[:], in0=qT[:], in1=DBt[:], op=ALU.mult), reads=[r_qT, r_c], writes=[r_qbT])

                def stage_s(n):
                    for h in range(4):
                        P.op(PE, lambda e, h=h: e.matmul(pU1[:, h * 128:(h + 1) * 128], lhsT=kT[:, h * 128:(h + 1) * 128], rhs=qT[:, h * 128:(h + 1) * 128], start=True, stop=True),
                             reads=[r_kT, r_qT], writes=[r_pU1], inc=(h == 3))
                    P.op(DVE, lambda e: e.tensor_tensor(out=Sm[:], in0=pU1, in1=MT[:], op=ALU.mult), reads=[r_pU1, r_c], writes=[r_Sm])

                def stage_u(n, half):
                    i = n % 2
                    pu, r_pu = (pU0, r_pU0) if half == 0 else (pU1, r_pU1)
                    for hh in range(2):
                        h = half * 2 + hh
                        P.op(PE, lambda e, h=h, hh=hh, n=n, i=i, pu=pu: e.matmul(pu[:, hh * 256:(hh + 1) * 256], lhsT=kdec2[i][:, h * 128:(h + 1) * 128], rhs=VA[:, n, h * 256:(h + 1) * 256], start=True, stop=True),
                             reads=[r_kdec2[i], r_VA[n]], writes=[r_pu], inc=(hh == 1))
                    for hh in range(2):
                        h = half * 2 + hh
                        P.op(DVE, lambda e, h=h, hh=hh, pu=pu: e.scalar_tensor_tensor(out=Sf32[:, h * 256:(h + 1) * 256], in0=Sf32[:, h * 256:(h + 1) * 256], scalar=gC[:, h:h + 1],
                                                                                     in1=pu[:, hh * 256:(hh + 1) * 256], op0=ALU.mult, op1=ALU.add),
                             reads=[r_pu, r_Sf32, r_c], writes=[r_Sf32])

                def stage_y(n):
                    i = n % 2
                    for h in range(4):
                        vs = slice(h * 256, (h + 1) * 256)
                        hs = slice(h * 128, (h + 1) * 128)
                        P.op(PE, lambda e, vs=vs, hs=hs, n=n: e.matmul(pY[:, vs], lhsT=Sm[:, hs], rhs=VA[:, n, vs], start=True, stop=False),
                             reads=[r_Sm, r_VA[n]], writes=[r_pY], inc=False)
                        P.op(PE, lambda e, vs=vs, hs=hs: e.matmul(pY[:, vs], lhsT=qfT[:, hs], rhs=Sfb[:, vs], start=False, stop=False),
                             reads=[r_qfT, r_Sfb], writes=[r_pY], inc=False)
                        P.op(PE, lambda e, vs=vs, hs=hs, i=i: e.matmul(pY[:, vs], lhsT=qbT[:, hs], rhs=Sbb[i][:, vs], start=False, stop=True),
                             reads=[r_qbT, r_Sbb[i]], writes=[r_pY], inc=(h == 3))

                def stage_c(n):
                    i = n % 2
                    for h in range(4):
                        P.op(ACT, lambda e, h=h: e.activation(out=junk[:], in_=pY[:, h * 256:(h + 1) * 256], func=AF.Square, accum_out=ssq4[:, h:h + 1]),
                             reads=[r_pY], writes=[r_junk, r_ssq4])
                    norm_rstd(P, ssq4[:], rstd4[:], r_ssq4, r_rstd4, neg_half, r_c, 4, 1.0 / 256)
                    for h in range(4):
                        P.op(DVE, lambda e, h=h, i=i: e.scalar_tensor_tensor(out=Abuf[:, h * 256:(h + 1) * 256], in0=pY[:, h * 256:(h + 1) * 256], scalar=rstd4[:, h:h + 1],
                                                                            in1=srg2[i][:, h * 256:(h + 1) * 256], op0=ALU.mult, op1=ALU.mult),
                             reads=[r_pY, r_rstd4, r_srg2[i]], writes=[r_A])
                    P.op(ACT, lambda e: e.activation(out=Sfb[:], in_=Sf32[:], func=AF.Copy), reads=[r_Sf32], writes=[r_Sfb])

                def stage_at(n):
                    for kc in range(8):
                        P.op(PE, lambda e, kc=kc: e.transpose(out=pTr[:, kc * 128:(kc + 1) * 128], in_=Abuf[:, kc * 128:(kc + 1) * 128], identity=ident[:]),
                             reads=[r_A, r_c], writes=[r_pTr], inc=(kc == 7))
                    P.op(ACT, lambda e, n=n: e.activation(out=VA[:, n, :], in_=pTr[:], func=AF.Copy), reads=[r_pTr], writes=[r_VA[n]])

                stage_a(0)
                for n in range(NT):
                    stage_t(n)
                    if n + 1 < NT:
                        stage_a(n + 1)
                    if n >= 1:
                        stage_at(n - 1)
                    stage_s(n)
                    stage_u(n, 0)
                    stage_y(n)
                    stage_u(n, 1)
                    stage_c(n)
                stage_at(NT - 1)
                P.barrier()

        if _dbg_run():
            _subphase()

        def _subphase():
            with ExitStack() as st:
                sb = lambda name, shape, dt: st.enter_context(nc.sbuf_tensor(uname(tag + name), shape, dt))
                ps = lambda name, shape, dt: st.enter_context(nc.psum_tensor(uname(tag + name), shape, dt))
                nqT = sb("nqT", [128, 4, S], BF16)
                nkT = sb("nkT", [128, 4, S], BF16)
                nv = sb("nv", [128, NT, 8, 65], BF16)
                r_nq, r_nk, r_nv = Res("nq"), Res("nk"), Res("nv")
                Bt = sb("Bt", [128, NVAR, 1024], BF16)
                r_Bt = Res("Bt")
                wa = [sb("wa%d" % i, [128, 8, 512], BF16) for i in range(3)]
                r_wa = [Res("wa") for _ in range(3)]
                PT = [sb("PT%d" % i, [128, 1024], BF16) for i in range(2)]
                r_PT = [Res("PT") for _ in range(2)]
                ona = sb("ona", [128, 512], BF16)
                r_ona = Res("ona")
                rden = sb("rden", [128, 8], F32)
                r_rden = Res("rden")
                pS = [ps("pS%d" % i, [128, 1024], F32) for i in range(2)]
                r_pS = [Res("pS") for _ in range(2)]
                pO = ps("pO", [128, 2, 512], F32)
                r_pO = Res("pO")
                pTn = ps("pTn", [128, 1024], BF16)
                r_pTn = Res("pTn")

                for j, c0 in enumerate((C_NQ, C_NK, C_NV)):
                    P.dma(SP, wa[j][:], wmi[:, :, c0:c0 + 512], reads=[r_wmi], writes=[r_wa[j]])
                nabbf_ap, r_nabbf = nab_in
                P.dma(SP, Bt[:, 0:5, :], nabbf_ap[:, 0:5, :], reads=[r_nabbf], writes=[r_Bt])
                P.dma(SP, Bt[:, 5:NVAR, :], nabbf_ap[:, 5:NVAR, :], reads=[r_nabbf], writes=[r_Bt])
                P.op(DVE, lambda e: e.memset(nv[:], 1.0), writes=[r_nv])
                cnt = 0
                for j, (dst, r_dst, scale) in enumerate(((nqT, r_nq, 0.125), (nkT, r_nk, 1.0))):
                    for c in range(4):
                        for tb in range(4):
                            i = cnt % 2
                            cnt += 1
                            for kc in range(8):
                                P.op(PE, lambda e, i=i, j=j, c=c, kc=kc, tb=tb: e.matmul(pS[i][:, 0:512], lhsT=wa[j][:, kc, c * 128:(c + 1) * 128], rhs=uT[:, kc, tb * 512:(tb + 1) * 512],
                                                                                          start=(kc == 0), stop=(kc == 7)),
                                     reads=[r_wa[j]] + r_uTt[tb * 4:tb * 4 + 4], writes=[r_pS[i]], inc=(kc == 7))
                            P.op(ACT, lambda e, i=i, dst=dst, c=c, tb=tb, scale=scale: e.activation(out=dst[:, c, tb * 512:(tb + 1) * 512], in_=pS[i][:, 0:512], func=AF.Copy, scale=scale),
                                 reads=[r_pS[i]], writes=[r_dst])
                for t in range(NT):
                    i = cnt % 2
                    cnt += 1
                    for kc in range(8):
                        P.op(PE, lambda e, i=i, kc=kc, t=t: e.matmul(pS[i][:, 0:512], lhsT=uT[:, kc, t * 128:(t + 1) * 128], rhs=wa[2][:, kc, :], start=(kc == 0), stop=(kc == 7)),
                             reads=[r_wa[2], r_uTt[t]], writes=[r_pS[i]], inc=(kc == 7))
                    P.op(ACT, lambda e, i=i, t=t: e.activation(out=nv[:, t, :, 0:64], in_=pS[i][:, 0:512].rearrange("p (h d) -> p h d", h=8), func=AF.Copy),
                         reads=[r_pS[i]], writes=[r_nv])
                nqM = [[sb("nqM%d_%d" % (s_, i), [128, 4, 128], BF16) for i in range(2)] for s_ in range(2)]
                r_nqM = [[Res("nqM") for _ in range(2)] for _ in range(2)]
                ona2 = [ona, sb("ona1", [128, 512], BF16)]
                r_ona2 = [r_ona, Res("ona1")]
                for s_ in range(2):
                    for i in range(2):
                        P.op(DVE, lambda e, s_=s_, i=i: e.memset(nqM[s_][i][:], 0.0), writes=[r_nqM[s_][i]])
                units = [(a, oi, o, len(na_offsets(a))) for a in range(NT) for oi, o in enumerate(na_offsets(a))]

                def na_scores(u):
                    a, oi, o, _ = units[u]
                    sa = a % 2
                    if oi == 0:
                        P.op(ACT, lambda e, a=a, sa=sa: e.activation(out=nqM[sa][0][0:64, :, :], in_=nqT[0:64, :, a * 128:(a + 1) * 128], func=AF.Copy),
                             reads=[r_nq], writes=[r_nqM[sa][0]])
                        P.op(ACT, lambda e, a=a, sa=sa: e.activation(out=nqM[sa][1][64:128, :, :], in_=nqT[64:128, :, a * 128:(a + 1) * 128], func=AF.Copy),
                             reads=[r_nq], writes=[r_nqM[sa][1]])
                    kt = a + o
                    var = NA_VAR_OF[(a, o)]
                    i = u % 2
                    for bank in range(2):
                        P.op(PE, lambda e, i=i, bank=bank, var=var: e.matmul(pS[i][:, bank * 512:(bank + 1) * 512], lhsT=ident[:], rhs=Bt[:, var, bank * 512:(bank + 1) * 512], start=True, stop=False),
                             reads=[r_Bt, r_c], writes=[r_pS[i]], inc=False)
                        for hh in range(4):
                            h = bank * 4 + hh
                            c = h // 2
                            P.op(PE, lambda e, i=i, h=h, c=c, kt=kt, sa=sa, hh=hh: e.matmul(
                                pS[i][:, h * 128:(h + 1) * 128], lhsT=nkT[:, c, kt * 128:(kt + 1) * 128], rhs=nqM[sa][h % 2][:, c, :],
                                start=False, stop=(hh == 3)),
                                reads=[r_nk, r_nqM[sa][h % 2]], writes=[r_pS[i]], inc=(bank == 1 and hh == 3))

                def na_pv(u):
                    a, oi, o, no = units[u]
                    kt = a + o
                    i = u % 2
                    P.op(ACT, lambda e, i=i: e.activation(out=PT[i][:], in_=pS[i][:], func=AF.Exp), reads=[r_pS[i]], writes=[r_PT[i]])
                    for h in range(8):
                        c0 = (h % 4) * 65
                        P.op(PE, lambda e, i=i, h=h, c0=c0, kt=kt, oi=oi, last=(oi == no - 1): e.matmul(
                            pO[:, h // 4, c0:c0 + 65], lhsT=PT[i][:, h * 128:(h + 1) * 128], rhs=nv[:, kt, h, :],
                            start=(oi == 0 and h % 4 == 0), stop=last),
                            reads=[r_PT[i], r_nv], writes=[r_pO], inc=(h == 7))

                def na_fin_dve(a):
                    sa = a % 2
                    po4 = pO[:, :, 0:260].rearrange("p b (h e) -> p b h e", e=65)
                    P.op(DVE, lambda e, po4=po4: e.reciprocal(out=rden[:].rearrange("p (b h e) -> p b h e", b=2, e=1), in_=po4[:, :, :, 64:65]),
                         reads=[r_pO], writes=[r_rden])
                    P.op(DVE, lambda e, po4=po4, sa=sa: e.tensor_tensor(out=ona2[sa][:].rearrange("p (b h d) -> p b h d", b=2, d=64), in0=po4[:, :, :, 0:64],
                                                                       in1=bc_ap(rden[:], [[4, 2], [1, 4], [0, 64]]), op=ALU.mult),
                         reads=[r_pO, r_rden], writes=[r_ona2[sa]])

                def na_fin_pe(a):
                    sa = a % 2
                    for c in range(4):
                        P.op(PE, lambda e, c=c, sa=sa: e.transpose(out=pTn[:, c * 128:(c + 1) * 128], in_=ona2[sa][:, c * 128:(c + 1) * 128], identity=ident[:]),
                             reads=[r_ona2[sa], r_c], writes=[r_pTn], inc=(c == 3))
                    P.op(ACT, lambda e, a=a: e.activation(out=onaT[:, :, a * 128:(a + 1) * 128], in_=pTn[:, 0:512].rearrange("p (k c) -> p k c", k=4), func=AF.Copy),
                         reads=[r_pTn], writes=[r_onaT])

                na_scores(0)
                pend = None
                for u in range(len(units)):
                    if u + 1 < len(units):
                        na_scores(u + 1)
                    na_pv(u)
                    if pend is not None:
                        na_fin_pe(pend)
                        pend = None
                    a, oi, o, no = units[u]
                    if oi == no - 1:
                        na_fin_dve(a)
                        pend = a
                na_fin_pe(pend)
                P.barrier()

        if _dbg_run():
            _subphase()

        def _subphase():
            with ExitStack() as st:
                sb = lambda name, shape, dt: st.enter_context(nc.sbuf_tensor(uname(tag + name), shape, dt))
                ps = lambda name, shape, dt: st.enter_context(nc.psum_tensor(uname(tag + name), shape, dt))
                wro = sb("wro", [128, 8, D], BF16)
                wno = sb("wno", [128, 4, D], BF16)
                wgr = sb("wgr", [128, 8, D], BF16)
                wgn = sb("wgn", [128, 8, D], BF16)
                wmo = sb("wmo", [128, 8, D], BF16)
                r_w = Res("w")
                gpost = sb("gpost", [128, D], F32)
                mT = sb("mT", [128, 8, 512], BF16)
                r_mT = Res("mT")
                t1 = [sb("t1_%d" % i, [128, 512], F32) for i in range(2)]
                t2 = [sb("t2_%d" % i, [128, 512], F32) for i in range(2)]
                m1 = [sb("m1_0", [128, 512], F32)] * 2
                m2 = [sb("m2_0", [128, 512], F32)] * 2
                r_t1 = [Res("t1") for _ in range(2)]
                r_t2 = [Res("t2") for _ in range(2)]
                r_m1 = [Res("m1")] * 2
                r_m2 = [Res("m2")] * 2
                tmp = [sb("tmp%d" % i, [128, D], F32) for i in range(2)]
                r_tmp = [Res("tmp") for _ in range(2)]
                xr = [sb("xr%d" % i, [128, D], F32) for i in range(2)]
                r_xr = [Res("xr") for _ in range(2)]
                junk = sb("junk", [128, D], BF16)
                r_junk = Res("junk")
                ssq = [sb("ssq%d" % i, [128, 1], F32) for i in range(2)]
                rstd = [sb("rstd%d" % i, [128, 1], F32) for i in range(2)]
                r_ssq = [Res("ssq") for _ in range(2)]
                r_rstd = [Res("rstd") for _ in range(2)]
                pA = [ps("pA%d" % i, [128, 512], F32) for i in range(4)]
                r_pA = [Res("pA") for _ in range(4)]
                pM = [ps("pM%d" % i, [128, D], F32) for i in range(2)]
                r_pM = [Res("pM") for _ in range(2)]

                P.dma(SP, gpost[:], bcast_rows(g_post, D), writes=[r_c])
                P.op(DVE, lambda e: e.tensor_scalar(out=gpost[:], in0=gpost[:], scalar1=0.5, scalar2=None, op0=ALU.mult), reads=[r_c], writes=[r_c])
                r_wro = [Res("wro") for _ in range(8)]
                r_wgr = [Res("wgr") for _ in range(8)]
                r_wno = [Res("wno") for _ in range(8)]
                r_wgn = [Res("wgn") for _ in range(8)]
                r_wmo = [Res("wmo") for _ in range(2)]
                for c in range(8):
                    cs = slice(c * 128, (c + 1) * 128)
                    P.dma(SP, wro[:, :, cs], w_bf["w_ret_out"][:, :, cs], reads=[r_wbf["w_ret_out"]], writes=[r_wro[c]])
                    P.dma(SP, wgr[:, :, cs], wmi[:, :, C_GR + c * 128:C_GR + (c + 1) * 128], reads=[r_wmi], writes=[r_wgr[c]])
                    P.dma(SP, wno[:, :, cs], w_bf["w_na_out"][:, :, cs], reads=[r_wbf["w_na_out"]], writes=[r_wno[c]])
                    P.dma(SP, wgn[:, :, cs], wmi[:, :, C_GN + c * 128:C_GN + (c + 1) * 128], reads=[r_wmi], writes=[r_wgn[c]])
                for half in range(2):
                    hs_ = slice(half * 512, (half + 1) * 512)
                    P.dma(SP, wmo[:, :, hs_], w_bf["w_mix_out"][:, :, hs_], reads=[r_wbf["w_mix_out"]], writes=[r_wmo[half]])
                k2 = 0
                for tb in range(4):
                    tsl = slice(tb * 512, (tb + 1) * 512)
                    for c in range(8):
                        cs = slice(c * 128, (c + 1) * 128)
                        b = k2 % 2
                        k2 += 1
                        for kc in range(8):
                            P.op(PE, lambda e, kc=kc, cs=cs, tsl=tsl: e.matmul(pA[1][:], lhsT=wgr[:, kc, cs], rhs=uT[:, kc, tsl], start=(kc == 0), stop=(kc == 7)),
                                 reads=[r_wgr[c]] + r_uTt[tb * 4:tb * 4 + 4], writes=[r_pA[1]], inc=(kc == 7))
                        for kc in range(8):
                            P.op(PE, lambda e, kc=kc, cs=cs, tsl=tsl: e.matmul(pA[3][:], lhsT=wgn[:, kc, cs], rhs=uT[:, kc, tsl], start=(kc == 0), stop=(kc == 7)),
                                 reads=[r_wgn[c]] + r_uTt[tb * 4:tb * 4 + 4], writes=[r_pA[3]], inc=(kc == 7))
                        for kc in range(8):
                            P.op(PE, lambda e, kc=kc, cs=cs, tb=tb: e.matmul(pA[0][:], lhsT=wro[:, kc, cs], rhs=VA[:, tb * 4:(tb + 1) * 4, kc * 128:(kc + 1) * 128],
                                                                            start=(kc == 0), stop=(kc == 7)),
                                 reads=[r_wro[c]] + r_VA[tb * 4:(tb + 1) * 4], writes=[r_pA[0]], inc=(kc == 7))
                        for kc in range(4):
                            P.op(PE, lambda e, kc=kc, cs=cs, tsl=tsl: e.matmul(pA[2][:], lhsT=wno[:, kc, cs], rhs=onaT[:, kc, tsl], start=(kc == 0), stop=(kc == 3)),
                                 reads=[r_wno[c], r_onaT], writes=[r_pA[2]], inc=(kc == 3))
                        P.op(ACT, lambda e, b=b: e.activation(out=t1[b][:], in_=pA[1][:], func=AF.Tanh, scale=0.5), reads=[r_pA[1]], writes=[r_t1[b]])
                        P.op(ACT, lambda e, b=b: e.activation(out=t2[b][:], in_=pA[3][:], func=AF.Tanh, scale=0.5), reads=[r_pA[3]], writes=[r_t2[b]])
                        P.op(DVE, lambda e, b=b: e.scalar_tensor_tensor(out=m1[b][:], in0=t1[b][:], scalar=1.0, in1=pA[0][:], op0=ALU.add, op1=ALU.mult),
                             reads=[r_t1[b], r_pA[0]], writes=[r_m1[b]])
                        P.op(DVE, lambda e, b=b: e.scalar_tensor_tensor(out=m2[b][:], in0=t2[b][:], scalar=1.0, in1=pA[2][:], op0=ALU.add, op1=ALU.mult),
                             reads=[r_t2[b], r_pA[2]], writes=[r_m2[b]])
                        P.op(POOL, lambda e, b=b, c=c: e.tensor_tensor(out=mT[:, c, :], in0=m1[b][:], in1=m2[b][:], op=ALU.add),
                             reads=[r_m1[b], r_m2[b]], writes=[r_mT])
                    for tt in range(4):
                        t = tb * 4 + tt
                        ip = t % 2
                        if t == 0:
                            P.dma(SP, xr[ip][:], x_src[t * 128:(t + 1) * 128, :], writes=[r_xr[ip]])
                        for half in range(2):
                            for kc in range(8):
                                P.op(PE, lambda e, ip=ip, half=half, kc=kc, tt=tt: e.matmul(pM[ip][:, half * 512:(half + 1) * 512], lhsT=mT[:, kc, tt * 128:(tt + 1) * 128],
                                                                                            rhs=wmo[:, kc, half * 512:(half + 1) * 512], start=(kc == 0), stop=(kc == 7)),
                                     reads=[r_mT, r_wmo[half]], writes=[r_pM[ip]], inc=(half == 1 and kc == 7))
                        if t + 1 < NT:
                            P.dma(SP, xr[1 - ip][:], x_src[(t + 1) * 128:(t + 2) * 128, :], writes=[r_xr[1 - ip]])
                        P.op(ACT, lambda e, ip=ip: e.activation(out=junk[:], in_=pM[ip][:], func=AF.Square, accum_out=ssq[ip][:]),
                             reads=[r_pM[ip]], writes=[r_junk, r_ssq[ip]])
                        norm_rstd(P, ssq[ip][:], rstd[ip][:], r_ssq[ip], r_rstd[ip], neg_half, r_c, 1, 0.25 / D)
                        P.op(DVE, lambda e, ip=ip: e.scalar_tensor_tensor(out=tmp[ip][:], in0=pM[ip][:], scalar=rstd[ip][:], in1=gpost[:], op0=ALU.mult, op1=ALU.mult),
                             reads=[r_pM[ip], r_rstd[ip], r_c], writes=[r_tmp[ip]])
                        P.op(POOL, lambda e, ip=ip: e.tensor_tensor(out=xr[ip][:], in0=xr[ip][:], in1=tmp[ip][:], op=ALU.add),
                             reads=[r_tmp[ip], r_xr[ip]], writes=[r_xr[ip]])
                        P.dma(SP, x_dst[t * 128:(t + 1) * 128, :], xr[ip][:], reads=[r_xr[ip]])
                P.barrier()

        if _dbg_run():
            _subphase()
```
